# Optimizing a Trainium2 kernel written in Bass

```python
import jax, jax.numpy as jnp
from jax import lax
import numpy as np

D_MODEL = 2048
BATCH = 4
SEQ = 4096
DEPTH = 1

CTX_LEN = 256
GRID_W = 64
D_MIX = D_MODEL
D_MLSTM = D_MIX // 2
D_LRU = D_MIX - D_MLSTM
N_MLSTM_HEADS = 4
MLSTM_HEAD_DIM = D_MLSTM // N_MLSTM_HEADS
MLSTM_CHUNK = 128
N_GATE_COLS = 4 * N_MLSTM_HEADS
N_LRU_BLOCKS = 8
LRU_BLOCK = D_LRU // N_LRU_BLOCKS
LRU_C = 8.0
CONV_W = 4
CONV_PAD = (2, 1)
D_FF = ((8 * D_MODEL // 3 + 255) // 256) * 256
P_IN = 4 * D_MLSTM + N_GATE_COLS + 2 * D_LRU
EPS = 1e-6

kernel_name = 'hymba_mlstm_rglru_dit_block'


def rms_norm(x, g):
    xf = x.astype(jnp.float32)
    y = xf * lax.rsqrt(jnp.mean(xf * xf, axis=-1, keepdims=True) + EPS)
    return (y * g.astype(jnp.float32)).astype(x.dtype)


def to_col_major(t, rows):
    b, s = t.shape[0], t.shape[1]
    t = t.reshape((b, rows, GRID_W) + t.shape[2:])
    return jnp.swapaxes(t, 1, 2).reshape((b, s) + t.shape[3:])


def from_col_major(t, rows):
    b, s = t.shape[0], t.shape[1]
    t = t.reshape((b, GRID_W, rows) + t.shape[2:])
    return jnp.swapaxes(t, 1, 2).reshape((b, s) + t.shape[3:])


def flip(t):
    return jnp.flip(t, axis=1)


def mlstm_chunkwise(q, k, v, ig, lf, state):
    bsz, s, nh, dh = q.shape
    nc, L = s // MLSTM_CHUNK, MLSTM_CHUNK
    q = q.reshape(bsz, nc, L, nh, dh)
    k = (k * (dh ** -0.5)).reshape(bsz, nc, L, nh, dh)
    v = v.reshape(bsz, nc, L, nh, dh)
    ig = ig.reshape(bsz, nc, L, nh)
    lf = lf.reshape(bsz, nc, L, nh)
    cum = jnp.cumsum(lf, axis=2)
    tot = cum[:, :, -1]
    w_log = tot[:, :, None] - cum + ig
    m_loc = jnp.max(w_log, axis=2)
    w = jnp.exp(w_log - m_loc[:, :, None])
    c_loc = jnp.einsum('bclh,bclhd,bclhe->bchde', w, v, k)
    n_loc = jnp.einsum('bclh,bclhe->bche', w, k)

    def step(carry, inp):
        c_st, n_st, m_st = carry
        cl, nl, ml, g = inp
        m_new = jnp.maximum(g + m_st, ml)
        a = jnp.exp(g + m_st - m_new)
        bb = jnp.exp(ml - m_new)
        c_new = a[..., None, None] * c_st + bb[..., None, None] * cl
        n_new = a[..., None] * n_st + bb[..., None] * nl
        return (c_new, n_new, m_new), (c_st, n_st, m_st)

    xs = (jnp.moveaxis(c_loc, 1, 0), jnp.moveaxis(n_loc, 1, 0),
          jnp.moveaxis(m_loc, 1, 0), jnp.moveaxis(tot, 1, 0))
    final, prev = lax.scan(step, state, xs)
    c_prev, n_prev, m_prev = [jnp.moveaxis(t, 0, 1) for t in prev]

    causal = jnp.tril(jnp.ones((L, L), dtype=bool))
    log_d = cum[:, :, :, None, :] - cum[:, :, None, :, :] + ig[:, :, None, :, :]
    log_d = jnp.where(causal[None, None, :, :, None], log_d, -jnp.inf)
    log_inter = cum + m_prev[:, :, None, :]
    m_row = jnp.maximum(log_inter, jnp.max(log_d, axis=3))
    dmat = jnp.exp(log_d - m_row[:, :, :, None, :])
    w_inter = jnp.exp(log_inter - m_row)
    qk = jnp.einsum('bcjhd,bcshd->bcjsh', q, k) * dmat
    num = (jnp.einsum('bcjsh,bcshd->bcjhd', qk, v)
           + w_inter[..., None] * jnp.einsum('bchde,bcjhe->bcjhd', c_prev, q))
    den = jnp.sum(qk, axis=3) + w_inter * jnp.einsum('bche,bcjhe->bcjh', n_prev, q)
    h = num / jnp.maximum(jnp.abs(den), jnp.exp(-m_row))[..., None]
    return h.reshape(bsz, s, nh, dh), final


def mlstm_bidir(q, k, v, ig_f, lf_f, ig_b, lf_b, state_f, state_b):
    h_f, fin_f = mlstm_chunkwise(q, k, v, ig_f, lf_f, state_f)
    h_b, fin_b = mlstm_chunkwise(flip(q), flip(k), flip(v), flip(ig_b), flip(lf_b), state_b)
    return h_f + flip(h_b), fin_f, fin_b


def lru_combine(left, right):
    return (left[0] * right[0], right[0] * left[1] + right[1])


def rglru_direction(xc, w_a, b_a, w_x, b_x, lam, h0):
    bsz, t, _ = xc.shape
    xb = xc.reshape(bsz, t, N_LRU_BLOCKS, LRU_BLOCK)
    r = jax.nn.sigmoid((jnp.einsum('btnc,ncd->btnd', xb, w_a).reshape(bsz, t, D_LRU) + b_a).astype(jnp.float32))
    i = jax.nn.sigmoid((jnp.einsum('btnc,ncd->btnd', xb, w_x).reshape(bsz, t, D_LRU) + b_x).astype(jnp.float32))
    log_a = -LRU_C * r * jax.nn.softplus(-lam.astype(jnp.float32))
    a = jnp.exp(log_a)
    u = jnp.sqrt(-jnp.expm1(2.0 * log_a)) * (i * xc.astype(jnp.float32))
    a_cum, b_cum = lax.associative_scan(lru_combine, (a, u), axis=1)
    h = b_cum + a_cum * h0[:, None, :]
    return h, h[:, -1]


def short_conv(xr, w, b):
    y = lax.conv_general_dilated(xr, w[:, None, :].astype(xr.dtype), (1,), [CONV_PAD],
                                 dimension_numbers=('NWC', 'WIO', 'NWC'),
                                 feature_group_count=xr.shape[-1])
    return y + b


def zero_state(bsz):
    f32 = jnp.float32
    m_state = (jnp.zeros((bsz, N_MLSTM_HEADS, MLSTM_HEAD_DIM, MLSTM_HEAD_DIM), f32),
               jnp.zeros((bsz, N_MLSTM_HEADS, MLSTM_HEAD_DIM), f32),
               jnp.zeros((bsz, N_MLSTM_HEADS), f32))
    h_state = jnp.zeros((bsz, D_LRU), f32)
    return (m_state, m_state, h_state, h_state)


def token_mixer(u, rows, state, w_in, b_gates, mh_g, conv_w, conv_b,
                lru_w_a, lru_b_a, lru_w_x, lru_b_x, lru_lam, w_out):
    bsz, t = u.shape[0], u.shape[1]
    proj = u @ w_in
    o1 = D_MLSTM
    o4 = 4 * D_MLSTM
    o5 = o4 + N_GATE_COLS
    o6 = o5 + D_LRU
    q, k, v, o_pre = proj[..., :o1], proj[..., o1:2 * o1], proj[..., 2 * o1:3 * o1], proj[..., 3 * o1:o4]
    gates, xr, gr = proj[..., o4:o5], proj[..., o5:o6], proj[..., o6:]

    heads = lambda z: z.reshape(bsz, t, N_MLSTM_HEADS, MLSTM_HEAD_DIM)
    g = gates.astype(jnp.float32).reshape(bsz, t, 4, N_MLSTM_HEADS) + b_gates.astype(jnp.float32)
    seq = (heads(q), heads(k), heads(v), g[:, :, 0], jax.nn.log_sigmoid(g[:, :, 1]),
           g[:, :, 2], jax.nn.log_sigmoid(g[:, :, 3]))
    if rows is not None:
        seq = tuple(to_col_major(z, rows) for z in seq)
    h_m, fin_mf, fin_mb = mlstm_bidir(*seq, state[0], state[1])
    if rows is not None:
        h_m = from_col_major(h_m, rows)
    h_m = h_m * lax.rsqrt(jnp.mean(h_m * h_m, axis=-1, keepdims=True) + EPS)
    y_m = jax.nn.sigmoid(o_pre.astype(jnp.float32)) * (h_m.reshape(bsz, t, D_MLSTM) * mh_g.astype(jnp.float32))

    xc = short_conv(xr, conv_w, conv_b)
    h_f, fin_rf = rglru_direction(xc, lru_w_a[0], lru_b_a[0], lru_w_x[0], lru_b_x[0], lru_lam[0], state[2])
    h_b, fin_rb = rglru_direction(flip(xc), lru_w_a[1], lru_b_a[1], lru_w_x[1], lru_b_x[1], lru_lam[1], state[3])
    y_r = jax.nn.gelu(gr.astype(jnp.float32)) * (h_f + flip(h_b))

    y = jnp.concatenate([y_m, y_r], axis=-1).astype(u.dtype) @ w_out
    return y, (fin_mf, fin_mb, fin_rf, fin_rb)


def swiglu(u, w_ffn_in, w_ffn_out):
    gu = u @ w_ffn_in
    return (jax.nn.silu(gu[..., :D_FF]) * gu[..., D_FF:]) @ w_ffn_out


def setup_inputs(seed: int = 0) -> dict:
    key = jax.random.key(seed)
    ks = jax.random.split(key, 24)
    nrm = jax.random.normal
    f32 = jnp.float32
    x = nrm(ks[0], (BATCH, SEQ, D_MODEL), f32)
    c = nrm(ks[1], (BATCH, D_MODEL), f32)
    ctx = nrm(ks[2], (BATCH, CTX_LEN, D_MODEL), f32)
    c_ctx = nrm(ks[3], (D_MODEL,), f32)
    w_mod = nrm(ks[4], (DEPTH, D_MODEL, 6 * D_MODEL), f32) * (0.5 * D_MODEL ** -0.5)
    b_mod = nrm(ks[5], (DEPTH, 6 * D_MODEL), f32) * 0.02
    gains = 1.0 + 0.05 * nrm(ks[6], (4, DEPTH, D_MODEL), f32)
    w_in = nrm(ks[7], (DEPTH, D_MODEL, P_IN), f32) * D_MODEL ** -0.5
    f_bias = jnp.linspace(3.0, 6.0, N_MLSTM_HEADS)
    i_bias = jnp.full((N_MLSTM_HEADS,), -1.0)
    b_gates = (jnp.stack([i_bias, f_bias, i_bias, f_bias])[None]
               + 0.1 * nrm(ks[8], (DEPTH, 4, N_MLSTM_HEADS), f32))
    mh_norm_g = 1.0 + 0.05 * nrm(ks[9], (DEPTH, D_MLSTM), f32)
    conv_w = nrm(ks[10], (DEPTH, CONV_W, D_LRU), f32) * CONV_W ** -0.5
    conv_b = nrm(ks[11], (DEPTH, D_LRU), f32) * 0.02
    lru_w_a = nrm(ks[12], (DEPTH, 2, N_LRU_BLOCKS, LRU_BLOCK, LRU_BLOCK), f32) * LRU_BLOCK ** -0.5
    lru_b_a = nrm(ks[13], (DEPTH, 2, D_LRU), f32) * 0.1
    lru_w_x = nrm(ks[14], (DEPTH, 2, N_LRU_BLOCKS, LRU_BLOCK, LRU_BLOCK), f32) * LRU_BLOCK ** -0.5
    lru_b_x = nrm(ks[15], (DEPTH, 2, D_LRU), f32) * 0.1
    a0 = jax.random.uniform(ks[16], (DEPTH, 2, D_LRU), f32, minval=0.9, maxval=0.999)
    lru_lambda = jnp.log(a0) - jnp.log1p(-a0)
    w_out = nrm(ks[17], (DEPTH, D_MIX, D_MODEL), f32) * D_MIX ** -0.5
    w_ffn_in = nrm(ks[18], (DEPTH, D_MODEL, 2 * D_FF), f32) * D_MODEL ** -0.5
    w_ffn_out = nrm(ks[19], (DEPTH, D_FF, D_MODEL), f32) * D_FF ** -0.5
    return {'x': x, 'c': c, 'ctx': ctx, 'c_ctx': c_ctx, 'w_mod': w_mod, 'b_mod': b_mod,
            'g_pre_mix': gains[0], 'g_post_mix': gains[1], 'g_pre_ffn': gains[2], 'g_post_ffn': gains[3],
            'w_in': w_in, 'b_gates': b_gates, 'mh_norm_g': mh_norm_g, 'conv_w': conv_w, 'conv_b': conv_b,
            'lru_w_a': lru_w_a, 'lru_b_a': lru_b_a, 'lru_w_x': lru_w_x, 'lru_b_x': lru_b_x,
            'lru_lambda': lru_lambda, 'w_out': w_out, 'w_ffn_in': w_ffn_in, 'w_ffn_out': w_ffn_out}


def reference(x, c, ctx, c_ctx, w_mod, b_mod, g_pre_mix, g_post_mix, g_pre_ffn, g_post_ffn,
              w_in, b_gates, mh_norm_g, conv_w, conv_b, lru_w_a, lru_b_a, lru_w_x, lru_b_x,
              lru_lambda, w_out, w_ffn_in, w_ffn_out):
    bsz = x.shape[0]
    rows = x.shape[1] // GRID_W
    for layer in range(DEPTH):
        mix_params = (w_in[layer], b_gates[layer], mh_norm_g[layer], conv_w[layer], conv_b[layer],
                      lru_w_a[layer], lru_b_a[layer], lru_w_x[layer], lru_b_x[layer],
                      lru_lambda[layer], w_out[layer])
        m_lat = jnp.split((jax.nn.silu(c) @ w_mod[layer] + b_mod[layer])[:, None, :], 6, axis=-1)
        m_ctx = jnp.split((jax.nn.silu(c_ctx) @ w_mod[layer] + b_mod[layer])[None, None, :], 6, axis=-1)

        hc = rms_norm(ctx, g_pre_mix[layer]) * (1.0 + m_ctx[1]) + m_ctx[0]
        yc, ctx_state = token_mixer(hc, None, zero_state(bsz), *mix_params)

        hx = rms_norm(x, g_pre_mix[layer]) * (1.0 + m_lat[1]) + m_lat[0]
        yx, _ = token_mixer(hx, rows, ctx_state, *mix_params)
        x = x + m_lat[2] * rms_norm(yx, g_post_mix[layer])
        hx = rms_norm(x, g_pre_ffn[layer]) * (1.0 + m_lat[4]) + m_lat[3]
        x = x + m_lat[5] * rms_norm(swiglu(hx, w_ffn_in[layer], w_ffn_out[layer]), g_post_ffn[layer])

        if layer + 1 < DEPTH:
            ctx = ctx + m_ctx[2] * rms_norm(yc, g_post_mix[layer])
            hc = rms_norm(ctx, g_pre_ffn[layer]) * (1.0 + m_ctx[4]) + m_ctx[3]
            ctx = ctx + m_ctx[5] * rms_norm(swiglu(hc, w_ffn_in[layer], w_ffn_out[layer]), g_post_ffn[layer])
    return x
```

```python
import numpy as np
from contextlib import ExitStack
import concourse.bass as bass
import concourse.mybir as mybir
from concourse.bass_utils import run_bass_kernel_spmd

F32 = mybir.dt.float32
BF16 = mybir.dt.bfloat16
AF = mybir.ActivationFunctionType
ALU = mybir.AluOpType
AX = mybir.AxisListType

D = 2048
KC = 16
NCTX = 256
NLAT = 4096
NT = NCTX + NLAT
NOWN = 2048
NCH = NT // 128
DFF = 5632
NF = DFF // 128
EPS = 1e-6
NEG = -30000.0


class _Op:
    __slots__ = ("eng", "fn", "dma", "deps", "signal", "ticket", "sem", "prev_ticket")


class Sched:
    COMPUTE = ("pe", "act", "dve", "pool")
    ALL = ("pe", "act", "dve", "pool", "sp")
    NDMA = {"sp": 24, "act": 16, "pool": 16}

    def __init__(self, nc):
        self.nc = nc
        self.ops = []
        self.last_w = {}
        self.readers = {}
        self.bar_deps = []
        self.bar_applied = set(self.ALL)
        self.last_op = {}
        self.dma_since_bar = []

    def add(self, eng, fn, reads=(), writes=(), dma=False):
        op = _Op()
        op.eng, op.fn, op.dma = eng, fn, dma
        op.deps, op.signal = [], False
        seen = set()

        def dep(o, force=False):
            if o is None or id(o) in seen:
                return
            if not force and o.eng == "pe" and eng == "pe" and not o.dma and not dma:
                return
            seen.add(id(o))
            op.deps.append(o)

        if eng not in self.bar_applied:
            self.bar_applied.add(eng)
            for o in self.bar_deps:
                if o.eng == eng and not o.dma:
                    continue
                dep(o, True)
        xr = [r for r in reads if isinstance(r, tuple) and r[0] == "ps"]
        if xr:
            reads = [r for r in reads if r not in xr]
            writes = list(writes) + xr
        for r in reads:
            dep(self.last_w.get(r))
        for k in writes:
            dep(self.last_w.get(k))
            for rd in self.readers.get(k, ()):
                dep(rd)
        for k in writes:
            self.last_w[k] = op
            self.readers[k] = []
        for r in reads:
            self.readers.setdefault(r, []).append(op)
        self.ops.append(op)
        if dma:
            self.dma_since_bar.append(op)
        else:
            self.last_op[eng] = op
        return op

    def barrier(self):
        self.bar_deps = list(self.last_op.values()) + list(self.dma_since_bar)
        self.dma_since_bar = []
        self.bar_applied = set()

    def emit(self, final_waits=()):
        nc = self.nc
        for op in self.ops:
            for d in op.deps:
                d.signal = True
        for op in final_waits:
            op.signal = True
        with ExitStack() as es:
            csem = {e: es.enter_context(nc.semaphore("s_" + e)) for e in self.COMPUTE}
            dsem = {q: [es.enter_context(nc.semaphore("d_%s%d" % (q, i))) for i in range(n)]
                    for q, n in self.NDMA.items()}
            ccount = {e: 0 for e in self.COMPUTE}
            dcount = {q: [0] * n for q, n in self.NDMA.items()}
            drr = {q: 0 for q in self.NDMA}
            for op in self.ops:
                if op.dma:
                    q = op.eng
                    i = drr[q]
                    drr[q] = (i + 1) % self.NDMA[q]
                    op.prev_ticket = dcount[q][i]
                    dcount[q][i] += 16
                    op.ticket = dcount[q][i]
                    op.sem = dsem[q][i]
                elif op.signal:
                    ccount[op.eng] += 1
                    op.ticket = ccount[op.eng]
                    op.sem = csem[op.eng]
            per_eng = {e: [] for e in self.ALL}
            for op in self.ops:
                per_eng[op.eng].append(op)
            block = es.enter_context(nc.Block())

            def run(engname, eng):
                seen = {}

                def wait(sem, val):
                    k = id(sem)
                    if seen.get(k, 0) >= val:
                        return
                    seen[k] = val
                    eng.wait_ge(sem, val)

                for op in per_eng[engname]:
                    for d in op.deps:
                        wait(d.sem, d.ticket)
                    if op.dma and op.prev_ticket > 0:
                        wait(op.sem, op.prev_ticket)
                    ins = op.fn(eng)
                    if op.dma:
                        ins.then_inc(op.sem, 16)
                    elif op.signal:
                        ins.then_inc(op.sem, 1)
                if engname == "sp":
                    for op in final_waits:
                        wait(op.sem, op.ticket)

            @block.tensor
            def _(e):
                run("pe", e)

            @block.scalar
            def _(e):
                run("act", e)

            @block.vector
            def _(e):
                run("dve", e)

            @block.gpsimd
            def _(e):
                run("pool", e)

            @block.sync
            def _(e):
                run("sp", e)


class Prog:
    def __init__(self, debug=False, upto=99, sub=9, nheads=4, lite=False, skip=()):
        self.lite = lite
        self.skip = skip
        self.debug = debug
        self.upto = upto
        self.sub = sub
        self.nheads = nheads
        self.nc = bass.Bass("TRN2", target_bir_lowering=False)
        self.S = Sched(self.nc)
        self.fin = []
        self.dbg_names = []

    def din(self, name, shape, dt=F32):
        return self.nc.dram_tensor(name, list(shape), dt, kind="ExternalInput").ap()

    def dscratch(self, name, shape, dt):
        if self.debug:
            self.dbg_names.append(name)
            return self.nc.dram_tensor(name, list(shape), dt, kind="ExternalOutput").ap()
        return self.nc.dram_tensor(name, list(shape), dt, kind="Internal").ap()

    def dump(self, name, ap, reads, dt=F32):
        if not self.debug:
            return
        self.dbg_names.append(name)
        d = self.nc.dram_tensor(name, list(ap.shape), dt, kind="ExternalOutput").ap()
        self.fin.append(self.S.add("sp", lambda e: e.dma_start(out=d, in_=ap), reads=reads, dma=True))

    def dma(self, q, out, in_, reads=(), writes=()):
        return self.S.add(q, lambda e: e.dma_start(out=out, in_=in_), reads=reads, writes=writes, dma=True)

    def mm(self, ps, lhsT, rhs, start, stop, reads, writes):
        return self.S.add("pe", lambda e: e.matmul(ps, lhsT=lhsT, rhs=rhs, start=start, stop=stop),
                          reads=reads, writes=writes)

    def act(self, out, in_, func, reads, writes, **kw):
        return self.S.add("act", lambda e: e.activation(out=out, in_=in_, func=func, **kw), reads=reads, writes=writes)

    def ts(self, eng, out, in0, s1, s2, op0, op1, reads, writes):
        if op1 is None:
            return self.S.add(eng, lambda e: e.tensor_scalar(out=out, in0=in0, scalar1=s1, scalar2=None, op0=op0),
                              reads=reads, writes=writes)
        return self.S.add(eng, lambda e: e.tensor_scalar(out=out, in0=in0, scalar1=s1, scalar2=s2, op0=op0, op1=op1),
                          reads=reads, writes=writes)

    def tt(self, eng, out, in0, in1, op, reads, writes):
        return self.S.add(eng, lambda e: e.tensor_tensor(out=out, in0=in0, in1=in1, op=op), reads=reads, writes=writes)

    def stt(self, out, in0, scalar, in1, op0, op1, reads, writes):
        return self.S.add("dve", lambda e: e.scalar_tensor_tensor(out=out, in0=in0, scalar=scalar, in1=in1, op0=op0, op1=op1),
                          reads=reads, writes=writes)

    def copy(self, eng, out, in_, reads, writes):
        return self.S.add(eng, lambda e: e.tensor_copy(out=out, in_=in_), reads=reads, writes=writes)

    def memset(self, eng, ap, val, writes):
        return self.S.add(eng, lambda e: e.memset(ap, val), writes=writes)

    def recip(self, out, in_, reads, writes):
        return self.S.add("dve", lambda e: e.reciprocal(out=out, in_=in_), reads=reads, writes=writes)

    def build(self):
        nc, S = self.nc, self.S
        P = self
        xT = P.din("xT", [KC, 128, NT])
        ccT = P.din("ccT", [128, KC, 2])
        consts = P.din("consts", [128, 5, 128])
        w_mod = None if self.lite else P.din("w_mod", [D, 6 * D])
        bmodT = P.din("bmodT", [128, 96])
        gains = P.din("gains", [128, 4, KC])
        w_in = None if (self.lite and self.upto < 3) else P.din("w_in", [D, 6160])
        wg = P.din("wg", [D, 16])
        bg = P.din("bg", [128, 16])
        mhg = P.din("mhg", [128, 8])
        cw5 = P.din("cw5", [128, 8, 5])
        cb = P.din("cb", [128, 8])
        lwa = P.din("lwa", [128, 2, 8, 128])
        lwx = P.din("lwx", [128, 2, 8, 128])
        lba = P.din("lba", [128, 2, 8])
        lbx = P.din("lbx", [128, 2, 8])
        llam = P.din("llam", [128, 2, 8])
        w_out = None if self.lite else P.din("w_out", [D, D])
        w_ffn_in = None if self.lite else P.din("w_ffn_in", [D, 2 * DFF])
        w_ffn_out = None if self.lite else P.din("w_ffn_out", [DFF, D])
        outT = nc.dram_tensor("outT", [KC, 128, NOWN], F32, kind="ExternalOutput").ap()
        hxS = P.dscratch("hxS", [KC, 128, NT], BF16)
        yT = P.dscratch("yT", [KC, 128, NOWN], BF16)
        yxT = P.dscratch("yxT", [KC, 128, NOWN], F32)
        xmidT = P.dscratch("xmidT", [KC, 128, NOWN], F32)
        ffnT = P.dscratch("ffnT", [KC, 128, NOWN], F32)
        XRs = self.nc.dram_tensor("XRs", [8, 128, NT], F32, kind="Internal").ap()
        GGs = self.nc.dram_tensor("GGs", [8, 128, NOWN], F32, kind="Internal").ap()

        def pk(ap):
            return ap.rearrange("k p t -> p k t")

        with ExitStack() as top:
            sbt = lambda name, shape, dt: top.enter_context(nc.sbuf_tensor(name, list(shape), dt))
            ps = [top.enter_context(nc.psum_tensor("ps%d" % i, [128, 512], F32)) for i in range(8)]
            PS = lambda i: ("ps", i)
            cst = sbt("cst", [128, 5, 128], F32)
            cstb = sbt("cstb", [128, 5, 128], BF16)
            ones_f = sbt("ones_f", [128, 128], F32)
            ones_b = sbt("ones_b", [128, 128], BF16)
            P.dma("sp", cst[:], consts, writes=["cst"])
            P.copy("dve", cstb[:], cst[:], ["cst"], ["cstb"])
            P.memset("dve", ones_f[:], 1.0, ["ones_f"])
            P.memset("dve", ones_b[:], 1.0, ["ones_b"])
            U_f, L_f = cst[:, 0, :], cst[:, 1, :]
            ident_b = cstb[:, 2, :]
            mneg_b = [cstb[:, 3, :], cstb[:, 4, :]]
            mod = sbt("mod", [128, 96, 2], F32)
            vec = sbt("vec", [128, 8, KC], F32)
            gn = sbt("gn", [128, 4, KC], F32)
            P.dma("sp", gn[:], gains, writes=["gn"])

            with ExitStack() as st:
                sb = lambda name, shape, dt: st.enter_context(nc.sbuf_tensor(name, list(shape), dt))
                cc = sb("cc", [128, KC, 2], F32)
                ccs = sb("ccs", [128, KC, 2], BF16)
                bm = sb("bm", [128, 96], F32)
                P.dma("sp", cc[:], ccT, writes=["cc"])
                P.dma("sp", bm[:], bmodT, writes=["bm"])
                P.act(ccs[:], cc[:], AF.Silu, ["cc"], ["ccs"])
                wmb = [sb("wmb%d" % i, [128, KC, 512], BF16) for i in range(3)]
                wv = None if self.lite else w_mod.rearrange("(k p) n -> p k n", p=128)
                if self.lite:
                    P.memset("dve", ps[0][:, 0:192], 0.0, [PS(0)])
                for g in range(0 if self.lite else 24):
                    bi = g % 3
                    for q_ in range(2):
                        P.dma("pool", wmb[bi][:, 8 * q_:8 * q_ + 8, :], wv[:, 8 * q_:8 * q_ + 8, g * 512:(g + 1) * 512],
                              writes=[("wmb", bi, q_)])
                    for jj in range(4):
                        j = g * 4 + jj
                        for k in range(KC):
                            P.mm(ps[0][:, 2 * j:2 * j + 2], wmb[bi][:, k, jj * 128:(jj + 1) * 128], ccs[:, k, :],
                                 k == 0, k == KC - 1, [("wmb", bi, k // 8), "ccs"], [PS(0)])
                psv = ps[0][:, 0:192].rearrange("p (j t) -> p j t", t=2)
                for t in range(2):
                    P.tt("dve", mod[:, :, t], psv[:, :, t], bm[:], ALU.add, [PS(0), "bm"], [("mod", t)])
                m = lambda i, t: mod[:, i * KC:(i + 1) * KC, t]
                for t in range(2):
                    P.stt(vec[:, 0 + 2 * t, :], m(1, t), 1.0, gn[:, 0, :], ALU.add, ALU.mult, [("mod", t), "gn"], [("vec", 0 + 2 * t)])
                    P.copy("dve", vec[:, 1 + 2 * t, :], m(0, t), [("mod", t)], [("vec", 1 + 2 * t)])
                P.tt("dve", vec[:, 4, :], m(2, 0), gn[:, 1, :], ALU.mult, [("mod", 0), "gn"], [("vec", 4)])
                P.stt(vec[:, 5, :], m(4, 0), 1.0, gn[:, 2, :], ALU.add, ALU.mult, [("mod", 0), "gn"], [("vec", 5)])
                P.copy("dve", vec[:, 6, :], m(3, 0), [("mod", 0)], [("vec", 6)])
                P.tt("dve", vec[:, 7, :], m(5, 0), gn[:, 3, :], ALU.mult, [("mod", 0), "gn"], [("vec", 7)])
                P.dump("d_mod", mod[:], [("mod", 0), ("mod", 1)])
                P.dump("d_vec", vec[:], [("vec", i) for i in range(8)])
                S.barrier()
            VEC = [("vec", i) for i in range(8)]
            if self.upto < 1:
                return self.finish(outT)

            def rms_tiles(st, src_of, ntok_tiles, gs_idx_of, dst_of, tagp):
                sb = lambda name, shape, dt: st.enter_context(nc.sbuf_tensor(name, list(shape), dt))
                xt = [sb(tagp + "xt%d" % i, [128, KC, 512], F32) for i in range(2)]
                sq = sb(tagp + "sq", [128, KC, 512], BF16)
                sd = sb(tagp + "sd", [128, 512], F32)
                rs = sb(tagp + "rs", [128, 512], F32)
                hx = [sb(tagp + "hx%d" % i, [128, KC, 512], BF16) for i in range(2)]
                for i in range(ntok_tiles):
                    src, n, rk = src_of(i)
                    bi = i % 2
                    for q_ in range(4):
                        P.dma("sp", xt[bi][:, 4 * q_:4 * q_ + 4, 0:n], src[:, 4 * q_:4 * q_ + 4, :], reads=rk, writes=[(tagp + "xt", bi, q_)])
                    XTK = [(tagp + "xt", bi, q_) for q_ in range(4)]
                    P.act(sq[:, :, 0:n], xt[bi][:, :, 0:n], AF.Square, XTK, [tagp + "sq"])
                    for k in range(KC):
                        P.mm(ps[1][:, 0:n], ones_b[:], sq[:, k, 0:n], k == 0, k == KC - 1, [tagp + "sq", "ones_b"], [PS(1)])
                    P.act(sd[:, 0:n], ps[1][:, 0:n], AF.Sqrt, [PS(1)], [tagp + "sd"], scale=1.0 / D, bias=EPS)
                    P.recip(rs[:, 0:n], sd[:, 0:n], [tagp + "sd"], [tagp + "rs"])
                    P.tt("dve", xt[bi][:, :, 0:n], xt[bi][:, :, 0:n], rs[:, 0:n].unsqueeze(1).broadcast_to([128, KC, n]),
                         ALU.mult, XTK + [tagp + "rs"], XTK)
                    gi = gs_idx_of(i)
                    for k in range(KC):
                        P.act(hx[bi][:, k, 0:n], xt[bi][:, k, 0:n], AF.Identity, [(tagp + "xt", bi, k // 4)] + VEC, [(tagp + "hx", bi, k)],
                              scale=vec[:, gi, k:k + 1], bias=vec[:, gi + 1, k:k + 1])
                    dst, wk = dst_of(i)
                    P.dma("act", dst, hx[bi][:, :, 0:n], reads=[(tagp + "hx", bi, k) for k in range(KC)], writes=wk)

            with ExitStack() as st:
                xTv = pk(xT)
                hxv = pk(hxS)

                def src1(i):
                    if i == 0:
                        return xTv[:, :, 0:256], 256, []
                    return xTv[:, :, 256 + (i - 1) * 512: 256 + i * 512], 512, []

                def dst1(i):
                    if i == 0:
                        return hxv[:, :, 0:256], [("hxS", 0)]
                    a = 256 + (i - 1) * 512
                    return hxv[:, :, a:a + 512], [("hxS", 2 * i - 1), ("hxS", 2 * i)]

                rms_tiles(st, src1, 9, lambda i: 2 if i == 0 else 0, dst1, "s1")
                S.barrier()
            if self.upto < 2:
                return self.finish(outT)
            hxv = pk(hxS)
            HXK = lambda a, n: [("hxS", c) for c in range(a // 128, (a + n) // 128)]
            w_in_v = None if w_in is None else w_in.rearrange("(k p) n -> p k n", p=128)

            with ExitStack() as st:
                sb = lambda name, shape, dt: st.enter_context(nc.sbuf_tensor(name, list(shape), dt))
                l1 = sb("l1", [128, NCH, 8], F32)
                bias_ = sb("bias_", [128, NCH, 8], F32)
                winter = sb("winter", [128, NCH, 8], F32)
                wloc = sb("wloc", [128, NCH, 8], F32)
                decay = sb("decay", [128, NCH, 8], F32)
                GK = ["l1", "bias_", "winter", "wloc", "decay"]
                hxt = [sb("hxt%d" % i, [128, KC, 256], BF16) for i in range(2)]
                with ExitStack() as st2:
                    sb2 = lambda name, shape, dt: st2.enter_context(nc.sbuf_tensor(name, list(shape), dt))
                    wgb = sb2("wgb", [128, KC, 16], BF16)
                    bgs = sb2("bgs", [128, 16], F32)
                    gts = sb2("gts", [128, NCH, 16], F32)
                    e1 = sb2("e1", [128, NCH, 8], F32)
                    tmpw = sb2("tmpw", [128, NCH, 8], F32)
                    P.dma("pool", wgb[:], wg.rearrange("(k p) n -> p k n", p=128), writes=["wgb"])
                    P.dma("sp", bgs[:], bg, writes=["bgs"])
                    Wxr = sb2("Wxr", [128, KC, 1024], BF16)
                    Wgr = sb2("Wgr", [128, KC, 1024], BF16)
                    xrs = [sb2("xrs%d" % i, [128, 8, 256], F32) for i in range(2)]
                    ggs = [sb2("ggs%d" % i, [128, 8, 256], F32) for i in range(2)]
                    for hh_ in range(2):
                        P.dma("pool", Wxr[:, :, 512 * hh_:512 * (hh_ + 1)], w_in_v[:, :, 4112 + 512 * hh_:4112 + 512 * (hh_ + 1)],
                              writes=[("Wxr", hh_)])
                        P.dma("pool", Wgr[:, :, 512 * hh_:512 * (hh_ + 1)], w_in_v[:, :, 5136 + 512 * hh_:5136 + 512 * (hh_ + 1)],
                              writes=[("Wgr", hh_)])
                    rot2 = [0]
                    BK2 = [0, 1, 4, 5, 6, 7]

                    def nb2():
                        rot2[0] = (rot2[0] + 1) % 6
                        return BK2[rot2[0]]
                    if "cut1" in self.skip:
                        return self.finish(outT)
                    for i in range(NT // 256):
                        bi = i % 2
                        for q_ in range(2):
                            P.dma("sp", hxt[bi][:, 8 * q_:8 * q_ + 8, :], hxv[:, 8 * q_:8 * q_ + 8, 256 * i:256 * i + 256],
                                  reads=HXK(256 * i, 256), writes=[("hxt", bi, q_)])
                        for cc in range(2):
                            c = 2 * i + cc
                            bank, col = (2 if c < 17 else 3), (c % 17) * 16
                            for k in range(KC):
                                P.mm(ps[bank][:, col:col + 16], hxt[bi][:, k, cc * 128:(cc + 1) * 128], wgb[:, k, :],
                                     k == 0, k == KC - 1, [("hxt", bi, k // 8), "wgb"], [PS(bank)])
                        for n in range(8):
                            bk = nb2()
                            for k in range(KC):
                                P.mm(ps[bk][:, 0:256], Wxr[:, k, 128 * n:128 * (n + 1)], hxt[bi][:, k, :], k == 0, k == KC - 1,
                                     [("hxt", bi, k // 8), ("Wxr", n // 4)], [PS(bk)])
                            if n % 2 == 0:
                                P.copy("dve", xrs[bi][:, n, :], ps[bk][:, 0:256], [PS(bk)], [("xrs", bi, n)])
                            else:
                                P.act(xrs[bi][:, n, :], ps[bk][:, 0:256], AF.Copy, [PS(bk)], [("xrs", bi, n)])
                        P.dma("act", XRs.rearrange("n p t -> p n t")[:, :, 256 * i:256 * i + 256], xrs[bi][:],
                              reads=[("xrs", bi, n) for n in range(8)], writes=[("XRs", i)])
                        if 1 <= i <= 8:
                            for n in range(8):
                                bk = nb2()
                                for k in range(KC):
                                    P.mm(ps[bk][:, 0:256], Wgr[:, k, 128 * n:128 * (n + 1)], hxt[bi][:, k, :], k == 0, k == KC - 1,
                                         [("hxt", bi, k // 8), ("Wgr", n // 4)], [PS(bk)])
                                P.act(ggs[bi][:, n, :], ps[bk][:, 0:256], AF.Gelu_apprx_tanh, [PS(bk)], [("ggs", bi, n)])
                            P.dma("act", GGs.rearrange("n p t -> p n t")[:, :, 256 * (i - 1):256 * i], ggs[bi][:],
                                  reads=[("ggs", bi, n) for n in range(8)], writes=[("GGs", i - 1)])
                    if "cut2" in self.skip:
                        return self.finish(outT)
                    g4 = lambda ap: ap.rearrange("p c (t h) -> p c t h", h=4)
                    for half_, (ca, cb_, bank) in enumerate(((0, 17, 2), (17, 34, 3))):
                        P.tt("dve", gts[:, ca:cb_, :], ps[bank][:, 0:272].rearrange("p (c g) -> p c g", g=16),
                             bgs[:].unsqueeze(1).broadcast_to([128, 17, 16]), ALU.add, [PS(bank), "bgs"], [("gts", half_)])
                    GT = [("gts", 0), ("gts", 1)]
                    if "cut3" in self.skip:
                        return self.finish(outT)
                    P.act(g4(e1[:]), g4(gts[:])[:, :, 1::2, :], AF.Exp, GT, ["e1"], scale=-1.0)
                    P.act(l1[:], e1[:], AF.Ln, ["e1"], ["l1"], bias=1.0)
                    if "cut4" in self.skip:
                        return self.finish(outT)
                    for c in range(NCH):
                        bank, col = (4 if c < 17 else 5), (c % 17) * 16
                        P.mm(ps[bank][:, col:col + 4], U_f, l1[:, c, 0:4], True, True, ["l1", "cst"], [PS(bank)])
                        P.mm(ps[bank][:, col + 4:col + 8], L_f, l1[:, c, 4:8], True, True, ["l1", "cst"], [PS(bank)])
                        P.mm(ps[bank][:, col + 8:col + 16], ones_f[:], l1[:, c, 0:8], True, True, ["l1", "ones_f"], [PS(bank)])
                    if "cut5" in self.skip:
                        return self.finish(outT)
                    for half_, (ca, cb_, bank) in enumerate(((0, 17, 4), (17, 34, 5))):
                        gp = ps[bank][:, 0:272].rearrange("p (c g) -> p c g", g=16)
                        P.tt("dve", g4(bias_[:, ca:cb_, :]), g4(gts[:, ca:cb_, :])[:, :, 0::2, :], g4(gp[:, :, 0:8]), ALU.add,
                             GT + [PS(bank)], [("bias_", half_)])
                        P.act(winter[:, ca:cb_, :], gp[:, :, 0:8], AF.Exp, [PS(bank)], [("winter", half_)], scale=-1.0)
                        P.act(decay[:, ca:cb_, :], gp[:, :, 8:16], AF.Exp, [PS(bank)], [("decay", half_)], scale=-1.0)
                        P.tt("dve", tmpw[:, ca:cb_, :], bias_[:, ca:cb_, :], gp[:, :, 8:16], ALU.subtract,
                             [("bias_", half_), PS(bank)], [("tmpw", half_)])
                        P.act(wloc[:, ca:cb_, :], tmpw[:, ca:cb_, :], AF.Exp, [("tmpw", half_)], [("wloc", half_)])
                    if "cut6" in self.skip:
                        return self.finish(outT)
                    P.dump("d_gts", gts[:], GT)
                    P.dump("d_l1", l1[:], ["l1"])
                    P.dump("d_bias", bias_[:], [("bias_", 0), ("bias_", 1)])
                    P.dump("d_winter", winter[:], [("winter", 0), ("winter", 1)])
                    P.dump("d_wloc", wloc[:], [("wloc", 0), ("wloc", 1)])
                    P.dump("d_decay", decay[:], [("decay", 0), ("decay", 1)])
                    if "cut7" in self.skip:
                        return self.finish(outT)
                    S.barrier()
                if self.upto < 3:
                    return self.finish(outT)
                GK = ["l1"] + [(n_, h_) for n_ in ("bias_", "winter", "wloc", "decay") for h_ in (0, 1)]

                Wkv = sb("Wkv", [128, KC, 512], BF16)
                Wq = sb("Wq", [128, KC, 256], BF16)
                Wo = sb("Wo", [128, KC, 256], BF16)
                QT = sb("QT", [128, 2, NOWN], BF16)
                KT = sb("KT", [128, 2, NOWN], BF16)
                OT = sb("OT", [128, 2, NOWN], BF16)
                Ktok = sb("Ktok", [128, NCH, 256], BF16)
                V1 = sb("V1", [128, NCH, 258], BF16)
                hs = sb("hs", [128, 16, 256], F32)
                CT = [sb("CT%d" % d_, [128, 2, 258], F32) for d_ in range(2)]
                CTb = [sb("CTb%d" % d_, [128, 2, 258], BF16) for d_ in range(2)]
                lfU = [[sb("lfU%d%d" % (d_, p_), [128, 128], F32) for p_ in range(2)] for d_ in range(2)]
                DT = [[sb("DT%d%d" % (d_, p_), [128, 128], F32) for p_ in range(2)] for d_ in range(2)]
                PT = [[sb("PT%d%d" % (d_, p_), [128, 128], BF16) for p_ in range(2)] for d_ in range(2)]
                isb = [[sb("isb%d%d" % (d_, p_), [128, 258], F32) for p_ in range(2)] for d_ in range(2)]
                nd = [sb("nd%d" % d_, [128, 258], F32) for d_ in range(2)]
                den = [sb("den%d" % d_, [128, 1], F32) for d_ in range(2)]
                rden = [sb("rden%d" % d_, [128, 1], F32) for d_ in range(2)]
                Vw = [[sb("Vw%d%d" % (d_, p_), [128, 258], BF16) for p_ in range(2)] for d_ in range(2)]
                junk = sb("junk", [128, 256], F32)
                ssq = sb("ssq", [128, 16], F32)
                ssd = sb("ssd", [128, 16], F32)
                srs = sb("srs", [128, 16], F32)
                hn = sb("hn", [128, 16, 256], BF16)
                ymT = sb("ymT", [128, 2, NOWN], BF16)
                mh = sb("mh", [128, 8], F32)
                P.dma("sp", mh[:], mhg, writes=["mh"])
                P.memset("pool", V1[:, :, 256:257], 1.0, ["V1one"])
                P.memset("pool", V1[:, :, 257:258], 0.0, ["V1pad"])
                masks = [U_f, L_f]

                def front(h, d_, c, produce, update, par):
                    idx = d_ * 4 + h
                    A, B = 4 * d_, 4 * d_ + 1
                    if produce:
                        oc = c - 2
                        tok = oc * 128
                        lf_, dt_, pt_, is_ = lfU[d_][par], DT[d_][par], PT[d_][par], isb[d_][par]
                        P.ts("dve", lf_[:], masks[d_], l1[:, c, idx:idx + 1], -1.0, ALU.mult, ALU.mult,
                             ["cst", "l1"], [("lfU", d_, par)])
                        P.mm(ps[A][:, 0:128], ones_f[:], lf_[:], True, False, ["ones_f", ("lfU", d_, par)], [PS(A)])
                        P.mm(ps[A][:, 0:128], ident_b, mneg_b[d_], False, True, ["cstb"], [PS(A)])
                        P.act(dt_[:], ps[A][:, 0:128], AF.Exp, [PS(A)] + GK, [("DT", d_, par)], bias=bias_[:, c, idx:idx + 1], scale=1.0)
                        for ec in range(2):
                            P.mm(ps[A][:, 128:256], KT[:, ec, tok:tok + 128], QT[:, ec, tok:tok + 128], ec == 0, ec == 1,
                                 [("KT", oc // 2), ("QT", oc // 2)], [PS(A)])
                        P.tt("dve", pt_[:], ps[A][:, 128:256], dt_[:], ALU.mult, [PS(A), ("DT", d_, par)], [("PT", d_, par)])
                        P.mm(ps[B][:, 0:257], pt_[:], V1[:, c, 0:257], True, True, [("PT", d_, par), ("V1", c), "V1one"], [PS(B)])
                        P.act(is_[:, 0:257], ps[B][:, 0:257], AF.Copy, [PS(B)], [("isb", d_, par)])
                    if update:
                        P.ts("pool", Vw[d_][par][:, 0:257], V1[:, c, 0:257], wloc[:, c, idx:idx + 1], 1.0, ALU.mult, ALU.mult,
                             [("V1", c), "V1one"] + GK, [("Vw", d_, par)])

                def back(h, d_, c, produce, update, par):
                    idx = d_ * 4 + h
                    C_, Dk = 4 * d_ + 2, 4 * d_ + 3
                    if produce:
                        oc = c - 2
                        tok = oc * 128
                        for ec in range(2):
                            P.mm(ps[C_][:, 0:257], QT[:, ec, tok:tok + 128], CTb[d_][:, ec, 0:257], ec == 0, ec == 1,
                                 [("QT", oc // 2), ("CTb", d_, ec)], [PS(C_)])
                        P.stt(nd[d_][:, 0:257], ps[C_][:, 0:257], winter[:, c, idx:idx + 1], isb[d_][par][:, 0:257], ALU.mult, ALU.add,
                              [PS(C_), ("isb", d_, par)] + GK, [("nd", d_)])
                        P.ts("dve", den[d_][:], nd[d_][:, 256:257], -1.0, 1.0, ALU.mult, ALU.max, [("nd", d_)], [("den", d_)])
                        P.tt("dve", den[d_][:], den[d_][:], nd[d_][:, 256:257], ALU.max, [("nd", d_), ("den", d_)], [("den", d_)])
                        P.recip(rden[d_][:], den[d_][:], [("den", d_)], [("rden", d_)])
                        P.stt(hs[:, oc, :], nd[d_][:, 0:256], rden[d_][:, 0:1], hs[:, oc, :], ALU.mult, ALU.add,
                              [("nd", d_), ("rden", d_), ("hs", oc)], [("hs", oc)])
                    if update:
                        for ec in range(2):
                            bank = Dk if ec == 0 else C_
                            P.mm(ps[bank][:, 0:257], Ktok[:, c, ec * 128:(ec + 1) * 128], Vw[d_][par][:, 0:257], True, True,
                                 [("Ktok", c), ("Vw", d_, par)], [PS(bank)])
                            P.stt(CT[d_][:, ec, 0:257], CT[d_][:, ec, 0:257], decay[:, c, idx:idx + 1], ps[bank][:, 0:257],
                                  ALU.mult, ALU.add, [("CT", d_, ec), PS(bank)] + GK, [("CT", d_, ec)])
                            P.act(CTb[d_][:, ec, 0:257], CT[d_][:, ec, 0:257], AF.Copy, [("CT", d_, ec)], [("CTb", d_, ec)])

                for h in range(self.nheads):
                    P.dma("pool", Wkv[:, :, 0:256], w_in_v[:, :, 1024 + 256 * h:1024 + 256 * (h + 1)], writes=["Wk"])
                    P.dma("pool", Wkv[:, :, 256:512], w_in_v[:, :, 2048 + 256 * h:2048 + 256 * (h + 1)], writes=["Wv"])
                    P.dma("pool", Wq[:], w_in_v[:, :, 256 * h:256 * (h + 1)], writes=["Wq"])
                    P.dma("pool", Wo[:], w_in_v[:, :, 3072 + 256 * h:3072 + 256 * (h + 1)], writes=["Wo"])
                    P.memset("dve", hs[:], 0.0, [("hs", oc) for oc in range(16)])
                    for d_ in range(2):
                        P.memset("dve", CT[d_][:], 0.0, [("CT", d_, 0), ("CT", d_, 1)])
                        P.memset("pool", CTb[d_][:], 0.0, [("CTb", d_, 0), ("CTb", d_, 1)])
                    rot = [0]

                    def nb():
                        rot[0] = (rot[0] + 1) % 6
                        return rot[0]

                    for i in range(NT // 256):
                        bi = i % 2
                        for q_ in range(2):
                            P.dma("sp", hxt[bi][:, 8 * q_:8 * q_ + 8, :], hxv[:, 8 * q_:8 * q_ + 8, 256 * i:256 * i + 256],
                                  reads=HXK(256 * i, 256), writes=[("hxt", bi, q_)])
                        for cc in range(2):
                            c = 2 * i + cc
                            bk = nb()
                            for k in range(KC):
                                P.mm(ps[bk][:, 0:512], hxt[bi][:, k, cc * 128:(cc + 1) * 128], Wkv[:, k, :], k == 0, k == KC - 1,
                                     [("hxt", bi, k // 8), "Wk", "Wv"], [PS(bk)])
                            P.act(Ktok[:, c, :], ps[bk][:, 0:256], AF.Copy, [PS(bk)], [("Ktok", c)], scale=0.0625)
                            P.copy("dve", V1[:, c, 0:256], ps[bk][:, 256:512], [PS(bk)], [("V1", c)])
                        if 1 <= i <= 8:
                            ti = i - 1
                            o = ti * 256
                            for wi, (W_, wk, dst, dk) in enumerate(((Wq, "Wq", QT, "QT"), (Wkv, "Wk", KT, "KT"), (Wo, "Wo", OT, "OT"))):
                                for ec in range(2):
                                    bk = nb()
                                    for k in range(KC):
                                        P.mm(ps[bk][:, 0:256], W_[:, k, ec * 128:(ec + 1) * 128], hxt[bi][:, k, :], k == 0, k == KC - 1,
                                             [("hxt", bi, k // 8), wk], [PS(bk)])
                                    if dk == "QT":
                                        P.copy("dve", dst[:, ec, o:o + 256], ps[bk][:, 0:256], [PS(bk)], [(dk, ti)])
                                    elif dk == "KT":
                                        P.act(dst[:, ec, o:o + 256], ps[bk][:, 0:256], AF.Copy, [PS(bk)], [(dk, ti)], scale=0.0625)
                                    else:
                                        P.act(dst[:, ec, o:o + 256], ps[bk][:, 0:256], AF.Sigmoid, [PS(bk)], [(dk, ti)])
                    if self.debug and h == 0:
                        P.dump("d_QT", QT[:], [("QT", t_) for t_ in range(8)], BF16)
                        P.dump("d_KT", KT[:], [("KT", t_) for t_ in range(8)], BF16)
                        P.dump("d_Ktok", Ktok[:], [("Ktok", c) for c in range(NCH)], BF16)
                        P.dump("d_V1", V1[:], [("V1", c) for c in range(NCH)] + ["V1one", "V1pad"], BF16)
                    if self.sub < 1:
                        continue
                    fwd = [(0, False, True), (1, False, True)] + [(c, True, c < 17) for c in range(2, 18)]
                    bwd = [(1, False, True), (0, False, True)] + [(c, False, True) for c in range(33, 17, -1)] + \
                          [(c, True, c > 2) for c in range(17, 1, -1)]
                    chains = [(0, fwd, 16), (1, bwd, 0)]
                    for t in range(-1, 34):
                        for (d_, steps, t0) in chains:
                            j = t + 1 - t0
                            if 0 <= j < len(steps):
                                front(h, d_, *steps[j], j % 2)
                        for (d_, steps, t0) in chains:
                            j = t - t0
                            if 0 <= j < len(steps):
                                back(h, d_, *steps[j], j % 2)
                    if self.debug and h == 0:
                        P.dump("d_hs", hs[:], [("hs", oc) for oc in range(16)])
                        P.dump("d_CT0", CT[0][:], [("CT", 0, 0), ("CT", 0, 1)])
                        P.dump("d_CT1", CT[1][:], [("CT", 1, 0), ("CT", 1, 1)])
                    if self.sub < 2:
                        continue
                    for oc in range(16):
                        S.add("act", (lambda oc: lambda e: e.activation(out=junk[:], in_=hs[:, oc, :], func=AF.Square,
                                                                      accum_out=ssq[:, oc:oc + 1]))(oc),
                              reads=[("hs", oc)], writes=["junk", ("ssq", oc)])
                    P.act(ssd[:], ssq[:], AF.Sqrt, [("ssq", oc) for oc in range(16)], ["ssd"], scale=1.0 / 256, bias=EPS)
                    P.recip(srs[:], ssd[:], ["ssd"], ["srs"])
                    P.tt("dve", hn[:], hs[:], srs[:].unsqueeze(2).broadcast_to([128, 16, 256]), ALU.mult,
                         [("hs", oc) for oc in range(16)] + ["srs"], ["hn"])
                    tbanks = [(ps[7][:].bitcast(BF16), PS(7)), (ps[6][:].bitcast(BF16), PS(6))]
                    for og in range(4):
                        tb, tk = tbanks[og % 2]
                        for oo in range(4):
                            oc = og * 4 + oo
                            for ec in range(2):
                                slot = oo * 2 + ec
                                S.add("pe", (lambda oc, ec, slot, tb: lambda e: e.transpose(out=tb[:, slot * 128:(slot + 1) * 128],
                                                                                              in_=hn[:, oc, ec * 128:(ec + 1) * 128],
                                                                                              identity=ident_b))(oc, ec, slot, tb),
                                      reads=["hn", "cstb"], writes=[tk])
                        for ec in range(2):
                            tv = tb.rearrange("p (o e t) -> p o e t", o=4, e=2)[:, :, ec, :]
                            P.stt(ymT[:, ec, og * 512:(og + 1) * 512].rearrange("p (o t) -> p o t", o=4), tv,
                                  mh[:, 2 * h + ec:2 * h + ec + 1],
                                  OT[:, ec, og * 512:(og + 1) * 512].rearrange("p (o t) -> p o t", o=4), ALU.mult, ALU.mult,
                                  [tk, "mh"] + [("OT", t_) for t_ in range(8)], [("ymT", ec)])
                    for ec in range(2):
                        P.dma("act", yT[2 * h + ec], ymT[:, ec, :], reads=[("ymT", ec)], writes=[("yT", 2 * h + ec)])
                S.barrier()
            if self.upto < 4:
                return self.finish(outT)

            with ExitStack() as st:
                sb = lambda name, shape, dt: st.enter_context(nc.sbuf_tensor(name, list(shape), dt))
                lwa_b = sb("lwa_b", [128, 2, 8, 128], BF16)
                lwx_b = sb("lwx_b", [128, 2, 8, 128], BF16)
                cw = sb("cw", [128, 8, 5], F32)
                cbv = sb("cbv", [128, 8], F32)
                ba = sb("ba", [128, 2, 8], F32)
                bx = sb("bx", [128, 2, 8], F32)
                lam = sb("lam", [128, 2, 8], F32)
                spl = sb("spl", [128, 2, 8], F32)
                s1 = sb("s1", [128, 2, 8], F32)
                s2 = sb("s2", [128, 2, 8], F32)
                xs = [sb("xs%d" % i, [128, NT], F32) for i in range(1)] * 2
                XR = sb("XR", [128, NT], F32)
                xc = sb("xc", [128, NT], F32)
                xcb = sb("xcb", [128, NT], BF16)
                rr = [sb("rr%d" % i, [128, NT], F32) for i in range(2)]
                ii = [sb("ii%d" % i, [128, NT], F32) for i in range(2)]
                aa = [sb("aa%d" % i, [128, NT], F32) for i in range(2)]
                hsum = sb("hsum", [128, NT], F32)
                GG = [sb("GG0", [128, NOWN], F32)] * 2
                yr = sb("yr", [128, NOWN], BF16)
                P.dma("pool", lwa_b[:], lwa, writes=["lwa_b"])
                P.dma("pool", lwx_b[:], lwx, writes=["lwx_b"])
                P.dma("sp", cw[:], cw5, writes=["cw"])
                P.dma("sp", cbv[:], cb, writes=["cbv"])
                P.dma("sp", ba[:], lba, writes=["ba"])
                P.dma("sp", bx[:], lbx, writes=["bx"])
                P.dma("sp", lam[:], llam, writes=["lam"])
                P.act(spl[:], lam[:], AF.Exp, ["lam"], ["spl"], scale=-1.0)
                P.act(spl[:], spl[:], AF.Ln, ["spl"], ["spl"], bias=1.0)
                P.ts("dve", s1[:], spl[:], -8.0, None, ALU.mult, None, ["spl"], ["s1"])
                P.ts("dve", s2[:], spl[:], -16.0, None, ALU.mult, None, ["spl"], ["s2"])
                SM = ["cw", "cbv", "ba", "bx", "s1", "s2"]
                rot = [0]

                def nb():
                    rot[0] = (rot[0] + 1) % 8
                    return rot[0]

                XRl = XR[:, NCTX:NT].rearrange("p (r w) -> p w r", w=64)
                HSl = hsum[:, NCTX:NT].rearrange("p (r w) -> p w r", w=64)
                TT = [(t * 512, 512) for t in range(8)] + [(4096, 256)]
                XRK = [("XRs", i) for i in range(NT // 256)]
                GGK = [("GGs", i) for i in range(8)]
                nblk = 8 if "lru1" not in self.skip else 1

                def load(n):
                    P.dma("sp", xs[0][:], XRs[n], reads=XRK, writes=[("xs", 0)])

                load(0)
                for n in range(nblk):
                    xsn = xs[n % 2]
                    P.dma("sp", GG[0][:], GGs[n], reads=GGK, writes=[("GG", 0)])
                    P.copy("pool", XR[:, 0:NCTX], xsn[:, 0:NCTX], [("xs", 0)], ["XRc"])
                    for q_ in range(2):
                        src = xsn[:, NCTX:NT].rearrange("p (w r) -> p w r", r=64)[:, 32 * q_:32 * (q_ + 1), :]
                        dst = XRl[:, 32 * q_:32 * (q_ + 1), :]
                        if q_ == 0:
                            P.copy("pool", dst, src, [("xs", 0)], [("XRl", q_)])
                        else:
                            P.act(dst, src, AF.Copy, [("xs", 0)], [("XRl", q_)])
                    XK = ["XRc", ("XRl", 0), ("XRl", 1)]
                    if n + 1 < nblk:
                        load(n + 1)
                    P.ts("dve", xc[:], XR[:], cw[:, n, 2:3], cbv[:, n:n + 1], ALU.mult, ALU.add, XK + SM, ["xc"])
                    for j in (0, 1, 3, 4):
                        o = j - 2
                        for (a_, b_) in ((0, NCTX), (NCTX, NT)):
                            lo, hi = max(a_, a_ - o), min(b_, b_ - o)
                            P.stt(xc[:, lo:hi], XR[:, lo + o:hi + o], cw[:, n, j:j + 1], xc[:, lo:hi], ALU.mult, ALU.add,
                                  XK + ["xc"] + SM, ["xc"])
                    P.act(xcb[:], xc[:], AF.Copy, ["xc"], ["xcb"])
                    if self.debug and n == 0:
                        P.dump("d_xc", xc[:], ["xc"])
                    for d_ in range(2):
                        R_, I_, A_ = ("rr", d_), ("ii", d_), ("aa", d_)
                        for (t0, tn) in TT:
                            b1, b2 = nb(), nb()
                            P.mm(ps[b1][:, 0:tn], lwa_b[:, d_, n, :], xcb[:, t0:t0 + tn], True, True, ["lwa_b", "xcb"], [PS(b1)])
                            P.mm(ps[b2][:, 0:tn], lwx_b[:, d_, n, :], xcb[:, t0:t0 + tn], True, True, ["lwx_b", "xcb"], [PS(b2)])
                            P.act(rr[d_][:, t0:t0 + tn], ps[b1][:, 0:tn], AF.Sigmoid, [PS(b1)] + SM, [R_], bias=ba[:, d_, n:n + 1], scale=1.0)
                            P.act(ii[d_][:, t0:t0 + tn], ps[b2][:, 0:tn], AF.Sigmoid, [PS(b2)] + SM, [I_], bias=bx[:, d_, n:n + 1], scale=1.0)
                    for d_ in range(2):
                        R_, I_, A_ = ("rr", d_), ("ii", d_), ("aa", d_)
                        P.act(aa[d_][:], rr[d_][:], AF.Exp, [R_] + SM, [A_], scale=s1[:, d_, n:n + 1])
                        P.act(rr[d_][:], rr[d_][:], AF.Exp, [R_] + SM, [R_], scale=s2[:, d_, n:n + 1])
                    for d_ in range(2):
                        R_, I_, A_ = ("rr", d_), ("ii", d_), ("aa", d_)
                        P.tt("pool", ii[d_][:], ii[d_][:], xc[:], ALU.mult, [I_, "xc"], [I_])
                    for d_ in range(2):
                        R_, I_, A_ = ("rr", d_), ("ii", d_), ("aa", d_)
                        P.ts("pool", rr[d_][:], rr[d_][:], 1.0, 0.0, ALU.min, ALU.max, [R_], [R_])
                        P.act(rr[d_][:], rr[d_][:], AF.Sqrt, [R_], [R_], scale=-1.0, bias=1.0)
                    for d_ in range(2):
                        R_, I_, A_ = ("rr", d_), ("ii", d_), ("aa", d_)
                        P.tt("dve", rr[d_][:], rr[d_][:], ii[d_][:], ALU.mult, [R_, I_], [R_])
                    S.add("dve", lambda e: e.tensor_tensor_scan(out=hsum[:], data0=aa[0][:], data1=rr[0][:], initial=0.0,
                                                                op0=ALU.mult, op1=ALU.add),
                          reads=[("aa", 0), ("rr", 0)], writes=["hsum"])
                    S.add("dve", lambda e: e.tensor_tensor_scan(out=XR[:, 0:NCTX][:, ::-1], data0=aa[1][:, 0:NCTX][:, ::-1],
                                                                data1=rr[1][:, 0:NCTX][:, ::-1], initial=0.0,
                                                                op0=ALU.mult, op1=ALU.add),
                          reads=[("aa", 1), ("rr", 1)], writes=XK)
                    S.add("dve", lambda e: e.tensor_tensor_scan(out=XR[:, NCTX:NT][:, ::-1], data0=aa[1][:, NCTX:NT][:, ::-1],
                                                                data1=rr[1][:, NCTX:NT][:, ::-1], initial=XR[:, 0:1],
                                                                op0=ALU.mult, op1=ALU.add),
                          reads=[("aa", 1), ("rr", 1)] + XK, writes=XK)
                    P.tt("pool", hsum[:, NCTX:NT], hsum[:, NCTX:NT], XR[:, NCTX:NT], ALU.add, ["hsum"] + XK, ["hsum"])
                    if self.debug and n == 0:
                        P.dump("d_hsum", hsum[:], ["hsum"])
                    P.tt("dve", yr[:].rearrange("p (w r) -> p w r", r=64), GG[n % 2][:].rearrange("p (w r) -> p w r", r=64),
                         HSl[:, 0:32, :], ALU.mult, [("GG", 0), "hsum"], ["yr"])
                    P.dma("act", yT[8 + n], yr[:], reads=["yr"], writes=[("yT", 8 + n)])
                S.barrier()
            if self.upto < 5:
                return self.finish(outT)

            with ExitStack() as st:
                sb = lambda name, shape, dt: st.enter_context(nc.sbuf_tensor(name, list(shape), dt))
                T = 1024
                hx2 = sb("hx2", [128, KC, T], BF16)
                aT = sb("aT", [128, NF, T], BF16)
                stg = [sb("stg%d" % i, [128, T], F32) for i in range(2)]
                sqf = [sb("sqf%d" % i, [128, 512], F32) for i in range(4)]
                sqsum = sb("sqsum", [128, T], F32)
                sdb = sb("sdb", [128, T], F32)
                rb = sb("rb", [128, T], F32)
                in1 = [sb("in1_%d" % i, [128, 512], F32) for i in range(4)]
                in2 = [sb("in2_%d" % i, [128, 512], F32) for i in range(4)]
                sg = [sb("sg%d" % i, [128, T], BF16) for i in range(2)]
                WB = [sb("WB%d" % i, [128, 8192], BF16) for i in range(2)]
                Wo4 = [WB[i][:, :].rearrange("p (k n) -> p k n", n=512) for i in range(2)]
                Wg_ = [WB[i][:, 0:4096].rearrange("p (k n) -> p k n", n=256) for i in range(2)]
                Wu_ = [WB[i][:, 4096:8192].rearrange("p (k n) -> p k n", n=256) for i in range(2)]
                Wf4 = [WB[i][:, 0:11 * 512].rearrange("p (f n) -> p f n", n=512) for i in range(2)]
                WK = lambda wi: [("WB", wi, q_) for q_ in range(4)]
                w_out_v = w_out.rearrange("(k p) n -> p k n", p=128)
                w_fi_v = w_ffn_in.rearrange("(k p) n -> p k n", p=128)
                w_fo_v = w_ffn_out.rearrange("(f p) n -> p f n", p=128)
                xTv = pk(xT)
                cnt = [0]
                wcnt = [0]
                H = lambda tt_: slice(tt_ * 512, (tt_ + 1) * 512)

                def rstd_from_sqsum():
                    for tt_ in range(2):
                        P.mm(ps[tt_][:, 0:512], ones_f[:], sqsum[:, H(tt_)], True, True, ["ones_f", ("sqsum", tt_)], [PS(tt_)])
                        P.act(sdb[:, H(tt_)], ps[tt_][:, 0:512], AF.Sqrt, [PS(tt_)], [("sdb", tt_)], scale=1.0 / D, bias=EPS)
                        P.recip(rb[:, H(tt_)], sdb[:, H(tt_)], [("sdb", tt_)], [("rb", tt_)])

                def accum_sq(src, src_keys, first, tt_):
                    i_ = cnt[0] % 4
                    cnt[0] += 1
                    if first:
                        P.act(sqsum[:, H(tt_)], src, AF.Square, src_keys, [("sqsum", tt_)])
                    else:
                        P.act(sqf[i_][:], src, AF.Square, src_keys, [("sqf", i_)])
                        P.tt("pool", sqsum[:, H(tt_)], sqsum[:, H(tt_)], sqf[i_][:], ALU.add, [("sqsum", tt_), ("sqf", i_)], [("sqsum", tt_)])

                def evac_group(cg, th, dst_dram, dst_key):
                    tok0 = th * T
                    for cc in range(4):
                        c = 4 * cg + cc
                        si = c % 2
                        for tt_ in range(2):
                            bk = cc * 2 + tt_
                            P.act(stg[si][:, H(tt_)], ps[bk][:, 0:512], AF.Copy, [PS(bk)], [("stg", si, tt_)])
                            accum_sq(stg[si][:, H(tt_)], [("stg", si, tt_)], c == 0, tt_)
                        P.dma("act", dst_dram[c][:, tok0:tok0 + T], stg[si][:], reads=[("stg", si, 0), ("stg", si, 1)],
                              writes=[(dst_key, c, th)])

                def norm_pass(th, a_dram, a_key, b_src, scale_idx, out_fn, crange=range(KC)):
                    tok0 = th * T
                    for c in crange:
                        for tt_ in range(2):
                            j = (2 * c + tt_) % 4
                            P.dma("sp", in1[j][:], a_dram[c][:, tok0 + tt_ * 512:tok0 + (tt_ + 1) * 512], reads=[(a_key, c, th)],
                                  writes=[("in1", j)])
                            bsrc, bk_ = b_src(c, tt_)
                            P.dma("pool", in2[j][:], bsrc, reads=bk_, writes=[("in2", j)])
                            P.tt("dve", in1[j][:], in1[j][:], rb[:, H(tt_)], ALU.mult, [("in1", j), ("rb", tt_)], [("in1", j)])
                            si = c % 2
                            P.stt(stg[si][:, H(tt_)], in1[j][:], vec[:, scale_idx, c:c + 1], in2[j][:], ALU.mult, ALU.add,
                                  [("in1", j), ("in2", j)] + VEC, [("stg", si, tt_)])
                            out_fn(c, tt_, si)

                pending = []

                def p7c(th_, crange):
                    t0_ = th_ * T

                    def out7c(c, tt_, si):
                        self.fin.append(P.dma("act", outT[c][:, t0_ + tt_ * 512:t0_ + (tt_ + 1) * 512], stg[si][:, H(tt_)],
                                              reads=[("stg", si, tt_)], writes=[("outT", c, th_, tt_)]))

                    norm_pass(th_, ffnT, "ffnT",
                              lambda c, tt_: (xmidT[c][:, t0_ + tt_ * 512:t0_ + (tt_ + 1) * 512], [("xmidT", c, th_, tt_)]),
                              7, out7c, crange)

                for th in range(2):
                    tok0 = th * T
                    P.dma("pool", aT[:, 0:KC, :], pk(yT)[:, :, tok0:tok0 + T], reads=[("yT", k) for k in range(KC)],
                          writes=[("aT", f) for f in range(KC)])
                    for cg in range(4):
                        wi = wcnt[0] % 2
                        wcnt[0] += 1
                        for q_ in range(4):
                            P.dma("pool", Wo4[wi][:, 4 * q_:4 * q_ + 4, :], w_out_v[:, 4 * q_:4 * q_ + 4, cg * 512:(cg + 1) * 512],
                                  writes=[("WB", wi, q_)])
                        for cc in range(4):
                            for tt_ in range(2):
                                bk = cc * 2 + tt_
                                for k in range(KC):
                                    P.mm(ps[bk][:, 0:512], Wo4[wi][:, k, cc * 128:(cc + 1) * 128], aT[:, k, H(tt_)],
                                         k == 0, k == KC - 1, [("WB", wi, k // 4), ("aT", k)], [PS(bk)])
                        if pending:
                            p7c(pending[0], range(4 * cg, 4 * cg + 4))
                        evac_group(cg, th, yxT, "yxT")
                    pending.clear()
                    rstd_from_sqsum()

                    def out5c(c, tt_, si):
                        accum_sq(stg[si][:, H(tt_)], [("stg", si, tt_)], c == 0, tt_)
                        P.dma("act", xmidT[c][:, tok0 + tt_ * 512:tok0 + (tt_ + 1) * 512], stg[si][:, H(tt_)],
                              reads=[("stg", si, tt_)], writes=[("xmidT", c, th, tt_)])

                    norm_pass(th, yxT, "yxT",
                              lambda c, tt_: (xTv[:, c, NCTX + tok0 + tt_ * 512:NCTX + tok0 + (tt_ + 1) * 512], []), 4, out5c)
                    rstd_from_sqsum()
                    for c in range(KC):
                        for tt_ in range(2):
                            j = (2 * c + tt_) % 4
                            P.dma("sp", in1[j][:], xmidT[c][:, tok0 + tt_ * 512:tok0 + (tt_ + 1) * 512], reads=[("xmidT", c, th, tt_)],
                                  writes=[("in1", j)])
                            P.tt("dve", in1[j][:], in1[j][:], rb[:, H(tt_)], ALU.mult, [("in1", j), ("rb", tt_)], [("in1", j)])
                            P.act(hx2[:, c, H(tt_)], in1[j][:], AF.Identity, [("in1", j)] + VEC, [("hx2", c)],
                                  scale=vec[:, 5, c:c + 1], bias=vec[:, 6, c:c + 1])
                    for fp in range(NF // 2):
                        wi = wcnt[0] % 2
                        wcnt[0] += 1
                        for q_ in range(2):
                            P.dma("pool", Wg_[wi][:, 8 * q_:8 * q_ + 8, :], w_fi_v[:, 8 * q_:8 * q_ + 8, fp * 256:(fp + 1) * 256],
                                  writes=[("WB", wi, q_)])
                            P.dma("pool", Wu_[wi][:, 8 * q_:8 * q_ + 8, :],
                                  w_fi_v[:, 8 * q_:8 * q_ + 8, DFF + fp * 256:DFF + (fp + 1) * 256], writes=[("WB", wi, 2 + q_)])
                        for ff in range(2):
                            f = 2 * fp + ff
                            base = 4 * (f % 2)
                            for tt_ in range(2):
                                for (W_, wk, off) in ((Wg_, 0, 0), (Wu_, 1, 2)):
                                    bk = base + off + tt_
                                    for k in range(KC):
                                        P.mm(ps[bk][:, 0:512], W_[wi][:, k, ff * 128:(ff + 1) * 128], hx2[:, k, H(tt_)],
                                             k == 0, k == KC - 1, [("WB", wi, 2 * wk + k // 8), ("hx2", k)], [PS(bk)])
                            si = f % 2
                            for tt_ in range(2):
                                P.act(sg[si][:, H(tt_)], ps[base + tt_][:, 0:512], AF.Silu, [PS(base + tt_)], [("sg", si, tt_)])
                                P.tt("dve", aT[:, f, H(tt_)], sg[si][:, H(tt_)], ps[base + 2 + tt_][:, 0:512],
                                     ALU.mult, [("sg", si, tt_), PS(base + 2 + tt_)], [("aT", f)])
                    for cg in range(4):
                        for fq in range(4):
                            wi = wcnt[0] % 2
                            wcnt[0] += 1
                            for q_, (fa, fb) in enumerate(((0, 4), (4, 8), (8, 11))):
                                P.dma("pool", Wf4[wi][:, fa:fb, :], w_fo_v[:, fq * 11 + fa:fq * 11 + fb, cg * 512:(cg + 1) * 512],
                                      writes=[("WB", wi, q_)])
                            for cc in range(4):
                                for tt_ in range(2):
                                    bk = cc * 2 + tt_
                                    for f_ in range(11):
                                        f = fq * 11 + f_
                                        P.mm(ps[bk][:, 0:512], Wf4[wi][:, f_, cc * 128:(cc + 1) * 128], aT[:, f, H(tt_)],
                                             f == 0, f == NF - 1, [("WB", wi, f_ // 4), ("aT", f)], [PS(bk)])
                        evac_group(cg, th, ffnT, "ffnT")
                    rstd_from_sqsum()
                    if th == 0:
                        pending.append(0)
                    else:
                        p7c(1, range(KC))
            return self.finish(outT)

    def finish(self, outT):
        if not self.fin:
            pass
        self.S.emit(final_waits=self.fin)
        return self.nc


def _consts():
    l = np.arange(128)[:, None]
    j = np.arange(128)[None, :]
    c = np.zeros((128, 5, 128), np.float32)
    c[:, 0, :] = (l <= j)
    c[:, 1, :] = (l >= j)
    c[:, 2, :] = (l == j)
    c[:, 3, :] = np.where(l > j, NEG, 0.0)
    c[:, 4, :] = np.where(l < j, NEG, 0.0)
    return c


def _pvec(v):
    return np.ascontiguousarray(v.reshape(-1, 128).T)


def prep_inputs(inp):
    f = lambda a: np.ascontiguousarray(np.asarray(a, dtype=np.float32))
    x, c, ctx, c_ctx = f(inp["x"]), f(inp["c"]), f(inp["ctx"]), f(inp["c_ctx"])
    w_in = f(inp["w_in"][0])
    shared = {
        "consts": _consts(),
        "w_mod": f(inp["w_mod"][0]),
        "bmodT": _pvec(f(inp["b_mod"][0])),
        "gains": np.ascontiguousarray(np.stack([_pvec(f(inp[k][0])) for k in
                                                ("g_pre_mix", "g_post_mix", "g_pre_ffn", "g_post_ffn")], axis=1)),
        "w_in": w_in,
        "mhg": _pvec(f(inp["mh_norm_g"][0])),
        "cb": _pvec(f(inp["conv_b"][0])),
        "w_out": f(inp["w_out"][0]),
        "w_ffn_in": f(inp["w_ffn_in"][0]),
        "w_ffn_out": f(inp["w_ffn_out"][0]),
    }
    wgate = w_in[:, 4096:4112]
    bgate = f(inp["b_gates"][0]).reshape(16)
    conv_w = f(inp["conv_w"][0])
    lwa, lwx = f(inp["lru_w_a"][0]), f(inp["lru_w_x"][0])
    lba, lbx, lam = f(inp["lru_b_a"][0]), f(inp["lru_b_x"][0]), f(inp["lru_lambda"][0])
    maps = []
    for core in range(8):
        b, half = core // 2, core % 2
        xl = x[b].reshape(64, 64, D).transpose(1, 0, 2).reshape(NLAT, D)
        xc = ctx[b]
        perm = np.arange(16)
        dsel = [0, 1]
        w5 = np.zeros((5, 1024), np.float32)
        if half == 1:
            xl = xl[::-1]
            xc = xc[::-1]
            perm = np.concatenate([np.arange(8, 16), np.arange(0, 8)])
            dsel = [1, 0]
            w5[1], w5[2], w5[3], w5[4] = conv_w[3], conv_w[2], conv_w[1], conv_w[0]
        else:
            w5[0], w5[1], w5[2], w5[3] = conv_w[0], conv_w[1], conv_w[2], conv_w[3]
        xs = np.concatenate([xc, xl], axis=0)
        m = dict(shared)
        m["xT"] = np.ascontiguousarray(xs.T).reshape(KC, 128, NT)
        cc2 = np.stack([c[b], c_ctx], axis=1)
        m["ccT"] = np.ascontiguousarray(cc2.reshape(KC, 128, 2).transpose(1, 0, 2))
        m["wg"] = np.ascontiguousarray(wgate[:, perm])
        m["bg"] = np.ascontiguousarray(np.broadcast_to(bgate[perm][None, :], (128, 16)))
        m["cw5"] = np.ascontiguousarray(w5.reshape(5, 8, 128).transpose(2, 1, 0))
        m["lwa"] = np.ascontiguousarray(lwa[dsel].transpose(2, 0, 1, 3))
        m["lwx"] = np.ascontiguousarray(lwx[dsel].transpose(2, 0, 1, 3))
        m["lba"] = np.ascontiguousarray(lba[dsel].reshape(2, 8, 128).transpose(2, 0, 1))
        m["lbx"] = np.ascontiguousarray(lbx[dsel].reshape(2, 8, 128).transpose(2, 0, 1))
        m["llam"] = np.ascontiguousarray(lam[dsel].reshape(2, 8, 128).transpose(2, 0, 1))
        maps.append(m)
    return maps


def assemble(results):
    out = np.zeros((4, NLAT, D), np.float32)
    for core in range(8):
        b, half = core // 2, core % 2
        o = results[core]["outT"].reshape(D, NOWN).T
        i = np.arange(NOWN)
        cm = i if half == 0 else (NLAT - 1 - i)
        w, r = cm // 64, cm % 64
        out[b, r * 64 + w] = o
    return out


_CACHE = {}


def kernel(**inputs):
    if "nc" not in _CACHE:
        _CACHE["nc"] = Prog().build()
    nc = _CACHE["nc"]
    maps = prep_inputs(inputs)
    res = run_bass_kernel_spmd(nc, maps, core_ids=list(range(8)))
    return assemble(res.results)
```

```python
import numpy as np
from contextlib import ExitStack
import concourse.bass as bass
import concourse.mybir as mybir
from concourse.bass_utils import run_bass_kernel_spmd

F32 = mybir.dt.float32
BF16 = mybir.dt.bfloat16
AF = mybir.ActivationFunctionType
ALU = mybir.AluOpType
AX = mybir.AxisListType

D = 2048
KC = 16
NCTX = 256
NLAT = 4096
NT = NCTX + NLAT
NOWN = 2048
NCH = NT // 128
DFF = 5632
NF = DFF // 128
EPS = 1e-6
NEG = -30000.0


class _Op:
    __slots__ = ("eng", "fn", "dma", "deps", "signal", "ticket", "sem", "prev_ticket")


class Sched:
    COMPUTE = ("pe", "act", "dve", "pool")
    ALL = ("pe", "act", "dve", "pool", "sp")
    NDMA = {"sp": 24, "act": 16, "pool": 16}

    def __init__(self, nc):
        self.nc = nc
        self.ops = []
        self.last_w = {}
        self.readers = {}
        self.bar_deps = []
        self.bar_applied = set(self.ALL)
        self.last_op = {}
        self.dma_since_bar = []

    def add(self, eng, fn, reads=(), writes=(), dma=False):
        op = _Op()
        op.eng, op.fn, op.dma = eng, fn, dma
        op.deps, op.signal = [], False
        seen = set()

        def dep(o, force=False):
            if o is None or id(o) in seen:
                return
            if not force and o.eng == "pe" and eng == "pe" and not o.dma and not dma:
                return
            seen.add(id(o))
            op.deps.append(o)

        if eng not in self.bar_applied:
            self.bar_applied.add(eng)
            for o in self.bar_deps:
                if o.eng == eng and not o.dma:
                    continue
                dep(o, True)
        xr = [r for r in reads if isinstance(r, tuple) and r[0] == "ps"]
        if xr:
            reads = [r for r in reads if r not in xr]
            writes = list(writes) + xr
        for r in reads:
            dep(self.last_w.get(r))
        for k in writes:
            dep(self.last_w.get(k))
            last = {}
            for rd in self.readers.get(k, ()):
                if rd.dma:
                    dep(rd)
                else:
                    last[rd.eng] = rd
            for rd in last.values():
                dep(rd)
        for k in writes:
            self.last_w[k] = op
            self.readers[k] = []
        for r in reads:
            self.readers.setdefault(r, []).append(op)
        self.ops.append(op)
        if dma:
            self.dma_since_bar.append(op)
        else:
            self.last_op[eng] = op
        return op

    def barrier(self):
        self.bar_deps = list(self.last_op.values()) + list(self.dma_since_bar)
        self.dma_since_bar = []
        self.bar_applied = set()

    def emit(self, final_waits=()):
        nc = self.nc
        for op in self.ops:
            for d in op.deps:
                d.signal = True
        for op in final_waits:
            op.signal = True
        with ExitStack() as es:
            csem = {e: es.enter_context(nc.semaphore("s_" + e)) for e in self.COMPUTE}
            dsem = {q: [es.enter_context(nc.semaphore("d_%s%d" % (q, i))) for i in range(n)]
                    for q, n in self.NDMA.items()}
            ccount = {e: 0 for e in self.COMPUTE}
            dcount = {q: [0] * n for q, n in self.NDMA.items()}
            drr = {q: 0 for q in self.NDMA}
            for op in self.ops:
                if op.dma:
                    q = op.eng
                    i = drr[q]
                    drr[q] = (i + 1) % self.NDMA[q]
                    op.prev_ticket = dcount[q][i]
                    dcount[q][i] += 16
                    op.ticket = dcount[q][i]
                    op.sem = dsem[q][i]
                elif op.signal:
                    ccount[op.eng] += 1
                    op.ticket = ccount[op.eng]
                    op.sem = csem[op.eng]
            per_eng = {e: [] for e in self.ALL}
            for op in self.ops:
                per_eng[op.eng].append(op)
            block = es.enter_context(nc.Block())

            def run(engname, eng):
                seen = {}

                def wait(sem, val):
                    k = id(sem)
                    if seen.get(k, 0) >= val:
                        return
                    seen[k] = val
                    eng.wait_ge(sem, val)

                for op in per_eng[engname]:
                    for d in op.deps:
                        wait(d.sem, d.ticket)
                    if op.dma and op.prev_ticket > 0:
                        wait(op.sem, op.prev_ticket)
                    ins = op.fn(eng)
                    if op.dma:
                        ins.then_inc(op.sem, 16)
                    elif op.signal:
                        ins.then_inc(op.sem, 1)
                if engname == "sp":
                    for op in final_waits:
                        wait(op.sem, op.ticket)

            @block.tensor
            def _(e):
                run("pe", e)

            @block.scalar
            def _(e):
                run("act", e)

            @block.vector
            def _(e):
                run("dve", e)

            @block.gpsimd
            def _(e):
                run("pool", e)

            @block.sync
            def _(e):
                run("sp", e)


class Prog:
    def __init__(self, debug=False, upto=99, sub=9, nheads=4, lite=False, skip=()):
        self.lite = lite
        self.skip = skip
        self.debug = debug
        self.upto = upto
        self.sub = sub
        self.nheads = nheads
        self.nc = bass.Bass("TRN2", target_bir_lowering=False)
        self.S = Sched(self.nc)
        self.fin = []
        self.dbg_names = []

    def din(self, name, shape, dt=F32):
        return self.nc.dram_tensor(name, list(shape), dt, kind="ExternalInput").ap()

    def dscratch(self, name, shape, dt):
        if self.debug:
            self.dbg_names.append(name)
            return self.nc.dram_tensor(name, list(shape), dt, kind="ExternalOutput").ap()
        return self.nc.dram_tensor(name, list(shape), dt, kind="Internal").ap()

    def dump(self, name, ap, reads, dt=F32):
        if not self.debug:
            return
        self.dbg_names.append(name)
        d = self.nc.dram_tensor(name, list(ap.shape), dt, kind="ExternalOutput").ap()
        self.fin.append(self.S.add("sp", lambda e: e.dma_start(out=d, in_=ap), reads=reads, dma=True))

    def dma(self, q, out, in_, reads=(), writes=()):
        return self.S.add(q, lambda e: e.dma_start(out=out, in_=in_), reads=reads, writes=writes, dma=True)

    def mm(self, ps, lhsT, rhs, start, stop, reads, writes):
        return self.S.add("pe", lambda e: e.matmul(ps, lhsT=lhsT, rhs=rhs, start=start, stop=stop),
                          reads=reads, writes=writes)

    def act(self, out, in_, func, reads, writes, **kw):
        return self.S.add("act", lambda e: e.activation(out=out, in_=in_, func=func, **kw), reads=reads, writes=writes)

    def ts(self, eng, out, in0, s1, s2, op0, op1, reads, writes):
        if op1 is None:
            return self.S.add(eng, lambda e: e.tensor_scalar(out=out, in0=in0, scalar1=s1, scalar2=None, op0=op0),
                              reads=reads, writes=writes)
        return self.S.add(eng, lambda e: e.tensor_scalar(out=out, in0=in0, scalar1=s1, scalar2=s2, op0=op0, op1=op1),
                          reads=reads, writes=writes)

    def tt(self, eng, out, in0, in1, op, reads, writes):
        return self.S.add(eng, lambda e: e.tensor_tensor(out=out, in0=in0, in1=in1, op=op), reads=reads, writes=writes)

    def stt(self, out, in0, scalar, in1, op0, op1, reads, writes):
        return self.S.add("dve", lambda e: e.scalar_tensor_tensor(out=out, in0=in0, scalar=scalar, in1=in1, op0=op0, op1=op1),
                          reads=reads, writes=writes)

    def copy(self, eng, out, in_, reads, writes):
        return self.S.add(eng, lambda e: e.tensor_copy(out=out, in_=in_), reads=reads, writes=writes)

    def memset(self, eng, ap, val, writes):
        return self.S.add(eng, lambda e: e.memset(ap, val), writes=writes)

    def recip(self, out, in_, reads, writes):
        return self.S.add("dve", lambda e: e.reciprocal(out=out, in_=in_), reads=reads, writes=writes)

    def build(self):
        nc, S = self.nc, self.S
        P = self
        xT = P.din("xT", [KC, 128, NT])
        ccT = P.din("ccT", [128, KC, 2])
        consts = P.din("consts", [128, 5, 128])
        w_mod = None if self.lite else P.din("w_mod", [D, 6 * D])
        bmodT = P.din("bmodT", [128, 96])
        gains = P.din("gains", [128, 4, KC])
        w_in = None if (self.lite and self.upto < 3) else P.din("w_in", [D, 6160])
        wg = P.din("wg", [D, 16])
        bg = P.din("bg", [128, 16])
        mhg = P.din("mhg", [128, 8])
        cw5 = P.din("cw5", [128, 8, 5])
        cb = P.din("cb", [128, 8])
        lwa = P.din("lwa", [128, 2, 8, 128])
        lwx = P.din("lwx", [128, 2, 8, 128])
        lba = P.din("lba", [128, 2, 8])
        lbx = P.din("lbx", [128, 2, 8])
        llam = P.din("llam", [128, 2, 8])
        w_out = None if self.lite else P.din("w_out", [D, D])
        w_ffn_in = None if self.lite else P.din("w_ffn_in", [D, 2 * DFF])
        w_ffn_out = None if self.lite else P.din("w_ffn_out", [DFF, D])
        outT = nc.dram_tensor("outT", [KC, 128, NOWN], F32, kind="ExternalOutput").ap()
        hxS = P.dscratch("hxS", [KC, 128, NT], BF16)
        yT = P.dscratch("yT", [KC, 128, NOWN], BF16)
        yxT = P.dscratch("yxT", [KC, 128, NOWN], F32)
        xmidT = P.dscratch("xmidT", [KC, 128, NOWN], F32)
        ffnT = P.dscratch("ffnT", [KC, 128, NOWN], F32)
        XRs = self.nc.dram_tensor("XRs", [8, 128, NT], F32, kind="Internal").ap()
        GGs = self.nc.dram_tensor("GGs", [8, 128, NOWN], F32, kind="Internal").ap()

        def pk(ap):
            return ap.rearrange("k p t -> p k t")

        with ExitStack() as top:
            sbt = lambda name, shape, dt: top.enter_context(nc.sbuf_tensor(name, list(shape), dt))
            ps = [top.enter_context(nc.psum_tensor("ps%d" % i, [128, 512], F32)) for i in range(8)]
            PS = lambda i: ("ps", i)
            cst = sbt("cst", [128, 5, 128], F32)
            cstb = sbt("cstb", [128, 5, 128], BF16)
            ones_f = sbt("ones_f", [128, 128], F32)
            ones_b = sbt("ones_b", [128, 128], BF16)
            P.dma("sp", cst[:], consts, writes=["cst"])
            P.copy("dve", cstb[:], cst[:], ["cst"], ["cstb"])
            P.memset("dve", ones_f[:], 1.0, ["ones_f"])
            P.memset("dve", ones_b[:], 1.0, ["ones_b"])
            U_f, L_f = cst[:, 0, :], cst[:, 1, :]
            ident_b = cstb[:, 2, :]
            mneg_b = [cstb[:, 3, :], cstb[:, 4, :]]
            mod = sbt("mod", [128, 96, 2], F32)
            vec = sbt("vec", [128, 8, KC], F32)
            gn = sbt("gn", [128, 4, KC], F32)
            P.dma("sp", gn[:], gains, writes=["gn"])

            with ExitStack() as st:
                sb = lambda name, shape, dt: st.enter_context(nc.sbuf_tensor(name, list(shape), dt))
                cc = sb("cc", [128, KC, 2], F32)
                ccs = sb("ccs", [128, KC, 2], BF16)
                bm = sb("bm", [128, 96], F32)
                P.dma("sp", cc[:], ccT, writes=["cc"])
                P.dma("sp", bm[:], bmodT, writes=["bm"])
                P.act(ccs[:], cc[:], AF.Silu, ["cc"], ["ccs"])
                wmb = [sb("wmb%d" % i, [128, KC, 512], BF16) for i in range(3)]
                wv = None if self.lite else w_mod.rearrange("(k p) n -> p k n", p=128)
                if self.lite:
                    P.memset("dve", ps[0][:, 0:192], 0.0, [PS(0)])
                for g in range(0 if self.lite else 24):
                    bi = g % 3
                    for q_ in range(2):
                        P.dma("pool", wmb[bi][:, 8 * q_:8 * q_ + 8, :], wv[:, 8 * q_:8 * q_ + 8, g * 512:(g + 1) * 512],
                              writes=[("wmb", bi, q_)])
                    for jj in range(4):
                        j = g * 4 + jj
                        for k in range(KC):
                            P.mm(ps[0][:, 2 * j:2 * j + 2], wmb[bi][:, k, jj * 128:(jj + 1) * 128], ccs[:, k, :],
                                 k == 0, k == KC - 1, [("wmb", bi, k // 8), "ccs"], [PS(0)])
                psv = ps[0][:, 0:192].rearrange("p (j t) -> p j t", t=2)
                for t in range(2):
                    P.tt("dve", mod[:, :, t], psv[:, :, t], bm[:], ALU.add, [PS(0), "bm"], [("mod", t)])
                m = lambda i, t: mod[:, i * KC:(i + 1) * KC, t]
                for t in range(2):
                    P.stt(vec[:, 0 + 2 * t, :], m(1, t), 1.0, gn[:, 0, :], ALU.add, ALU.mult, [("mod", t), "gn"], [("vec", 0 + 2 * t)])
                    P.copy("dve", vec[:, 1 + 2 * t, :], m(0, t), [("mod", t)], [("vec", 1 + 2 * t)])
                P.tt("dve", vec[:, 4, :], m(2, 0), gn[:, 1, :], ALU.mult, [("mod", 0), "gn"], [("vec", 4)])
                P.stt(vec[:, 5, :], m(4, 0), 1.0, gn[:, 2, :], ALU.add, ALU.mult, [("mod", 0), "gn"], [("vec", 5)])
                P.copy("dve", vec[:, 6, :], m(3, 0), [("mod", 0)], [("vec", 6)])
                P.tt("dve", vec[:, 7, :], m(5, 0), gn[:, 3, :], ALU.mult, [("mod", 0), "gn"], [("vec", 7)])
                P.dump("d_mod", mod[:], [("mod", 0), ("mod", 1)])
                P.dump("d_vec", vec[:], [("vec", i) for i in range(8)])
                S.barrier()
            VEC = [("vec", i) for i in range(8)]
            if self.upto < 1:
                return self.finish(outT)

            def rms_tiles(st, src_of, ntok_tiles, gs_idx_of, dst_of, tagp):
                sb = lambda name, shape, dt: st.enter_context(nc.sbuf_tensor(name, list(shape), dt))
                xt = [sb(tagp + "xt%d" % i, [128, KC, 512], F32) for i in range(2)]
                sq = sb(tagp + "sq", [128, KC, 512], BF16)
                sd = sb(tagp + "sd", [128, 512], F32)
                rs = sb(tagp + "rs", [128, 512], F32)
                hx = [sb(tagp + "hx%d" % i, [128, KC, 512], BF16) for i in range(2)]
                for i in range(ntok_tiles):
                    src, n, rk = src_of(i)
                    bi = i % 2
                    for q_ in range(4):
                        P.dma("sp", xt[bi][:, 4 * q_:4 * q_ + 4, 0:n], src[:, 4 * q_:4 * q_ + 4, :], reads=rk, writes=[(tagp + "xt", bi, q_)])
                    XTK = [(tagp + "xt", bi, q_) for q_ in range(4)]
                    P.act(sq[:, :, 0:n], xt[bi][:, :, 0:n], AF.Square, XTK, [tagp + "sq"])
                    for k in range(KC):
                        P.mm(ps[1][:, 0:n], ones_b[:], sq[:, k, 0:n], k == 0, k == KC - 1, [tagp + "sq", "ones_b"], [PS(1)])
                    P.act(sd[:, 0:n], ps[1][:, 0:n], AF.Sqrt, [PS(1)], [tagp + "sd"], scale=1.0 / D, bias=EPS)
                    P.recip(rs[:, 0:n], sd[:, 0:n], [tagp + "sd"], [tagp + "rs"])
                    P.tt("dve", xt[bi][:, :, 0:n], xt[bi][:, :, 0:n], rs[:, 0:n].unsqueeze(1).broadcast_to([128, KC, n]),
                         ALU.mult, XTK + [tagp + "rs"], XTK)
                    gi = gs_idx_of(i)
                    for k in range(KC):
                        P.act(hx[bi][:, k, 0:n], xt[bi][:, k, 0:n], AF.Identity, [(tagp + "xt", bi, k // 4)] + VEC, [(tagp + "hx", bi, k)],
                              scale=vec[:, gi, k:k + 1], bias=vec[:, gi + 1, k:k + 1])
                    dst, wk = dst_of(i)
                    P.dma("act", dst, hx[bi][:, :, 0:n], reads=[(tagp + "hx", bi, k) for k in range(KC)], writes=wk)

            with ExitStack() as st:
                xTv = pk(xT)
                hxv = pk(hxS)

                def src1(i):
                    if i == 0:
                        return xTv[:, :, 0:256], 256, []
                    return xTv[:, :, 256 + (i - 1) * 512: 256 + i * 512], 512, []

                def dst1(i):
                    if i == 0:
                        return hxv[:, :, 0:256], [("hxS", 0)]
                    a = 256 + (i - 1) * 512
                    return hxv[:, :, a:a + 512], [("hxS", 2 * i - 1), ("hxS", 2 * i)]

                rms_tiles(st, src1, 9, lambda i: 2 if i == 0 else 0, dst1, "s1")
                S.barrier()
            if self.upto < 2:
                return self.finish(outT)
            hxv = pk(hxS)
            HXK = lambda a, n: [("hxS", c) for c in range(a // 128, (a + n) // 128)]
            w_in_v = None if w_in is None else w_in.rearrange("(k p) n -> p k n", p=128)

            with ExitStack() as st:
                sb = lambda name, shape, dt: st.enter_context(nc.sbuf_tensor(name, list(shape), dt))
                l1 = sb("l1", [128, NCH, 8], F32)
                bias_ = sb("bias_", [128, NCH, 8], F32)
                winter = sb("winter", [128, NCH, 8], F32)
                wloc = sb("wloc", [128, NCH, 8], F32)
                decay = sb("decay", [128, NCH, 8], F32)
                GK = ["l1", "bias_", "winter", "wloc", "decay"]
                hxt = [sb("hxt%d" % i, [128, KC, 256], BF16) for i in range(2)]
                with ExitStack() as st2:
                    sb2 = lambda name, shape, dt: st2.enter_context(nc.sbuf_tensor(name, list(shape), dt))
                    wgb = sb2("wgb", [128, KC, 16], BF16)
                    bgs = sb2("bgs", [128, 16], F32)
                    gts = sb2("gts", [128, NCH, 16], F32)
                    e1 = sb2("e1", [128, NCH, 8], F32)
                    tmpw = sb2("tmpw", [128, NCH, 8], F32)
                    P.dma("pool", wgb[:], wg.rearrange("(k p) n -> p k n", p=128), writes=["wgb"])
                    P.dma("sp", bgs[:], bg, writes=["bgs"])
                    Wxr = sb2("Wxr", [128, KC, 1024], BF16)
                    Wgr = sb2("Wgr", [128, KC, 1024], BF16)
                    xrs = [sb2("xrs%d" % i, [128, 8, 256], F32) for i in range(2)]
                    ggs = [sb2("ggs%d" % i, [128, 8, 256], F32) for i in range(2)]
                    for hh_ in range(2):
                        P.dma("pool", Wxr[:, :, 512 * hh_:512 * (hh_ + 1)], w_in_v[:, :, 4112 + 512 * hh_:4112 + 512 * (hh_ + 1)],
                              writes=[("Wxr", hh_)])
                        P.dma("pool", Wgr[:, :, 512 * hh_:512 * (hh_ + 1)], w_in_v[:, :, 5136 + 512 * hh_:5136 + 512 * (hh_ + 1)],
                              writes=[("Wgr", hh_)])
                    rot2 = [0]
                    BK2 = [0, 1, 4, 5, 6, 7]

                    def nb2():
                        rot2[0] = (rot2[0] + 1) % 6
                        return BK2[rot2[0]]
                    if "cut1" in self.skip:
                        return self.finish(outT)
                    for i in range(NT // 256):
                        bi = i % 2
                        for q_ in range(2):
                            P.dma("sp", hxt[bi][:, 8 * q_:8 * q_ + 8, :], hxv[:, 8 * q_:8 * q_ + 8, 256 * i:256 * i + 256],
                                  reads=HXK(256 * i, 256), writes=[("hxt", bi, q_)])
                        for cc in range(2):
                            c = 2 * i + cc
                            bank, col = (2 if c < 17 else 3), (c % 17) * 16
                            for k in range(KC):
                                P.mm(ps[bank][:, col:col + 16], hxt[bi][:, k, cc * 128:(cc + 1) * 128], wgb[:, k, :],
                                     k == 0, k == KC - 1, [("hxt", bi, k // 8), "wgb"], [PS(bank)])
                        for n in range(8):
                            bk = nb2()
                            for k in range(KC):
                                P.mm(ps[bk][:, 0:256], Wxr[:, k, 128 * n:128 * (n + 1)], hxt[bi][:, k, :], k == 0, k == KC - 1,
                                     [("hxt", bi, k // 8), ("Wxr", n // 4)], [PS(bk)])
                            if n % 2 == 0:
                                P.copy("dve", xrs[bi][:, n, :], ps[bk][:, 0:256], [PS(bk)], [("xrs", bi, n)])
                            else:
                                P.act(xrs[bi][:, n, :], ps[bk][:, 0:256], AF.Copy, [PS(bk)], [("xrs", bi, n)])
                        P.dma("act", XRs.rearrange("n p t -> p n t")[:, :, 256 * i:256 * i + 256], xrs[bi][:],
                              reads=[("xrs", bi, n) for n in range(8)], writes=[("XRs", i)])
                        if 1 <= i <= 8:
                            for n in range(8):
                                bk = nb2()
                                for k in range(KC):
                                    P.mm(ps[bk][:, 0:256], Wgr[:, k, 128 * n:128 * (n + 1)], hxt[bi][:, k, :], k == 0, k == KC - 1,
                                         [("hxt", bi, k // 8), ("Wgr", n // 4)], [PS(bk)])
                                P.act(ggs[bi][:, n, :], ps[bk][:, 0:256], AF.Gelu_apprx_tanh, [PS(bk)], [("ggs", bi, n)])
                            P.dma("act", GGs.rearrange("n p t -> p n t")[:, :, 256 * (i - 1):256 * i], ggs[bi][:],
                                  reads=[("ggs", bi, n) for n in range(8)], writes=[("GGs", i - 1)])
                    if "cut2" in self.skip:
                        return self.finish(outT)
                    g4 = lambda ap: ap.rearrange("p c (t h) -> p c t h", h=4)
                    for half_, (ca, cb_, bank) in enumerate(((0, 17, 2), (17, 34, 3))):
                        P.tt("dve", gts[:, ca:cb_, :], ps[bank][:, 0:272].rearrange("p (c g) -> p c g", g=16),
                             bgs[:].unsqueeze(1).broadcast_to([128, 17, 16]), ALU.add, [PS(bank), "bgs"], [("gts", half_)])
                    GT = [("gts", 0), ("gts", 1)]
                    if "cut3" in self.skip:
                        return self.finish(outT)
                    P.act(g4(e1[:]), g4(gts[:])[:, :, 1::2, :], AF.Exp, GT, ["e1"], scale=-1.0)
                    P.act(l1[:], e1[:], AF.Ln, ["e1"], ["l1"], bias=1.0)
                    if "cut4" in self.skip:
                        return self.finish(outT)
                    for c in range(NCH):
                        bank, col = (4 if c < 17 else 5), (c % 17) * 16
                        P.mm(ps[bank][:, col:col + 4], U_f, l1[:, c, 0:4], True, True, ["l1", "cst"], [PS(bank)])
                        P.mm(ps[bank][:, col + 4:col + 8], L_f, l1[:, c, 4:8], True, True, ["l1", "cst"], [PS(bank)])
                        P.mm(ps[bank][:, col + 8:col + 16], ones_f[:], l1[:, c, 0:8], True, True, ["l1", "ones_f"], [PS(bank)])
                    if "cut5" in self.skip:
                        return self.finish(outT)
                    for half_, (ca, cb_, bank) in enumerate(((0, 17, 4), (17, 34, 5))):
                        gp = ps[bank][:, 0:272].rearrange("p (c g) -> p c g", g=16)
                        P.tt("dve", g4(bias_[:, ca:cb_, :]), g4(gts[:, ca:cb_, :])[:, :, 0::2, :], g4(gp[:, :, 0:8]), ALU.add,
                             GT + [PS(bank)], [("bias_", half_)])
                        P.act(winter[:, ca:cb_, :], gp[:, :, 0:8], AF.Exp, [PS(bank)], [("winter", half_)], scale=-1.0)
                        P.act(decay[:, ca:cb_, :], gp[:, :, 8:16], AF.Exp, [PS(bank)], [("decay", half_)], scale=-1.0)
                        P.tt("dve", tmpw[:, ca:cb_, :], bias_[:, ca:cb_, :], gp[:, :, 8:16], ALU.subtract,
                             [("bias_", half_), PS(bank)], [("tmpw", half_)])
                        P.act(wloc[:, ca:cb_, :], tmpw[:, ca:cb_, :], AF.Exp, [("tmpw", half_)], [("wloc", half_)])
                    if "cut6" in self.skip:
                        return self.finish(outT)
                    P.dump("d_gts", gts[:], GT)
                    P.dump("d_l1", l1[:], ["l1"])
                    P.dump("d_bias", bias_[:], [("bias_", 0), ("bias_", 1)])
                    P.dump("d_winter", winter[:], [("winter", 0), ("winter", 1)])
                    P.dump("d_wloc", wloc[:], [("wloc", 0), ("wloc", 1)])
                    P.dump("d_decay", decay[:], [("decay", 0), ("decay", 1)])
                    if "cut7" in self.skip:
                        return self.finish(outT)
                    S.barrier()
                if self.upto < 3:
                    return self.finish(outT)
                GK = ["l1"] + [(n_, h_) for n_ in ("bias_", "winter", "wloc", "decay") for h_ in (0, 1)]

                Wkv = sb("Wkv", [128, KC, 512], BF16)
                Wq = sb("Wq", [128, KC, 256], BF16)
                Wo = sb("Wo", [128, KC, 256], BF16)
                QT = sb("QT", [128, 2, NOWN], BF16)
                KT = sb("KT", [128, 2, NOWN], BF16)
                OT = sb("OT", [128, 2, NOWN], BF16)
                Ktok = sb("Ktok", [128, NCH, 256], BF16)
                V1 = sb("V1", [128, NCH, 258], BF16)
                hs = sb("hs", [128, 16, 256], F32)
                CT = [sb("CT%d" % d_, [128, 2, 258], F32) for d_ in range(2)]
                CTb = [sb("CTb%d" % d_, [128, 2, 258], BF16) for d_ in range(2)]
                lfU = [[sb("lfU%d%d" % (d_, p_), [128, 128], F32) for p_ in range(2)] for d_ in range(2)]
                DT = [[sb("DT%d%d" % (d_, p_), [128, 128], F32) for p_ in range(2)] for d_ in range(2)]
                PT = [[sb("PT%d%d" % (d_, p_), [128, 128], BF16) for p_ in range(2)] for d_ in range(2)]
                isb = [[sb("isb%d%d" % (d_, p_), [128, 258], F32) for p_ in range(2)] for d_ in range(2)]
                nd = [sb("nd%d" % d_, [128, 258], F32) for d_ in range(2)]
                den = [sb("den%d" % d_, [128, 1], F32) for d_ in range(2)]
                rden = [sb("rden%d" % d_, [128, 1], F32) for d_ in range(2)]
                Vw = [[sb("Vw%d%d" % (d_, p_), [128, 258], BF16) for p_ in range(2)] for d_ in range(2)]
                junk = sb("junk", [128, 256], F32)
                ssq = sb("ssq", [128, 16], F32)
                ssd = sb("ssd", [128, 16], F32)
                srs = sb("srs", [128, 16], F32)
                hn = sb("hn", [128, 16, 256], BF16)
                ymT = sb("ymT", [128, 2, NOWN], BF16)
                mh = sb("mh", [128, 8], F32)
                P.dma("sp", mh[:], mhg, writes=["mh"])
                P.memset("pool", V1[:, :, 256:257], 1.0, ["V1one"])
                P.memset("pool", V1[:, :, 257:258], 0.0, ["V1pad"])
                masks = [U_f, L_f]

                def front(h, d_, c, produce, update, par):
                    idx = d_ * 4 + h
                    A, B = 4 * d_, 4 * d_ + 1
                    if produce:
                        oc = c - 2
                        tok = oc * 128
                        lf_, dt_, pt_, is_ = lfU[d_][par], DT[d_][par], PT[d_][par], isb[d_][par]
                        P.ts("dve", lf_[:], masks[d_], l1[:, c, idx:idx + 1], -1.0, ALU.mult, ALU.mult,
                             ["cst", "l1"], [("lfU", d_, par)])
                        P.mm(ps[A][:, 0:128], ones_f[:], lf_[:], True, False, ["ones_f", ("lfU", d_, par)], [PS(A)])
                        P.mm(ps[A][:, 0:128], ident_b, mneg_b[d_], False, True, ["cstb"], [PS(A)])
                        P.act(dt_[:], ps[A][:, 0:128], AF.Exp, [PS(A)] + GK, [("DT", d_, par)], bias=bias_[:, c, idx:idx + 1], scale=1.0)
                        for ec in range(2):
                            P.mm(ps[A][:, 128:256], KT[:, ec, tok:tok + 128], QT[:, ec, tok:tok + 128], ec == 0, ec == 1,
                                 [("KT", oc // 2), ("QT", oc // 2)], [PS(A)])
                        P.tt("dve", pt_[:], ps[A][:, 128:256], dt_[:], ALU.mult, [PS(A), ("DT", d_, par)], [("PT", d_, par)])
                        P.mm(ps[B][:, 0:257], pt_[:], V1[:, c, 0:257], True, True, [("PT", d_, par), ("V1", c), "V1one"], [PS(B)])
                        P.act(is_[:, 0:257], ps[B][:, 0:257], AF.Copy, [PS(B)], [("isb", d_, par)])
                    if update:
                        P.ts("pool", Vw[d_][par][:, 0:257], V1[:, c, 0:257], wloc[:, c, idx:idx + 1], 1.0, ALU.mult, ALU.mult,
                             [("V1", c), "V1one"] + GK, [("Vw", d_, par)])

                def back(h, d_, c, produce, update, par):
                    idx = d_ * 4 + h
                    C_, Dk = 4 * d_ + 2, 4 * d_ + 3
                    if produce:
                        oc = c - 2
                        tok = oc * 128
                        for ec in range(2):
                            P.mm(ps[C_][:, 0:257], QT[:, ec, tok:tok + 128], CTb[d_][:, ec, 0:257], ec == 0, ec == 1,
                                 [("QT", oc // 2), ("CTb", d_, ec)], [PS(C_)])
                        P.stt(nd[d_][:, 0:257], ps[C_][:, 0:257], winter[:, c, idx:idx + 1], isb[d_][par][:, 0:257], ALU.mult, ALU.add,
                              [PS(C_), ("isb", d_, par)] + GK, [("nd", d_)])
                        P.ts("dve", den[d_][:], nd[d_][:, 256:257], -1.0, 1.0, ALU.mult, ALU.max, [("nd", d_)], [("den", d_)])
                        P.tt("dve", den[d_][:], den[d_][:], nd[d_][:, 256:257], ALU.max, [("nd", d_), ("den", d_)], [("den", d_)])
                        P.recip(rden[d_][:], den[d_][:], [("den", d_)], [("rden", d_)])
                        P.stt(hs[:, oc, :], nd[d_][:, 0:256], rden[d_][:, 0:1], hs[:, oc, :], ALU.mult, ALU.add,
                              [("nd", d_), ("rden", d_), ("hs", oc)], [("hs", oc)])
                    if update:
                        for ec in range(2):
                            bank = Dk if ec == 0 else C_
                            P.mm(ps[bank][:, 0:257], Ktok[:, c, ec * 128:(ec + 1) * 128], Vw[d_][par][:, 0:257], True, True,
                                 [("Ktok", c), ("Vw", d_, par)], [PS(bank)])
                            P.stt(CT[d_][:, ec, 0:257], CT[d_][:, ec, 0:257], decay[:, c, idx:idx + 1], ps[bank][:, 0:257],
                                  ALU.mult, ALU.add, [("CT", d_, ec), PS(bank)] + GK, [("CT", d_, ec)])
                            P.act(CTb[d_][:, ec, 0:257], CT[d_][:, ec, 0:257], AF.Copy, [("CT", d_, ec)], [("CTb", d_, ec)])

                for h in range(self.nheads):
                    P.dma("pool", Wkv[:, :, 0:256], w_in_v[:, :, 1024 + 256 * h:1024 + 256 * (h + 1)], writes=["Wk"])
                    P.dma("pool", Wkv[:, :, 256:512], w_in_v[:, :, 2048 + 256 * h:2048 + 256 * (h + 1)], writes=["Wv"])
                    P.dma("pool", Wq[:], w_in_v[:, :, 256 * h:256 * (h + 1)], writes=["Wq"])
                    P.dma("pool", Wo[:], w_in_v[:, :, 3072 + 256 * h:3072 + 256 * (h + 1)], writes=["Wo"])
                    P.memset("dve", hs[:], 0.0, [("hs", oc) for oc in range(16)])
                    for d_ in range(2):
                        P.memset("dve", CT[d_][:], 0.0, [("CT", d_, 0), ("CT", d_, 1)])
                        P.memset("pool", CTb[d_][:], 0.0, [("CTb", d_, 0), ("CTb", d_, 1)])
                    rot = [0]

                    def nb():
                        rot[0] = (rot[0] + 1) % 6
                        return rot[0]

                    for i in range(NT // 256):
                        bi = i % 2
                        for q_ in range(2):
                            P.dma("sp", hxt[bi][:, 8 * q_:8 * q_ + 8, :], hxv[:, 8 * q_:8 * q_ + 8, 256 * i:256 * i + 256],
                                  reads=HXK(256 * i, 256), writes=[("hxt", bi, q_)])
                        for cc in range(2):
                            c = 2 * i + cc
                            bk = nb()
                            for k in range(KC):
                                P.mm(ps[bk][:, 0:512], hxt[bi][:, k, cc * 128:(cc + 1) * 128], Wkv[:, k, :], k == 0, k == KC - 1,
                                     [("hxt", bi, k // 8), "Wk", "Wv"], [PS(bk)])
                            P.act(Ktok[:, c, :], ps[bk][:, 0:256], AF.Copy, [PS(bk)], [("Ktok", c)], scale=0.0625)
                            P.copy("dve", V1[:, c, 0:256], ps[bk][:, 256:512], [PS(bk)], [("V1", c)])
                        if 1 <= i <= 8:
                            ti = i - 1
                            o = ti * 256
                            for wi, (W_, wk, dst, dk) in enumerate(((Wq, "Wq", QT, "QT"), (Wkv, "Wk", KT, "KT"), (Wo, "Wo", OT, "OT"))):
                                for ec in range(2):
                                    bk = nb()
                                    for k in range(KC):
                                        P.mm(ps[bk][:, 0:256], W_[:, k, ec * 128:(ec + 1) * 128], hxt[bi][:, k, :], k == 0, k == KC - 1,
                                             [("hxt", bi, k // 8), wk], [PS(bk)])
                                    if dk == "QT":
                                        P.copy("dve", dst[:, ec, o:o + 256], ps[bk][:, 0:256], [PS(bk)], [(dk, ti)])
                                    elif dk == "KT":
                                        P.act(dst[:, ec, o:o + 256], ps[bk][:, 0:256], AF.Copy, [PS(bk)], [(dk, ti)], scale=0.0625)
                                    else:
                                        P.act(dst[:, ec, o:o + 256], ps[bk][:, 0:256], AF.Sigmoid, [PS(bk)], [(dk, ti)])
                    if self.debug and h == 0:
                        P.dump("d_QT", QT[:], [("QT", t_) for t_ in range(8)], BF16)
                        P.dump("d_KT", KT[:], [("KT", t_) for t_ in range(8)], BF16)
                        P.dump("d_Ktok", Ktok[:], [("Ktok", c) for c in range(NCH)], BF16)
                        P.dump("d_V1", V1[:], [("V1", c) for c in range(NCH)] + ["V1one", "V1pad"], BF16)
                    if self.sub < 1:
                        continue
                    fwd = [(0, False, True), (1, False, True)] + [(c, True, c < 17) for c in range(2, 18)]
                    bwd = [(1, False, True), (0, False, True)] + [(c, False, True) for c in range(33, 17, -1)] + \
                          [(c, True, c > 2) for c in range(17, 1, -1)]
                    chains = [(0, fwd, 16), (1, bwd, 0)]
                    for t in range(-1, 34):
                        for (d_, steps, t0) in chains:
                            j = t + 1 - t0
                            if 0 <= j < len(steps):
                                front(h, d_, *steps[j], j % 2)
                        for (d_, steps, t0) in chains:
                            j = t - t0
                            if 0 <= j < len(steps):
                                back(h, d_, *steps[j], j % 2)
                    if self.debug and h == 0:
                        P.dump("d_hs", hs[:], [("hs", oc) for oc in range(16)])
                        P.dump("d_CT0", CT[0][:], [("CT", 0, 0), ("CT", 0, 1)])
                        P.dump("d_CT1", CT[1][:], [("CT", 1, 0), ("CT", 1, 1)])
                    if self.sub < 2:
                        continue
                    for oc in range(16):
                        S.add("act", (lambda oc: lambda e: e.activation(out=junk[:], in_=hs[:, oc, :], func=AF.Square,
                                                                      accum_out=ssq[:, oc:oc + 1]))(oc),
                              reads=[("hs", oc)], writes=["junk", ("ssq", oc)])
                    P.act(ssd[:], ssq[:], AF.Sqrt, [("ssq", oc) for oc in range(16)], ["ssd"], scale=1.0 / 256, bias=EPS)
                    P.recip(srs[:], ssd[:], ["ssd"], ["srs"])
                    P.tt("dve", hn[:], hs[:], srs[:].unsqueeze(2).broadcast_to([128, 16, 256]), ALU.mult,
                         [("hs", oc) for oc in range(16)] + ["srs"], ["hn"])
                    tbanks = [(ps[7][:].bitcast(BF16), PS(7)), (ps[6][:].bitcast(BF16), PS(6))]
                    for og in range(4):
                        tb, tk = tbanks[og % 2]
                        for oo in range(4):
                            oc = og * 4 + oo
                            for ec in range(2):
                                slot = oo * 2 + ec
                                S.add("pe", (lambda oc, ec, slot, tb: lambda e: e.transpose(out=tb[:, slot * 128:(slot + 1) * 128],
                                                                                              in_=hn[:, oc, ec * 128:(ec + 1) * 128],
                                                                                              identity=ident_b))(oc, ec, slot, tb),
                                      reads=["hn", "cstb"], writes=[tk])
                        for ec in range(2):
                            tv = tb.rearrange("p (o e t) -> p o e t", o=4, e=2)[:, :, ec, :]
                            P.stt(ymT[:, ec, og * 512:(og + 1) * 512].rearrange("p (o t) -> p o t", o=4), tv,
                                  mh[:, 2 * h + ec:2 * h + ec + 1],
                                  OT[:, ec, og * 512:(og + 1) * 512].rearrange("p (o t) -> p o t", o=4), ALU.mult, ALU.mult,
                                  [tk, "mh"] + [("OT", t_) for t_ in range(8)], [("ymT", ec)])
                    for ec in range(2):
                        P.dma("act", yT[2 * h + ec], ymT[:, ec, :], reads=[("ymT", ec)], writes=[("yT", 2 * h + ec)])
                S.barrier()
            if self.upto < 4:
                return self.finish(outT)

            with ExitStack() as st:
                sb = lambda name, shape, dt: st.enter_context(nc.sbuf_tensor(name, list(shape), dt))
                lwa_b = sb("lwa_b", [128, 2, 8, 128], BF16)
                lwx_b = sb("lwx_b", [128, 2, 8, 128], BF16)
                cw = sb("cw", [128, 8, 5], F32)
                cbv = sb("cbv", [128, 8], F32)
                ba = sb("ba", [128, 2, 8], F32)
                bx = sb("bx", [128, 2, 8], F32)
                lam = sb("lam", [128, 2, 8], F32)
                spl = sb("spl", [128, 2, 8], F32)
                s1 = sb("s1", [128, 2, 8], F32)
                s2 = sb("s2", [128, 2, 8], F32)
                xs = [sb("xs%d" % i, [128, NT], F32) for i in range(1)] * 2
                XR = sb("XR", [128, NT], F32)
                xc = sb("xc", [128, NT], F32)
                xcb = sb("xcb", [128, NT], BF16)
                rr = [sb("rr%d" % i, [128, NT], F32) for i in range(2)]
                ii = [sb("ii%d" % i, [128, NT], F32) for i in range(2)]
                aa = [sb("aa%d" % i, [128, NT], F32) for i in range(2)]
                hsum = sb("hsum", [128, NT], F32)
                GG = [sb("GG0", [128, NOWN], F32)] * 2
                yr = sb("yr", [128, NOWN], BF16)
                P.dma("pool", lwa_b[:], lwa, writes=["lwa_b"])
                P.dma("pool", lwx_b[:], lwx, writes=["lwx_b"])
                P.dma("sp", cw[:], cw5, writes=["cw"])
                P.dma("sp", cbv[:], cb, writes=["cbv"])
                P.dma("sp", ba[:], lba, writes=["ba"])
                P.dma("sp", bx[:], lbx, writes=["bx"])
                P.dma("sp", lam[:], llam, writes=["lam"])
                P.act(spl[:], lam[:], AF.Exp, ["lam"], ["spl"], scale=-1.0)
                P.act(spl[:], spl[:], AF.Ln, ["spl"], ["spl"], bias=1.0)
                P.ts("dve", s1[:], spl[:], -8.0, None, ALU.mult, None, ["spl"], ["s1"])
                P.ts("dve", s2[:], spl[:], -16.0, None, ALU.mult, None, ["spl"], ["s2"])
                SM = ["cw", "cbv", "ba", "bx", "s1", "s2"]
                rot = [0]

                def nb():
                    rot[0] = (rot[0] + 1) % 8
                    return rot[0]

                XRl = XR[:, NCTX:NT].rearrange("p (r w) -> p w r", w=64)
                HSl = hsum[:, NCTX:NT].rearrange("p (r w) -> p w r", w=64)
                TT = [(t * 512, 512) for t in range(8)] + [(4096, 256)]
                XRK = [("XRs", i) for i in range(NT // 256)]
                GGK = [("GGs", i) for i in range(8)]
                nblk = 8 if "lru1" not in self.skip else 1

                def load(n):
                    P.dma("sp", xs[0][:], XRs[n], reads=XRK, writes=[("xs", 0)])

                load(0)
                for n in range(nblk):
                    xsn = xs[n % 2]
                    P.dma("sp", GG[0][:], GGs[n], reads=GGK, writes=[("GG", 0)])
                    P.copy("pool", XR[:, 0:NCTX], xsn[:, 0:NCTX], [("xs", 0)], ["XRc"])
                    for q_ in range(2):
                        src = xsn[:, NCTX:NT].rearrange("p (w r) -> p w r", r=64)[:, 32 * q_:32 * (q_ + 1), :]
                        dst = XRl[:, 32 * q_:32 * (q_ + 1), :]
                        if q_ == 0:
                            P.copy("pool", dst, src, [("xs", 0)], [("XRl", q_)])
                        else:
                            P.act(dst, src, AF.Copy, [("xs", 0)], [("XRl", q_)])
                    XK = ["XRc", ("XRl", 0), ("XRl", 1)]
                    if n + 1 < nblk:
                        load(n + 1)
                    P.ts("dve", xc[:], XR[:], cw[:, n, 2:3], cbv[:, n:n + 1], ALU.mult, ALU.add, XK + SM, ["xc"])
                    for j in (0, 1, 3, 4):
                        o = j - 2
                        for (a_, b_) in ((0, NCTX), (NCTX, NT)):
                            lo, hi = max(a_, a_ - o), min(b_, b_ - o)
                            P.stt(xc[:, lo:hi], XR[:, lo + o:hi + o], cw[:, n, j:j + 1], xc[:, lo:hi], ALU.mult, ALU.add,
                                  XK + ["xc"] + SM, ["xc"])
                    P.act(xcb[:], xc[:], AF.Copy, ["xc"], ["xcb"])
                    if self.debug and n == 0:
                        P.dump("d_xc", xc[:], ["xc"])
                    for d_ in range(2):
                        R_, I_, A_ = ("rr", d_), ("ii", d_), ("aa", d_)
                        for (t0, tn) in TT:
                            b1, b2 = nb(), nb()
                            P.mm(ps[b1][:, 0:tn], lwa_b[:, d_, n, :], xcb[:, t0:t0 + tn], True, True, ["lwa_b", "xcb"], [PS(b1)])
                            P.mm(ps[b2][:, 0:tn], lwx_b[:, d_, n, :], xcb[:, t0:t0 + tn], True, True, ["lwx_b", "xcb"], [PS(b2)])
                            P.act(rr[d_][:, t0:t0 + tn], ps[b1][:, 0:tn], AF.Sigmoid, [PS(b1)] + SM, [R_], bias=ba[:, d_, n:n + 1], scale=1.0)
                            P.act(ii[d_][:, t0:t0 + tn], ps[b2][:, 0:tn], AF.Sigmoid, [PS(b2)] + SM, [I_], bias=bx[:, d_, n:n + 1], scale=1.0)
                    for d_ in range(2):
                        R_, I_, A_ = ("rr", d_), ("ii", d_), ("aa", d_)
                        P.act(aa[d_][:], rr[d_][:], AF.Exp, [R_] + SM, [A_], scale=s1[:, d_, n:n + 1])
                        P.act(rr[d_][:], rr[d_][:], AF.Exp, [R_] + SM, [R_], scale=s2[:, d_, n:n + 1])
                    for d_ in range(2):
                        R_, I_, A_ = ("rr", d_), ("ii", d_), ("aa", d_)
                        P.tt("pool", ii[d_][:], ii[d_][:], xc[:], ALU.mult, [I_, "xc"], [I_])
                    for d_ in range(2):
                        R_, I_, A_ = ("rr", d_), ("ii", d_), ("aa", d_)
                        P.ts("pool", rr[d_][:], rr[d_][:], 1.0, 0.0, ALU.min, ALU.max, [R_], [R_])
                        P.act(rr[d_][:], rr[d_][:], AF.Sqrt, [R_], [R_], scale=-1.0, bias=1.0)
                    for d_ in range(2):
                        R_, I_, A_ = ("rr", d_), ("ii", d_), ("aa", d_)
                        P.tt("dve", rr[d_][:], rr[d_][:], ii[d_][:], ALU.mult, [R_, I_], [R_])
                    S.add("dve", lambda e: e.tensor_tensor_scan(out=hsum[:], data0=aa[0][:], data1=rr[0][:], initial=0.0,
                                                                op0=ALU.mult, op1=ALU.add),
                          reads=[("aa", 0), ("rr", 0)], writes=["hsum"])
                    S.add("dve", lambda e: e.tensor_tensor_scan(out=XR[:, 0:NCTX][:, ::-1], data0=aa[1][:, 0:NCTX][:, ::-1],
                                                                data1=rr[1][:, 0:NCTX][:, ::-1], initial=0.0,
                                                                op0=ALU.mult, op1=ALU.add),
                          reads=[("aa", 1), ("rr", 1)], writes=XK)
                    S.add("dve", lambda e: e.tensor_tensor_scan(out=XR[:, NCTX:NT][:, ::-1], data0=aa[1][:, NCTX:NT][:, ::-1],
                                                                data1=rr[1][:, NCTX:NT][:, ::-1], initial=XR[:, 0:1],
                                                                op0=ALU.mult, op1=ALU.add),
                          reads=[("aa", 1), ("rr", 1)] + XK, writes=XK)
                    P.tt("pool", hsum[:, NCTX:NT], hsum[:, NCTX:NT], XR[:, NCTX:NT], ALU.add, ["hsum"] + XK, ["hsum"])
                    if self.debug and n == 0:
                        P.dump("d_hsum", hsum[:], ["hsum"])
                    P.tt("dve", yr[:].rearrange("p (w r) -> p w r", r=64), GG[n % 2][:].rearrange("p (w r) -> p w r", r=64),
                         HSl[:, 0:32, :], ALU.mult, [("GG", 0), "hsum"], ["yr"])
                    P.dma("act", yT[8 + n], yr[:], reads=["yr"], writes=[("yT", 8 + n)])
                S.barrier()
            if self.upto < 5:
                return self.finish(outT)

            with ExitStack() as st:
                sb = lambda name, shape, dt: st.enter_context(nc.sbuf_tensor(name, list(shape), dt))
                T = 1024
                hx2 = sb("hx2", [128, KC, T], BF16)
                aT = sb("aT", [128, NF, T], BF16)
                stg = [sb("stg%d" % i, [128, T], F32) for i in range(2)]
                sqf = [sb("sqf%d" % i, [128, 512], F32) for i in range(4)]
                sqsum = sb("sqsum", [128, T], F32)
                sdb = sb("sdb", [128, T], F32)
                rb = sb("rb", [128, T], F32)
                in1 = [sb("in1_%d" % i, [128, 512], F32) for i in range(4)]
                in2 = [sb("in2_%d" % i, [128, 512], F32) for i in range(4)]
                sg = [sb("sg%d" % i, [128, T], BF16) for i in range(2)]
                WB = [sb("WB%d" % i, [128, 8192], BF16) for i in range(2)]
                Wo4 = [WB[i][:, :].rearrange("p (k n) -> p k n", n=512) for i in range(2)]
                Wg_ = [WB[i][:, 0:4096].rearrange("p (k n) -> p k n", n=256) for i in range(2)]
                Wu_ = [WB[i][:, 4096:8192].rearrange("p (k n) -> p k n", n=256) for i in range(2)]
                Wf4 = [WB[i][:, 0:11 * 512].rearrange("p (f n) -> p f n", n=512) for i in range(2)]
                WK = lambda wi: [("WB", wi, q_) for q_ in range(4)]
                w_out_v = w_out.rearrange("(k p) n -> p k n", p=128)
                w_fi_v = w_ffn_in.rearrange("(k p) n -> p k n", p=128)
                w_fo_v = w_ffn_out.rearrange("(f p) n -> p f n", p=128)
                xTv = pk(xT)
                cnt = [0]
                wcnt = [0]
                H = lambda tt_: slice(tt_ * 512, (tt_ + 1) * 512)

                def rstd_from_sqsum():
                    for tt_ in range(2):
                        P.mm(ps[tt_][:, 0:512], ones_f[:], sqsum[:, H(tt_)], True, True, ["ones_f", ("sqsum", tt_)], [PS(tt_)])
                        P.act(sdb[:, H(tt_)], ps[tt_][:, 0:512], AF.Sqrt, [PS(tt_)], [("sdb", tt_)], scale=1.0 / D, bias=EPS)
                        P.recip(rb[:, H(tt_)], sdb[:, H(tt_)], [("sdb", tt_)], [("rb", tt_)])

                def accum_sq(src, src_keys, first, tt_):
                    i_ = cnt[0] % 4
                    cnt[0] += 1
                    if first:
                        P.act(sqsum[:, H(tt_)], src, AF.Square, src_keys, [("sqsum", tt_)])
                    else:
                        P.act(sqf[i_][:], src, AF.Square, src_keys, [("sqf", i_)])
                        P.tt("pool", sqsum[:, H(tt_)], sqsum[:, H(tt_)], sqf[i_][:], ALU.add, [("sqsum", tt_), ("sqf", i_)], [("sqsum", tt_)])

                def evac_group(cg, th, dst_dram, dst_key):
                    tok0 = th * T
                    for cc in range(4):
                        c = 4 * cg + cc
                        si = c % 2
                        for tt_ in range(2):
                            bk = cc * 2 + tt_
                            P.act(stg[si][:, H(tt_)], ps[bk][:, 0:512], AF.Copy, [PS(bk)], [("stg", si, tt_)])
                            accum_sq(stg[si][:, H(tt_)], [("stg", si, tt_)], c == 0, tt_)
                        P.dma("act", dst_dram[c][:, tok0:tok0 + T], stg[si][:], reads=[("stg", si, 0), ("stg", si, 1)],
                              writes=[(dst_key, c, th)])

                def norm_pass(th, a_dram, a_key, b_src, scale_idx, out_fn, crange=range(KC)):
                    tok0 = th * T
                    for c in crange:
                        for tt_ in range(2):
                            j = (2 * c + tt_) % 4
                            P.dma("sp", in1[j][:], a_dram[c][:, tok0 + tt_ * 512:tok0 + (tt_ + 1) * 512], reads=[(a_key, c, th)],
                                  writes=[("in1", j)])
                            bsrc, bk_ = b_src(c, tt_)
                            P.dma("sp", in2[j][:], bsrc, reads=bk_, writes=[("in2", j)])
                            P.tt("dve", in1[j][:], in1[j][:], rb[:, H(tt_)], ALU.mult, [("in1", j), ("rb", tt_)], [("in1", j)])
                            si = c % 2
                            P.stt(stg[si][:, H(tt_)], in1[j][:], vec[:, scale_idx, c:c + 1], in2[j][:], ALU.mult, ALU.add,
                                  [("in1", j), ("in2", j)] + VEC, [("stg", si, tt_)])
                            out_fn(c, tt_, si)

                pending = []

                def p7c(th_, crange):
                    t0_ = th_ * T

                    def out7c(c, tt_, si):
                        self.fin.append(P.dma("act", outT[c][:, t0_ + tt_ * 512:t0_ + (tt_ + 1) * 512], stg[si][:, H(tt_)],
                                              reads=[("stg", si, tt_)], writes=[("outT", c, th_, tt_)]))

                    norm_pass(th_, ffnT, "ffnT",
                              lambda c, tt_: (xmidT[c][:, t0_ + tt_ * 512:t0_ + (tt_ + 1) * 512], [("xmidT", c, th_, tt_)]),
                              7, out7c, crange)

                for th in range(2):
                    tok0 = th * T
                    P.dma("pool", aT[:, 0:KC, :], pk(yT)[:, :, tok0:tok0 + T], reads=[("yT", k) for k in range(KC)],
                          writes=[("aT", f) for f in range(KC)])
                    for cg in range(4):
                        wi = wcnt[0] % 2
                        wcnt[0] += 1
                        for q_ in range(4):
                            P.dma("pool", Wo4[wi][:, 4 * q_:4 * q_ + 4, :], w_out_v[:, 4 * q_:4 * q_ + 4, cg * 512:(cg + 1) * 512],
                                  writes=[("WB", wi, q_)])
                        for cc in range(4):
                            for tt_ in range(2):
                                bk = cc * 2 + tt_
                                for k in range(KC):
                                    P.mm(ps[bk][:, 0:512], Wo4[wi][:, k, cc * 128:(cc + 1) * 128], aT[:, k, H(tt_)],
                                         k == 0, k == KC - 1, [("WB", wi, k // 4), ("aT", k)], [PS(bk)])
                        if pending:
                            p7c(pending[0], range(4 * cg, 4 * cg + 4))
                        evac_group(cg, th, yxT, "yxT")
                    pending.clear()
                    rstd_from_sqsum()

                    def out5c(c, tt_, si):
                        accum_sq(stg[si][:, H(tt_)], [("stg", si, tt_)], c == 0, tt_)
                        P.dma("act", xmidT[c][:, tok0 + tt_ * 512:tok0 + (tt_ + 1) * 512], stg[si][:, H(tt_)],
                              reads=[("stg", si, tt_)], writes=[("xmidT", c, th, tt_)])

                    norm_pass(th, yxT, "yxT",
                              lambda c, tt_: (xTv[:, c, NCTX + tok0 + tt_ * 512:NCTX + tok0 + (tt_ + 1) * 512], []), 4, out5c)
                    rstd_from_sqsum()
                    for c in range(KC):
                        for tt_ in range(2):
                            j = (2 * c + tt_) % 4
                            P.dma("sp", in1[j][:], xmidT[c][:, tok0 + tt_ * 512:tok0 + (tt_ + 1) * 512], reads=[("xmidT", c, th, tt_)],
                                  writes=[("in1", j)])
                            P.tt("dve", in1[j][:], in1[j][:], rb[:, H(tt_)], ALU.mult, [("in1", j), ("rb", tt_)], [("in1", j)])
                            P.act(hx2[:, c, H(tt_)], in1[j][:], AF.Identity, [("in1", j)] + VEC, [("hx2", c)],
                                  scale=vec[:, 5, c:c + 1], bias=vec[:, 6, c:c + 1])
                    for fp in range(NF // 2):
                        wi = wcnt[0] % 2
                        wcnt[0] += 1
                        for q_ in range(2):
                            P.dma("pool", Wg_[wi][:, 8 * q_:8 * q_ + 8, :], w_fi_v[:, 8 * q_:8 * q_ + 8, fp * 256:(fp + 1) * 256],
                                  writes=[("WB", wi, q_)])
                            P.dma("pool", Wu_[wi][:, 8 * q_:8 * q_ + 8, :],
                                  w_fi_v[:, 8 * q_:8 * q_ + 8, DFF + fp * 256:DFF + (fp + 1) * 256], writes=[("WB", wi, 2 + q_)])
                        for ff in range(2):
                            f = 2 * fp + ff
                            base = 4 * (f % 2)
                            for tt_ in range(2):
                                for (W_, wk, off) in ((Wg_, 0, 0), (Wu_, 1, 2)):
                                    bk = base + off + tt_
                                    for k in range(KC):
                                        P.mm(ps[bk][:, 0:512], W_[wi][:, k, ff * 128:(ff + 1) * 128], hx2[:, k, H(tt_)],
                                             k == 0, k == KC - 1, [("WB", wi, 2 * wk + k // 8), ("hx2", k)], [PS(bk)])
                            si = f % 2
                            for tt_ in range(2):
                                P.act(sg[si][:, H(tt_)], ps[base + tt_][:, 0:512], AF.Silu, [PS(base + tt_)], [("sg", si, tt_)])
                                P.tt("dve", aT[:, f, H(tt_)], sg[si][:, H(tt_)], ps[base + 2 + tt_][:, 0:512],
                                     ALU.mult, [("sg", si, tt_), PS(base + 2 + tt_)], [("aT", f)])
                    for cg in range(4):
                        for fq in range(4):
                            wi = wcnt[0] % 2
                            wcnt[0] += 1
                            for q_, (fa, fb) in enumerate(((0, 4), (4, 8), (8, 11))):
                                P.dma("pool", Wf4[wi][:, fa:fb, :], w_fo_v[:, fq * 11 + fa:fq * 11 + fb, cg * 512:(cg + 1) * 512],
                                      writes=[("WB", wi, q_)])
                            for cc in range(4):
                                for tt_ in range(2):
                                    bk = cc * 2 + tt_
                                    for f_ in range(11):
                                        f = fq * 11 + f_
                                        P.mm(ps[bk][:, 0:512], Wf4[wi][:, f_, cc * 128:(cc + 1) * 128], aT[:, f, H(tt_)],
                                             f == 0, f == NF - 1, [("WB", wi, f_ // 4), ("aT", f)], [PS(bk)])
                        evac_group(cg, th, ffnT, "ffnT")
                    rstd_from_sqsum()
                    if th == 0:
                        pending.append(0)
                    else:
                        p7c(1, range(KC))
            return self.finish(outT)

    def finish(self, outT):
        if not self.fin:
            pass
        self.S.emit(final_waits=self.fin)
        return self.nc


def _consts():
    l = np.arange(128)[:, None]
    j = np.arange(128)[None, :]
    c = np.zeros((128, 5, 128), np.float32)
    c[:, 0, :] = (l <= j)
    c[:, 1, :] = (l >= j)
    c[:, 2, :] = (l == j)
    c[:, 3, :] = np.where(l > j, NEG, 0.0)
    c[:, 4, :] = np.where(l < j, NEG, 0.0)
    return c


def _pvec(v):
    return np.ascontiguousarray(v.reshape(-1, 128).T)


def prep_inputs(inp):
    f = lambda a: np.ascontiguousarray(np.asarray(a, dtype=np.float32))
    x, c, ctx, c_ctx = f(inp["x"]), f(inp["c"]), f(inp["ctx"]), f(inp["c_ctx"])
    w_in = f(inp["w_in"][0])
    shared = {
        "consts": _consts(),
        "w_mod": f(inp["w_mod"][0]),
        "bmodT": _pvec(f(inp["b_mod"][0])),
        "gains": np.ascontiguousarray(np.stack([_pvec(f(inp[k][0])) for k in
                                                ("g_pre_mix", "g_post_mix", "g_pre_ffn", "g_post_ffn")], axis=1)),
        "w_in": w_in,
        "mhg": _pvec(f(inp["mh_norm_g"][0])),
        "cb": _pvec(f(inp["conv_b"][0])),
        "w_out": f(inp["w_out"][0]),
        "w_ffn_in": f(inp["w_ffn_in"][0]),
        "w_ffn_out": f(inp["w_ffn_out"][0]),
    }
    wgate = w_in[:, 4096:4112]
    bgate = f(inp["b_gates"][0]).reshape(16)
    conv_w = f(inp["conv_w"][0])
    lwa, lwx = f(inp["lru_w_a"][0]), f(inp["lru_w_x"][0])
    lba, lbx, lam = f(inp["lru_b_a"][0]), f(inp["lru_b_x"][0]), f(inp["lru_lambda"][0])
    maps = []
    for core in range(8):
        b, half = core // 2, core % 2
        xl = x[b].reshape(64, 64, D).transpose(1, 0, 2).reshape(NLAT, D)
        xc = ctx[b]
        perm = np.arange(16)
        dsel = [0, 1]
        w5 = np.zeros((5, 1024), np.float32)
        if half == 1:
            xl = xl[::-1]
            xc = xc[::-1]
            perm = np.concatenate([np.arange(8, 16), np.arange(0, 8)])
            dsel = [1, 0]
            w5[1], w5[2], w5[3], w5[4] = conv_w[3], conv_w[2], conv_w[1], conv_w[0]
        else:
            w5[0], w5[1], w5[2], w5[3] = conv_w[0], conv_w[1], conv_w[2], conv_w[3]
        xs = np.concatenate([xc, xl], axis=0)
        m = dict(shared)
        m["xT"] = np.ascontiguousarray(xs.T).reshape(KC, 128, NT)
        cc2 = np.stack([c[b], c_ctx], axis=1)
        m["ccT"] = np.ascontiguousarray(cc2.reshape(KC, 128, 2).transpose(1, 0, 2))
        m["wg"] = np.ascontiguousarray(wgate[:, perm])
        m["bg"] = np.ascontiguousarray(np.broadcast_to(bgate[perm][None, :], (128, 16)))
        m["cw5"] = np.ascontiguousarray(w5.reshape(5, 8, 128).transpose(2, 1, 0))
        m["lwa"] = np.ascontiguousarray(lwa[dsel].transpose(2, 0, 1, 3))
        m["lwx"] = np.ascontiguousarray(lwx[dsel].transpose(2, 0, 1, 3))
        m["lba"] = np.ascontiguousarray(lba[dsel].reshape(2, 8, 128).transpose(2, 0, 1))
        m["lbx"] = np.ascontiguousarray(lbx[dsel].reshape(2, 8, 128).transpose(2, 0, 1))
        m["llam"] = np.ascontiguousarray(lam[dsel].reshape(2, 8, 128).transpose(2, 0, 1))
        maps.append(m)
    return maps


def assemble(results):
    out = np.zeros((4, NLAT, D), np.float32)
    for core in range(8):
        b, half = core // 2, core % 2
        o = results[core]["outT"].reshape(D, NOWN).T
        i = np.arange(NOWN)
        cm = i if half == 0 else (NLAT - 1 - i)
        w, r = cm // 64, cm % 64
        out[b, r * 64 + w] = o
    return out


_CACHE = {}


def kernel(**inputs):
    if "nc" not in _CACHE:
        _CACHE["nc"] = Prog().build()
    nc = _CACHE["nc"]
    maps = prep_inputs(inputs)
    res = run_bass_kernel_spmd(nc, maps, core_ids=list(range(8)))
    return assemble(res.results)
```

```python
import numpy as np
from contextlib import ExitStack
import concourse.bass as bass
import concourse.mybir as mybir
from concourse.bass_utils import run_bass_kernel_spmd

F32 = mybir.dt.float32
BF16 = mybir.dt.bfloat16
AF = mybir.ActivationFunctionType
ALU = mybir.AluOpType
AX = mybir.AxisListType

D = 2048
KC = 16
NCTX = 256
NLAT = 4096
NT = NCTX + NLAT
NOWN = 2048
NCH = NT // 128
DFF = 5632
NF = DFF // 128
EPS = 1e-6
NEG = -30000.0


class _Op:
    __slots__ = ("eng", "fn", "dma", "deps", "signal", "ticket", "sem", "prev_ticket")


class Sched:
    COMPUTE = ("pe", "act", "dve", "pool")
    ALL = ("pe", "act", "dve", "pool", "sp")
    NDMA = {"sp": 24, "act": 16, "pool": 16}

    def __init__(self, nc):
        self.nc = nc
        self.ops = []
        self.last_w = {}
        self.readers = {}
        self.bar_deps = []
        self.bar_applied = set(self.ALL)
        self.last_op = {}
        self.dma_since_bar = []

    def add(self, eng, fn, reads=(), writes=(), dma=False):
        op = _Op()
        op.eng, op.fn, op.dma = eng, fn, dma
        op.deps, op.signal = [], False
        seen = set()

        def dep(o, force=False):
            if o is None or id(o) in seen:
                return
            if not force and o.eng == "pe" and eng == "pe" and not o.dma and not dma:
                return
            seen.add(id(o))
            op.deps.append(o)

        if eng not in self.bar_applied:
            self.bar_applied.add(eng)
            for o in self.bar_deps:
                if o.eng == eng and not o.dma:
                    continue
                dep(o, True)
        xr = [r for r in reads if isinstance(r, tuple) and r[0] == "ps"]
        if xr:
            reads = [r for r in reads if r not in xr]
            writes = list(writes) + xr
        for r in reads:
            dep(self.last_w.get(r))
        for k in writes:
            dep(self.last_w.get(k))
            last = {}
            for rd in self.readers.get(k, ()):
                if rd.dma:
                    dep(rd)
                else:
                    last[rd.eng] = rd
            for rd in last.values():
                dep(rd)
        for k in writes:
            self.last_w[k] = op
            self.readers[k] = []
        for r in reads:
            self.readers.setdefault(r, []).append(op)
        self.ops.append(op)
        if dma:
            self.dma_since_bar.append(op)
        else:
            self.last_op[eng] = op
        return op

    def barrier(self):
        self.bar_deps = list(self.last_op.values()) + list(self.dma_since_bar)
        self.dma_since_bar = []
        self.bar_applied = set()

    def emit(self, final_waits=()):
        nc = self.nc
        for op in self.ops:
            for d in op.deps:
                d.signal = True
        for op in final_waits:
            op.signal = True
        with ExitStack() as es:
            csem = {e: es.enter_context(nc.semaphore("s_" + e)) for e in self.COMPUTE}
            dsem = {q: [es.enter_context(nc.semaphore("d_%s%d" % (q, i))) for i in range(n)]
                    for q, n in self.NDMA.items()}
            ccount = {e: 0 for e in self.COMPUTE}
            dcount = {q: [0] * n for q, n in self.NDMA.items()}
            drr = {q: 0 for q in self.NDMA}
            for op in self.ops:
                if op.dma:
                    q = op.eng
                    i = drr[q]
                    drr[q] = (i + 1) % self.NDMA[q]
                    op.prev_ticket = dcount[q][i]
                    dcount[q][i] += 16
                    op.ticket = dcount[q][i]
                    op.sem = dsem[q][i]
                elif op.signal:
                    ccount[op.eng] += 1
                    op.ticket = ccount[op.eng]
                    op.sem = csem[op.eng]
            per_eng = {e: [] for e in self.ALL}
            for op in self.ops:
                per_eng[op.eng].append(op)
            block = es.enter_context(nc.Block())

            def run(engname, eng):
                seen = {}

                def wait(sem, val):
                    k = id(sem)
                    if seen.get(k, 0) >= val:
                        return
                    seen[k] = val
                    eng.wait_ge(sem, val)

                for op in per_eng[engname]:
                    for d in op.deps:
                        wait(d.sem, d.ticket)
                    if op.dma and op.prev_ticket > 0:
                        wait(op.sem, op.prev_ticket)
                    ins = op.fn(eng)
                    if op.dma:
                        ins.then_inc(op.sem, 16)
                    elif op.signal:
                        ins.then_inc(op.sem, 1)
                if engname == "sp":
                    for op in final_waits:
                        wait(op.sem, op.ticket)

            @block.tensor
            def _(e):
                run("pe", e)

            @block.scalar
            def _(e):
                run("act", e)

            @block.vector
            def _(e):
                run("dve", e)

            @block.gpsimd
            def _(e):
                run("pool", e)

            @block.sync
            def _(e):
                run("sp", e)


class Prog:
    def __init__(self, debug=False, upto=99, sub=9, nheads=4, lite=False, skip=()):
        self.lite = lite
        self.skip = skip
        self.debug = debug
        self.upto = upto
        self.sub = sub
        self.nheads = nheads
        self.nc = bass.Bass("TRN2", target_bir_lowering=False)
        self.S = Sched(self.nc)
        self.fin = []
        self.dbg_names = []

    def din(self, name, shape, dt=F32):
        return self.nc.dram_tensor(name, list(shape), dt, kind="ExternalInput").ap()

    def dscratch(self, name, shape, dt):
        if self.debug:
            self.dbg_names.append(name)
            return self.nc.dram_tensor(name, list(shape), dt, kind="ExternalOutput").ap()
        return self.nc.dram_tensor(name, list(shape), dt, kind="Internal").ap()

    def dump(self, name, ap, reads, dt=F32):
        if not self.debug:
            return
        self.dbg_names.append(name)
        d = self.nc.dram_tensor(name, list(ap.shape), dt, kind="ExternalOutput").ap()
        self.fin.append(self.S.add("sp", lambda e: e.dma_start(out=d, in_=ap), reads=reads, dma=True))

    def dma(self, q, out, in_, reads=(), writes=()):
        return self.S.add(q, lambda e: e.dma_start(out=out, in_=in_), reads=reads, writes=writes, dma=True)

    def mm(self, ps, lhsT, rhs, start, stop, reads, writes):
        return self.S.add("pe", lambda e: e.matmul(ps, lhsT=lhsT, rhs=rhs, start=start, stop=stop),
                          reads=reads, writes=writes)

    def act(self, out, in_, func, reads, writes, **kw):
        return self.S.add("act", lambda e: e.activation(out=out, in_=in_, func=func, **kw), reads=reads, writes=writes)

    def ts(self, eng, out, in0, s1, s2, op0, op1, reads, writes):
        if op1 is None:
            return self.S.add(eng, lambda e: e.tensor_scalar(out=out, in0=in0, scalar1=s1, scalar2=None, op0=op0),
                              reads=reads, writes=writes)
        return self.S.add(eng, lambda e: e.tensor_scalar(out=out, in0=in0, scalar1=s1, scalar2=s2, op0=op0, op1=op1),
                          reads=reads, writes=writes)

    def tt(self, eng, out, in0, in1, op, reads, writes):
        return self.S.add(eng, lambda e: e.tensor_tensor(out=out, in0=in0, in1=in1, op=op), reads=reads, writes=writes)

    def stt(self, out, in0, scalar, in1, op0, op1, reads, writes):
        return self.S.add("dve", lambda e: e.scalar_tensor_tensor(out=out, in0=in0, scalar=scalar, in1=in1, op0=op0, op1=op1),
                          reads=reads, writes=writes)

    def copy(self, eng, out, in_, reads, writes):
        return self.S.add(eng, lambda e: e.tensor_copy(out=out, in_=in_), reads=reads, writes=writes)

    def memset(self, eng, ap, val, writes):
        return self.S.add(eng, lambda e: e.memset(ap, val), writes=writes)

    def recip(self, out, in_, reads, writes):
        return self.S.add("dve", lambda e: e.reciprocal(out=out, in_=in_), reads=reads, writes=writes)

    def build(self):
        nc, S = self.nc, self.S
        P = self
        xT = P.din("xT", [KC, 128, NT])
        ccT = P.din("ccT", [128, KC, 2])
        consts = P.din("consts", [128, 5, 128])
        w_mod = None if self.lite else P.din("w_mod", [D, 6 * D])
        bmodT = P.din("bmodT", [128, 96])
        gains = P.din("gains", [128, 4, KC])
        w_in = None if (self.lite and self.upto < 3) else P.din("w_in", [D, 6160])
        wg = P.din("wg", [D, 16])
        bg = P.din("bg", [128, 16])
        mhg = P.din("mhg", [128, 8])
        cw5 = P.din("cw5", [128, 8, 5])
        cb = P.din("cb", [128, 8])
        lwa = P.din("lwa", [128, 2, 8, 128])
        lwx = P.din("lwx", [128, 2, 8, 128])
        lba = P.din("lba", [128, 2, 8])
        lbx = P.din("lbx", [128, 2, 8])
        llam = P.din("llam", [128, 2, 8])
        w_out = None if self.lite else P.din("w_out", [D, D])
        w_ffn_in = None if self.lite else P.din("w_ffn_in", [D, 2 * DFF])
        w_ffn_out = None if self.lite else P.din("w_ffn_out", [DFF, D])
        outT = nc.dram_tensor("outT", [KC, 128, NOWN], F32, kind="ExternalOutput").ap()
        hxS = P.dscratch("hxS", [KC, 128, NT], BF16)
        yT = P.dscratch("yT", [KC, 128, NOWN], BF16)
        yxT = P.dscratch("yxT", [KC, 128, NOWN], F32)
        xmidT = P.dscratch("xmidT", [KC, 128, NOWN], F32)
        ffnT = P.dscratch("ffnT", [KC, 128, NOWN], F32)
        XRs = self.nc.dram_tensor("XRs", [8, 128, NT], F32, kind="Internal").ap()
        GGs = self.nc.dram_tensor("GGs", [8, 128, NOWN], F32, kind="Internal").ap()

        def pk(ap):
            return ap.rearrange("k p t -> p k t")

        with ExitStack() as top:
            sbt = lambda name, shape, dt: top.enter_context(nc.sbuf_tensor(name, list(shape), dt))
            ps = [top.enter_context(nc.psum_tensor("ps%d" % i, [128, 512], F32)) for i in range(8)]
            PS = lambda i: ("ps", i)
            cst = sbt("cst", [128, 5, 128], F32)
            cstb = sbt("cstb", [128, 5, 128], BF16)
            ones_f = sbt("ones_f", [128, 128], F32)
            ones_b = sbt("ones_b", [128, 128], BF16)
            P.dma("sp", cst[:], consts, writes=["cst"])
            P.copy("dve", cstb[:], cst[:], ["cst"], ["cstb"])
            P.memset("dve", ones_f[:], 1.0, ["ones_f"])
            P.memset("dve", ones_b[:], 1.0, ["ones_b"])
            U_f, L_f = cst[:, 0, :], cst[:, 1, :]
            ident_b = cstb[:, 2, :]
            mneg_b = [cstb[:, 3, :], cstb[:, 4, :]]
            mod = sbt("mod", [128, 96, 2], F32)
            vec = sbt("vec", [128, 8, KC], F32)
            gn = sbt("gn", [128, 4, KC], F32)
            P.dma("sp", gn[:], gains, writes=["gn"])

            with ExitStack() as st:
                sb = lambda name, shape, dt: st.enter_context(nc.sbuf_tensor(name, list(shape), dt))
                cc = sb("cc", [128, KC, 2], F32)
                ccs = sb("ccs", [128, KC, 2], BF16)
                bm = sb("bm", [128, 96], F32)
                P.dma("sp", cc[:], ccT, writes=["cc"])
                P.dma("sp", bm[:], bmodT, writes=["bm"])
                P.act(ccs[:], cc[:], AF.Silu, ["cc"], ["ccs"])
                wmb = [sb("wmb%d" % i, [128, KC, 512], BF16) for i in range(3)]
                wv = None if self.lite else w_mod.rearrange("(k p) n -> p k n", p=128)
                if self.lite:
                    P.memset("dve", ps[0][:, 0:192], 0.0, [PS(0)])
                for g in range(0 if self.lite else 24):
                    bi = g % 3
                    for q_ in range(2):
                        P.dma("pool", wmb[bi][:, 8 * q_:8 * q_ + 8, :], wv[:, 8 * q_:8 * q_ + 8, g * 512:(g + 1) * 512],
                              writes=[("wmb", bi, q_)])
                    for jj in range(4):
                        j = g * 4 + jj
                        for k in range(KC):
                            P.mm(ps[0][:, 2 * j:2 * j + 2], wmb[bi][:, k, jj * 128:(jj + 1) * 128], ccs[:, k, :],
                                 k == 0, k == KC - 1, [("wmb", bi, k // 8), "ccs"], [PS(0)])
                psv = ps[0][:, 0:192].rearrange("p (j t) -> p j t", t=2)
                for t in range(2):
                    P.tt("dve", mod[:, :, t], psv[:, :, t], bm[:], ALU.add, [PS(0), "bm"], [("mod", t)])
                m = lambda i, t: mod[:, i * KC:(i + 1) * KC, t]
                for t in range(2):
                    P.stt(vec[:, 0 + 2 * t, :], m(1, t), 1.0, gn[:, 0, :], ALU.add, ALU.mult, [("mod", t), "gn"], [("vec", 0 + 2 * t)])
                    P.copy("dve", vec[:, 1 + 2 * t, :], m(0, t), [("mod", t)], [("vec", 1 + 2 * t)])
                P.tt("dve", vec[:, 4, :], m(2, 0), gn[:, 1, :], ALU.mult, [("mod", 0), "gn"], [("vec", 4)])
                P.stt(vec[:, 5, :], m(4, 0), 1.0, gn[:, 2, :], ALU.add, ALU.mult, [("mod", 0), "gn"], [("vec", 5)])
                P.copy("dve", vec[:, 6, :], m(3, 0), [("mod", 0)], [("vec", 6)])
                P.tt("dve", vec[:, 7, :], m(5, 0), gn[:, 3, :], ALU.mult, [("mod", 0), "gn"], [("vec", 7)])
                P.dump("d_mod", mod[:], [("mod", 0), ("mod", 1)])
                P.dump("d_vec", vec[:], [("vec", i) for i in range(8)])
                S.barrier()
            VEC = [("vec", i) for i in range(8)]
            if self.upto < 1:
                return self.finish(outT)

            def rms_tiles(st, src_of, ntok_tiles, gs_idx_of, dst_of, tagp):
                sb = lambda name, shape, dt: st.enter_context(nc.sbuf_tensor(name, list(shape), dt))
                xt = [sb(tagp + "xt%d" % i, [128, KC, 512], F32) for i in range(2)]
                sq = sb(tagp + "sq", [128, KC, 512], BF16)
                sd = sb(tagp + "sd", [128, 512], F32)
                rs = sb(tagp + "rs", [128, 512], F32)
                hx = [sb(tagp + "hx%d" % i, [128, KC, 512], BF16) for i in range(2)]
                for i in range(ntok_tiles):
                    src, n, rk = src_of(i)
                    bi = i % 2
                    for q_ in range(4):
                        P.dma("sp", xt[bi][:, 4 * q_:4 * q_ + 4, 0:n], src[:, 4 * q_:4 * q_ + 4, :], reads=rk, writes=[(tagp + "xt", bi, q_)])
                    XTK = [(tagp + "xt", bi, q_) for q_ in range(4)]
                    P.act(sq[:, :, 0:n], xt[bi][:, :, 0:n], AF.Square, XTK, [tagp + "sq"])
                    for k in range(KC):
                        P.mm(ps[1][:, 0:n], ones_b[:], sq[:, k, 0:n], k == 0, k == KC - 1, [tagp + "sq", "ones_b"], [PS(1)])
                    P.act(sd[:, 0:n], ps[1][:, 0:n], AF.Sqrt, [PS(1)], [tagp + "sd"], scale=1.0 / D, bias=EPS)
                    P.recip(rs[:, 0:n], sd[:, 0:n], [tagp + "sd"], [tagp + "rs"])
                    P.tt("dve", xt[bi][:, :, 0:n], xt[bi][:, :, 0:n], rs[:, 0:n].unsqueeze(1).broadcast_to([128, KC, n]),
                         ALU.mult, XTK + [tagp + "rs"], XTK)
                    gi = gs_idx_of(i)
                    for k in range(KC):
                        P.act(hx[bi][:, k, 0:n], xt[bi][:, k, 0:n], AF.Identity, [(tagp + "xt", bi, k // 4)] + VEC, [(tagp + "hx", bi, k)],
                              scale=vec[:, gi, k:k + 1], bias=vec[:, gi + 1, k:k + 1])
                    dst, wk = dst_of(i)
                    P.dma("act", dst, hx[bi][:, :, 0:n], reads=[(tagp + "hx", bi, k) for k in range(KC)], writes=wk)

            with ExitStack() as st:
                xTv = pk(xT)
                hxv = pk(hxS)

                def src1(i):
                    if i == 0:
                        return xTv[:, :, 0:256], 256, []
                    return xTv[:, :, 256 + (i - 1) * 512: 256 + i * 512], 512, []

                def dst1(i):
                    if i == 0:
                        return hxv[:, :, 0:256], [("hxS", 0)]
                    a = 256 + (i - 1) * 512
                    return hxv[:, :, a:a + 512], [("hxS", 2 * i - 1), ("hxS", 2 * i)]

                rms_tiles(st, src1, 9, lambda i: 2 if i == 0 else 0, dst1, "s1")
                S.barrier()
            if self.upto < 2:
                return self.finish(outT)
            hxv = pk(hxS)
            HXK = lambda a, n: [("hxS", c) for c in range(a // 128, (a + n) // 128)]
            w_in_v = None if w_in is None else w_in.rearrange("(k p) n -> p k n", p=128)

            with ExitStack() as st:
                sb = lambda name, shape, dt: st.enter_context(nc.sbuf_tensor(name, list(shape), dt))
                l1 = sb("l1", [128, NCH, 8], F32)
                bias_ = sb("bias_", [128, NCH, 8], F32)
                winter = sb("winter", [128, NCH, 8], F32)
                wloc = sb("wloc", [128, NCH, 8], F32)
                decay = sb("decay", [128, NCH, 8], F32)
                GK = ["l1", "bias_", "winter", "wloc", "decay"]
                hxt = [sb("hxt%d" % i, [128, KC, 256], BF16) for i in range(2)]
                with ExitStack() as st2:
                    sb2 = lambda name, shape, dt: st2.enter_context(nc.sbuf_tensor(name, list(shape), dt))
                    wgb = sb2("wgb", [128, KC, 16], BF16)
                    bgs = sb2("bgs", [128, 16], F32)
                    gts = sb2("gts", [128, NCH, 16], F32)
                    e1 = sb2("e1", [128, NCH, 8], F32)
                    tmpw = sb2("tmpw", [128, NCH, 8], F32)
                    P.dma("pool", wgb[:], wg.rearrange("(k p) n -> p k n", p=128), writes=["wgb"])
                    P.dma("sp", bgs[:], bg, writes=["bgs"])
                    Wxr = sb2("Wxr", [128, KC, 1024], BF16)
                    Wgr = sb2("Wgr", [128, KC, 1024], BF16)
                    xrs = [sb2("xrs%d" % i, [128, 8, 256], F32) for i in range(2)]
                    ggs = [sb2("ggs%d" % i, [128, 8, 256], F32) for i in range(2)]
                    for hh_ in range(2):
                        P.dma("pool", Wxr[:, :, 512 * hh_:512 * (hh_ + 1)], w_in_v[:, :, 4112 + 512 * hh_:4112 + 512 * (hh_ + 1)],
                              writes=[("Wxr", hh_)])
                        P.dma("pool", Wgr[:, :, 512 * hh_:512 * (hh_ + 1)], w_in_v[:, :, 5136 + 512 * hh_:5136 + 512 * (hh_ + 1)],
                              writes=[("Wgr", hh_)])
                    rot2 = [0]
                    BK2 = [0, 1, 4, 5, 6, 7]

                    def nb2():
                        rot2[0] = (rot2[0] + 1) % 6
                        return BK2[rot2[0]]
                    if "cut1" in self.skip:
                        return self.finish(outT)
                    for i in range(NT // 256):
                        bi = i % 2
                        for q_ in range(2):
                            P.dma("sp", hxt[bi][:, 8 * q_:8 * q_ + 8, :], hxv[:, 8 * q_:8 * q_ + 8, 256 * i:256 * i + 256],
                                  reads=HXK(256 * i, 256), writes=[("hxt", bi, q_)])
                        for cc in range(2):
                            c = 2 * i + cc
                            bank, col = (2 if c < 17 else 3), (c % 17) * 16
                            for k in range(KC):
                                P.mm(ps[bank][:, col:col + 16], hxt[bi][:, k, cc * 128:(cc + 1) * 128], wgb[:, k, :],
                                     k == 0, k == KC - 1, [("hxt", bi, k // 8), "wgb"], [PS(bank)])
                        for n in range(8):
                            bk = nb2()
                            for k in range(KC):
                                P.mm(ps[bk][:, 0:256], Wxr[:, k, 128 * n:128 * (n + 1)], hxt[bi][:, k, :], k == 0, k == KC - 1,
                                     [("hxt", bi, k // 8), ("Wxr", n // 4)], [PS(bk)])
                            if n % 2 == 0:
                                P.copy("dve", xrs[bi][:, n, :], ps[bk][:, 0:256], [PS(bk)], [("xrs", bi, n)])
                            else:
                                P.act(xrs[bi][:, n, :], ps[bk][:, 0:256], AF.Copy, [PS(bk)], [("xrs", bi, n)])
                        P.dma("act", XRs.rearrange("n p t -> p n t")[:, :, 256 * i:256 * i + 256], xrs[bi][:],
                              reads=[("xrs", bi, n) for n in range(8)], writes=[("XRs", i)])
                        if 1 <= i <= 8:
                            for n in range(8):
                                bk = nb2()
                                for k in range(KC):
                                    P.mm(ps[bk][:, 0:256], Wgr[:, k, 128 * n:128 * (n + 1)], hxt[bi][:, k, :], k == 0, k == KC - 1,
                                         [("hxt", bi, k // 8), ("Wgr", n // 4)], [PS(bk)])
                                P.act(ggs[bi][:, n, :], ps[bk][:, 0:256], AF.Gelu_apprx_tanh, [PS(bk)], [("ggs", bi, n)])
                            P.dma("act", GGs.rearrange("n p t -> p n t")[:, :, 256 * (i - 1):256 * i], ggs[bi][:],
                                  reads=[("ggs", bi, n) for n in range(8)], writes=[("GGs", i - 1)])
                    if "cut2" in self.skip:
                        return self.finish(outT)
                    g4 = lambda ap: ap.rearrange("p c (t h) -> p c t h", h=4)
                    for half_, (ca, cb_, bank) in enumerate(((0, 17, 2), (17, 34, 3))):
                        P.tt("dve", gts[:, ca:cb_, :], ps[bank][:, 0:272].rearrange("p (c g) -> p c g", g=16),
                             bgs[:].unsqueeze(1).broadcast_to([128, 17, 16]), ALU.add, [PS(bank), "bgs"], [("gts", half_)])
                    GT = [("gts", 0), ("gts", 1)]
                    if "cut3" in self.skip:
                        return self.finish(outT)
                    P.act(g4(e1[:]), g4(gts[:])[:, :, 1::2, :], AF.Exp, GT, ["e1"], scale=-1.0)
                    P.act(l1[:], e1[:], AF.Ln, ["e1"], ["l1"], bias=1.0)
                    if "cut4" in self.skip:
                        return self.finish(outT)
                    for c in range(NCH):
                        bank, col = (4 if c < 17 else 5), (c % 17) * 16
                        P.mm(ps[bank][:, col:col + 4], U_f, l1[:, c, 0:4], True, True, ["l1", "cst"], [PS(bank)])
                        P.mm(ps[bank][:, col + 4:col + 8], L_f, l1[:, c, 4:8], True, True, ["l1", "cst"], [PS(bank)])
                        P.mm(ps[bank][:, col + 8:col + 16], ones_f[:], l1[:, c, 0:8], True, True, ["l1", "ones_f"], [PS(bank)])
                    if "cut5" in self.skip:
                        return self.finish(outT)
                    for half_, (ca, cb_, bank) in enumerate(((0, 17, 4), (17, 34, 5))):
                        gp = ps[bank][:, 0:272].rearrange("p (c g) -> p c g", g=16)
                        P.tt("dve", g4(bias_[:, ca:cb_, :]), g4(gts[:, ca:cb_, :])[:, :, 0::2, :], g4(gp[:, :, 0:8]), ALU.add,
                             GT + [PS(bank)], [("bias_", half_)])
                        P.act(winter[:, ca:cb_, :], gp[:, :, 0:8], AF.Exp, [PS(bank)], [("winter", half_)], scale=-1.0)
                        P.act(decay[:, ca:cb_, :], gp[:, :, 8:16], AF.Exp, [PS(bank)], [("decay", half_)], scale=-1.0)
                        P.tt("dve", tmpw[:, ca:cb_, :], bias_[:, ca:cb_, :], gp[:, :, 8:16], ALU.subtract,
                             [("bias_", half_), PS(bank)], [("tmpw", half_)])
                        P.act(wloc[:, ca:cb_, :], tmpw[:, ca:cb_, :], AF.Exp, [("tmpw", half_)], [("wloc", half_)])
                    if "cut6" in self.skip:
                        return self.finish(outT)
                    P.dump("d_gts", gts[:], GT)
                    P.dump("d_l1", l1[:], ["l1"])
                    P.dump("d_bias", bias_[:], [("bias_", 0), ("bias_", 1)])
                    P.dump("d_winter", winter[:], [("winter", 0), ("winter", 1)])
                    P.dump("d_wloc", wloc[:], [("wloc", 0), ("wloc", 1)])
                    P.dump("d_decay", decay[:], [("decay", 0), ("decay", 1)])
                    if "cut7" in self.skip:
                        return self.finish(outT)
                    S.barrier()
                if self.upto < 3:
                    return self.finish(outT)
                GK = ["l1"] + [(n_, h_) for n_ in ("bias_", "winter", "wloc", "decay") for h_ in (0, 1)]

                Wkv2 = [sb("Wkv%d" % i, [128, KC, 512], BF16) for i in range(2)]
                Wq2 = [sb("Wq%d" % i, [128, KC, 256], BF16) for i in range(2)]
                Wo2 = [sb("Wo%d" % i, [128, KC, 256], BF16) for i in range(2)]

                def load_head_weights(h_):
                    p_ = h_ % 2
                    P.dma("pool", Wkv2[p_][:, :, 0:256], w_in_v[:, :, 1024 + 256 * h_:1024 + 256 * (h_ + 1)], writes=[("Wk", p_)])
                    P.dma("pool", Wkv2[p_][:, :, 256:512], w_in_v[:, :, 2048 + 256 * h_:2048 + 256 * (h_ + 1)], writes=[("Wv", p_)])
                    P.dma("pool", Wq2[p_][:], w_in_v[:, :, 256 * h_:256 * (h_ + 1)], writes=[("Wq", p_)])
                    P.dma("pool", Wo2[p_][:], w_in_v[:, :, 3072 + 256 * h_:3072 + 256 * (h_ + 1)], writes=[("Wo", p_)])

                QT = sb("QT", [128, 2, NOWN], BF16)
                KT = sb("KT", [128, 2, NOWN], BF16)
                OT = sb("OT", [128, 2, NOWN], BF16)
                Ktok = sb("Ktok", [128, NCH, 256], BF16)
                V1 = sb("V1", [128, NCH, 258], BF16)
                hs = sb("hs", [128, 16, 256], F32)
                CT = [sb("CT%d" % d_, [128, 2, 258], F32) for d_ in range(2)]
                CTb = [sb("CTb%d" % d_, [128, 2, 258], BF16) for d_ in range(2)]
                lfU = [[sb("lfU%d%d" % (d_, p_), [128, 128], F32) for p_ in range(2)] for d_ in range(2)]
                DT = [[sb("DT%d%d" % (d_, p_), [128, 128], F32) for p_ in range(2)] for d_ in range(2)]
                PT = [[sb("PT%d%d" % (d_, p_), [128, 128], BF16) for p_ in range(2)] for d_ in range(2)]
                isb = [[sb("isb%d%d" % (d_, p_), [128, 258], F32) for p_ in range(2)] for d_ in range(2)]
                nd = [sb("nd%d" % d_, [128, 258], F32) for d_ in range(2)]
                den = [sb("den%d" % d_, [128, 1], F32) for d_ in range(2)]
                rden = [sb("rden%d" % d_, [128, 1], F32) for d_ in range(2)]
                Vw = [[sb("Vw%d%d" % (d_, p_), [128, 258], BF16) for p_ in range(2)] for d_ in range(2)]
                junk = sb("junk", [128, 256], F32)
                ssq = sb("ssq", [128, 16], F32)
                ssd = sb("ssd", [128, 16], F32)
                srs = sb("srs", [128, 16], F32)
                hn = sb("hn", [128, 16, 256], BF16)
                ymT = sb("ymT", [128, 2, NOWN], BF16)
                mh = sb("mh", [128, 8], F32)
                P.dma("sp", mh[:], mhg, writes=["mh"])
                P.memset("pool", V1[:, :, 256:257], 1.0, ["V1one"])
                P.memset("pool", V1[:, :, 257:258], 0.0, ["V1pad"])
                masks = [U_f, L_f]

                def front(h, d_, c, produce, update, par):
                    idx = d_ * 4 + h
                    A, B = 4 * d_, 4 * d_ + 1
                    if produce:
                        oc = c - 2
                        tok = oc * 128
                        lf_, dt_, pt_, is_ = lfU[d_][par], DT[d_][par], PT[d_][par], isb[d_][par]
                        P.ts("dve", lf_[:], masks[d_], l1[:, c, idx:idx + 1], -1.0, ALU.mult, ALU.mult,
                             ["cst", "l1"], [("lfU", d_, par)])
                        P.mm(ps[A][:, 0:128], ones_f[:], lf_[:], True, False, ["ones_f", ("lfU", d_, par)], [PS(A)])
                        P.mm(ps[A][:, 0:128], ident_b, mneg_b[d_], False, True, ["cstb"], [PS(A)])
                        P.act(dt_[:], ps[A][:, 0:128], AF.Exp, [PS(A)] + GK, [("DT", d_, par)], bias=bias_[:, c, idx:idx + 1], scale=1.0)
                        for ec in range(2):
                            P.mm(ps[A][:, 128:256], KT[:, ec, tok:tok + 128], QT[:, ec, tok:tok + 128], ec == 0, ec == 1,
                                 [("KT", oc // 2), ("QT", oc // 2)], [PS(A)])
                        P.tt("dve", pt_[:], ps[A][:, 128:256], dt_[:], ALU.mult, [PS(A), ("DT", d_, par)], [("PT", d_, par)])
                        P.mm(ps[B][:, 0:257], pt_[:], V1[:, c, 0:257], True, True, [("PT", d_, par), ("V1", c), "V1one"], [PS(B)])
                        P.act(is_[:, 0:257], ps[B][:, 0:257], AF.Copy, [PS(B)], [("isb", d_, par)])
                    if update:
                        P.ts("pool", Vw[d_][par][:, 0:257], V1[:, c, 0:257], wloc[:, c, idx:idx + 1], 1.0, ALU.mult, ALU.mult,
                             [("V1", c), "V1one"] + GK, [("Vw", d_, par)])

                def back(h, d_, c, produce, update, par):
                    idx = d_ * 4 + h
                    C_, Dk = 4 * d_ + 2, 4 * d_ + 3
                    if produce:
                        oc = c - 2
                        tok = oc * 128
                        for ec in range(2):
                            P.mm(ps[C_][:, 0:257], QT[:, ec, tok:tok + 128], CTb[d_][:, ec, 0:257], ec == 0, ec == 1,
                                 [("QT", oc // 2), ("CTb", d_, ec)], [PS(C_)])
                        P.stt(nd[d_][:, 0:257], ps[C_][:, 0:257], winter[:, c, idx:idx + 1], isb[d_][par][:, 0:257], ALU.mult, ALU.add,
                              [PS(C_), ("isb", d_, par)] + GK, [("nd", d_)])
                        P.ts("dve", den[d_][:], nd[d_][:, 256:257], -1.0, 1.0, ALU.mult, ALU.max, [("nd", d_)], [("den", d_)])
                        P.tt("dve", den[d_][:], den[d_][:], nd[d_][:, 256:257], ALU.max, [("nd", d_), ("den", d_)], [("den", d_)])
                        P.recip(rden[d_][:], den[d_][:], [("den", d_)], [("rden", d_)])
                        P.stt(hs[:, oc, :], nd[d_][:, 0:256], rden[d_][:, 0:1], hs[:, oc, :], ALU.mult, ALU.add,
                              [("nd", d_), ("rden", d_), ("hs", oc)], [("hs", oc)])
                    if update:
                        for ec in range(2):
                            bank = Dk if ec == 0 else C_
                            P.mm(ps[bank][:, 0:257], Ktok[:, c, ec * 128:(ec + 1) * 128], Vw[d_][par][:, 0:257], True, True,
                                 [("Ktok", c), ("Vw", d_, par)], [PS(bank)])
                            P.stt(CT[d_][:, ec, 0:257], CT[d_][:, ec, 0:257], decay[:, c, idx:idx + 1], ps[bank][:, 0:257],
                                  ALU.mult, ALU.add, [("CT", d_, ec), PS(bank)] + GK, [("CT", d_, ec)])
                            P.act(CTb[d_][:, ec, 0:257], CT[d_][:, ec, 0:257], AF.Copy, [("CT", d_, ec)], [("CTb", d_, ec)])

                for h in range(self.nheads):
                    hp = h % 2
                    Wkv, Wq, Wo = Wkv2[hp], Wq2[hp], Wo2[hp]
                    if h == 0:
                        load_head_weights(0)
                    P.memset("dve", hs[:], 0.0, [("hs", oc) for oc in range(16)])
                    for d_ in range(2):
                        P.memset("dve", CT[d_][:], 0.0, [("CT", d_, 0), ("CT", d_, 1)])
                        P.memset("pool", CTb[d_][:], 0.0, [("CTb", d_, 0), ("CTb", d_, 1)])
                    rot = [0]

                    def nb():
                        rot[0] = (rot[0] + 1) % 6
                        return rot[0]

                    for i in range(NT // 256):
                        bi = i % 2
                        for q_ in range(2):
                            P.dma("sp", hxt[bi][:, 8 * q_:8 * q_ + 8, :], hxv[:, 8 * q_:8 * q_ + 8, 256 * i:256 * i + 256],
                                  reads=HXK(256 * i, 256), writes=[("hxt", bi, q_)])
                        for cc in range(2):
                            c = 2 * i + cc
                            bk = nb()
                            for k in range(KC):
                                P.mm(ps[bk][:, 0:512], hxt[bi][:, k, cc * 128:(cc + 1) * 128], Wkv[:, k, :], k == 0, k == KC - 1,
                                     [("hxt", bi, k // 8), ("Wk", hp), ("Wv", hp)], [PS(bk)])
                            P.act(Ktok[:, c, :], ps[bk][:, 0:256], AF.Copy, [PS(bk)], [("Ktok", c)], scale=0.0625)
                            P.copy("dve", V1[:, c, 0:256], ps[bk][:, 256:512], [PS(bk)], [("V1", c)])
                        if 1 <= i <= 8:
                            ti = i - 1
                            o = ti * 256
                            for wi, (W_, wk, dst, dk) in enumerate(((Wq, ("Wq", hp), QT, "QT"), (Wkv, ("Wk", hp), KT, "KT"), (Wo, ("Wo", hp), OT, "OT"))):
                                for ec in range(2):
                                    bk = nb()
                                    for k in range(KC):
                                        P.mm(ps[bk][:, 0:256], W_[:, k, ec * 128:(ec + 1) * 128], hxt[bi][:, k, :], k == 0, k == KC - 1,
                                             [("hxt", bi, k // 8), wk], [PS(bk)])
                                    if dk == "QT":
                                        P.copy("dve", dst[:, ec, o:o + 256], ps[bk][:, 0:256], [PS(bk)], [(dk, ti)])
                                    elif dk == "KT":
                                        P.act(dst[:, ec, o:o + 256], ps[bk][:, 0:256], AF.Copy, [PS(bk)], [(dk, ti)], scale=0.0625)
                                    else:
                                        P.act(dst[:, ec, o:o + 256], ps[bk][:, 0:256], AF.Sigmoid, [PS(bk)], [(dk, ti)])
                    if h + 1 < self.nheads:
                        load_head_weights(h + 1)
                    if self.debug and h == 0:
                        P.dump("d_QT", QT[:], [("QT", t_) for t_ in range(8)], BF16)
                        P.dump("d_KT", KT[:], [("KT", t_) for t_ in range(8)], BF16)
                        P.dump("d_Ktok", Ktok[:], [("Ktok", c) for c in range(NCH)], BF16)
                        P.dump("d_V1", V1[:], [("V1", c) for c in range(NCH)] + ["V1one", "V1pad"], BF16)
                    if self.sub < 1:
                        continue
                    fwd = [(0, False, True), (1, False, True)] + [(c, True, c < 17) for c in range(2, 18)]
                    bwd = [(1, False, True), (0, False, True)] + [(c, False, True) for c in range(33, 17, -1)] + \
                          [(c, True, c > 2) for c in range(17, 1, -1)]
                    chains = [(0, fwd, 16), (1, bwd, 0)]
                    for t in range(-1, 34):
                        for (d_, steps, t0) in chains:
                            j = t + 1 - t0
                            if 0 <= j < len(steps):
                                front(h, d_, *steps[j], j % 2)
                        for (d_, steps, t0) in chains:
                            j = t - t0
                            if 0 <= j < len(steps):
                                back(h, d_, *steps[j], j % 2)
                    if self.debug and h == 0:
                        P.dump("d_hs", hs[:], [("hs", oc) for oc in range(16)])
                        P.dump("d_CT0", CT[0][:], [("CT", 0, 0), ("CT", 0, 1)])
                        P.dump("d_CT1", CT[1][:], [("CT", 1, 0), ("CT", 1, 1)])
                    if self.sub < 2:
                        continue
                    for oc in range(16):
                        S.add("act", (lambda oc: lambda e: e.activation(out=junk[:], in_=hs[:, oc, :], func=AF.Square,
                                                                      accum_out=ssq[:, oc:oc + 1]))(oc),
                              reads=[("hs", oc)], writes=["junk", ("ssq", oc)])
                    P.act(ssd[:], ssq[:], AF.Sqrt, [("ssq", oc) for oc in range(16)], ["ssd"], scale=1.0 / 256, bias=EPS)
                    P.recip(srs[:], ssd[:], ["ssd"], ["srs"])
                    P.tt("dve", hn[:], hs[:], srs[:].unsqueeze(2).broadcast_to([128, 16, 256]), ALU.mult,
                         [("hs", oc) for oc in range(16)] + ["srs"], ["hn"])
                    tbanks = [(ps[7][:].bitcast(BF16), PS(7)), (ps[6][:].bitcast(BF16), PS(6))]
                    for og in range(4):
                        tb, tk = tbanks[og % 2]
                        for oo in range(4):
                            oc = og * 4 + oo
                            for ec in range(2):
                                slot = oo * 2 + ec
                                S.add("pe", (lambda oc, ec, slot, tb: lambda e: e.transpose(out=tb[:, slot * 128:(slot + 1) * 128],
                                                                                              in_=hn[:, oc, ec * 128:(ec + 1) * 128],
                                                                                              identity=ident_b))(oc, ec, slot, tb),
                                      reads=["hn", "cstb"], writes=[tk])
                        for ec in range(2):
                            tv = tb.rearrange("p (o e t) -> p o e t", o=4, e=2)[:, :, ec, :]
                            P.stt(ymT[:, ec, og * 512:(og + 1) * 512].rearrange("p (o t) -> p o t", o=4), tv,
                                  mh[:, 2 * h + ec:2 * h + ec + 1],
                                  OT[:, ec, og * 512:(og + 1) * 512].rearrange("p (o t) -> p o t", o=4), ALU.mult, ALU.mult,
                                  [tk, "mh"] + [("OT", t_) for t_ in range(8)], [("ymT", ec)])
                    for ec in range(2):
                        P.dma("act", yT[2 * h + ec], ymT[:, ec, :], reads=[("ymT", ec)], writes=[("yT", 2 * h + ec)])
                S.barrier()
            if self.upto < 4:
                return self.finish(outT)

            with ExitStack() as st:
                sb = lambda name, shape, dt: st.enter_context(nc.sbuf_tensor(name, list(shape), dt))
                lwa_b = sb("lwa_b", [128, 2, 8, 128], BF16)
                lwx_b = sb("lwx_b", [128, 2, 8, 128], BF16)
                cw = sb("cw", [128, 8, 5], F32)
                cbv = sb("cbv", [128, 8], F32)
                ba = sb("ba", [128, 2, 8], F32)
                bx = sb("bx", [128, 2, 8], F32)
                lam = sb("lam", [128, 2, 8], F32)
                spl = sb("spl", [128, 2, 8], F32)
                s1 = sb("s1", [128, 2, 8], F32)
                s2 = sb("s2", [128, 2, 8], F32)
                xs = [sb("xs%d" % i, [128, NT], F32) for i in range(1)] * 2
                XR = sb("XR", [128, NT], F32)
                xc = sb("xc", [128, NT], F32)
                xcb = sb("xcb", [128, NT], BF16)
                rr = [sb("rr%d" % i, [128, NT], F32) for i in range(2)]
                ii = [sb("ii%d" % i, [128, NT], F32) for i in range(2)]
                aa = [sb("aa%d" % i, [128, NT], F32) for i in range(2)]
                hsum = sb("hsum", [128, NT], F32)
                GG = [sb("GG0", [128, NOWN], F32)] * 2
                yr = sb("yr", [128, NOWN], BF16)
                P.dma("pool", lwa_b[:], lwa, writes=["lwa_b"])
                P.dma("pool", lwx_b[:], lwx, writes=["lwx_b"])
                P.dma("sp", cw[:], cw5, writes=["cw"])
                P.dma("sp", cbv[:], cb, writes=["cbv"])
                P.dma("sp", ba[:], lba, writes=["ba"])
                P.dma("sp", bx[:], lbx, writes=["bx"])
                P.dma("sp", lam[:], llam, writes=["lam"])
                P.act(spl[:], lam[:], AF.Exp, ["lam"], ["spl"], scale=-1.0)
                P.act(spl[:], spl[:], AF.Ln, ["spl"], ["spl"], bias=1.0)
                P.ts("dve", s1[:], spl[:], -8.0, None, ALU.mult, None, ["spl"], ["s1"])
                P.ts("dve", s2[:], spl[:], -16.0, None, ALU.mult, None, ["spl"], ["s2"])
                SM = ["cw", "cbv", "ba", "bx", "s1", "s2"]
                rot = [0]

                def nb():
                    rot[0] = (rot[0] + 1) % 8
                    return rot[0]

                XRl = XR[:, NCTX:NT].rearrange("p (r w) -> p w r", w=64)
                HSl = hsum[:, NCTX:NT].rearrange("p (r w) -> p w r", w=64)
                TT = [(t * 512, 512) for t in range(8)] + [(4096, 256)]
                XRK = [("XRs", i) for i in range(NT // 256)]
                GGK = [("GGs", i) for i in range(8)]
                nblk = 8 if "lru1" not in self.skip else 1

                def load(n):
                    P.dma("sp", xs[0][:], XRs[n], reads=XRK, writes=[("xs", 0)])

                load(0)
                for n in range(nblk):
                    xsn = xs[n % 2]
                    P.dma("sp", GG[0][:], GGs[n], reads=GGK, writes=[("GG", 0)])
                    P.copy("pool", XR[:, 0:NCTX], xsn[:, 0:NCTX], [("xs", 0)], ["XRc"])
                    for q_ in range(2):
                        src = xsn[:, NCTX:NT].rearrange("p (w r) -> p w r", r=64)[:, 32 * q_:32 * (q_ + 1), :]
                        dst = XRl[:, 32 * q_:32 * (q_ + 1), :]
                        if q_ == 0:
                            P.copy("pool", dst, src, [("xs", 0)], [("XRl", q_)])
                        else:
                            P.act(dst, src, AF.Copy, [("xs", 0)], [("XRl", q_)])
                    XK = ["XRc", ("XRl", 0), ("XRl", 1)]
                    if n + 1 < nblk:
                        load(n + 1)
                    P.ts("dve", xc[:], XR[:], cw[:, n, 2:3], cbv[:, n:n + 1], ALU.mult, ALU.add, XK + SM, ["xc"])
                    for j in (0, 1, 3, 4):
                        o = j - 2
                        for (a_, b_) in ((0, NCTX), (NCTX, NT)):
                            lo, hi = max(a_, a_ - o), min(b_, b_ - o)
                            P.stt(xc[:, lo:hi], XR[:, lo + o:hi + o], cw[:, n, j:j + 1], xc[:, lo:hi], ALU.mult, ALU.add,
                                  XK + ["xc"] + SM, ["xc"])
                    P.act(xcb[:], xc[:], AF.Copy, ["xc"], ["xcb"])
                    if self.debug and n == 0:
                        P.dump("d_xc", xc[:], ["xc"])
                    for d_ in range(2):
                        R_, I_, A_ = ("rr", d_), ("ii", d_), ("aa", d_)
                        for (t0, tn) in TT:
                            b1, b2 = nb(), nb()
                            P.mm(ps[b1][:, 0:tn], lwa_b[:, d_, n, :], xcb[:, t0:t0 + tn], True, True, ["lwa_b", "xcb"], [PS(b1)])
                            P.mm(ps[b2][:, 0:tn], lwx_b[:, d_, n, :], xcb[:, t0:t0 + tn], True, True, ["lwx_b", "xcb"], [PS(b2)])
                            P.act(rr[d_][:, t0:t0 + tn], ps[b1][:, 0:tn], AF.Sigmoid, [PS(b1)] + SM, [R_], bias=ba[:, d_, n:n + 1], scale=1.0)
                            P.act(ii[d_][:, t0:t0 + tn], ps[b2][:, 0:tn], AF.Sigmoid, [PS(b2)] + SM, [I_], bias=bx[:, d_, n:n + 1], scale=1.0)
                    for d_ in range(2):
                        R_, I_, A_ = ("rr", d_), ("ii", d_), ("aa", d_)
                        P.act(aa[d_][:], rr[d_][:], AF.Exp, [R_] + SM, [A_], scale=s1[:, d_, n:n + 1])
                        P.act(rr[d_][:], rr[d_][:], AF.Exp, [R_] + SM, [R_], scale=s2[:, d_, n:n + 1])
                    for d_ in range(2):
                        R_, I_, A_ = ("rr", d_), ("ii", d_), ("aa", d_)
                        P.tt("pool", ii[d_][:], ii[d_][:], xc[:], ALU.mult, [I_, "xc"], [I_])
                    for d_ in range(2):
                        R_, I_, A_ = ("rr", d_), ("ii", d_), ("aa", d_)
                        P.ts("pool", rr[d_][:], rr[d_][:], 1.0, 0.0, ALU.min, ALU.max, [R_], [R_])
                        P.act(rr[d_][:], rr[d_][:], AF.Sqrt, [R_], [R_], scale=-1.0, bias=1.0)
                    for d_ in range(2):
                        R_, I_, A_ = ("rr", d_), ("ii", d_), ("aa", d_)
                        P.tt("dve", rr[d_][:], rr[d_][:], ii[d_][:], ALU.mult, [R_, I_], [R_])
                    S.add("dve", lambda e: e.tensor_tensor_scan(out=hsum[:], data0=aa[0][:], data1=rr[0][:], initial=0.0,
                                                                op0=ALU.mult, op1=ALU.add),
                          reads=[("aa", 0), ("rr", 0)], writes=["hsum"])
                    S.add("dve", lambda e: e.tensor_tensor_scan(out=XR[:, 0:NCTX][:, ::-1], data0=aa[1][:, 0:NCTX][:, ::-1],
                                                                data1=rr[1][:, 0:NCTX][:, ::-1], initial=0.0,
                                                                op0=ALU.mult, op1=ALU.add),
                          reads=[("aa", 1), ("rr", 1)], writes=XK)
                    S.add("dve", lambda e: e.tensor_tensor_scan(out=XR[:, NCTX:NT][:, ::-1], data0=aa[1][:, NCTX:NT][:, ::-1],
                                                                data1=rr[1][:, NCTX:NT][:, ::-1], initial=XR[:, 0:1],
                                                                op0=ALU.mult, op1=ALU.add),
                          reads=[("aa", 1), ("rr", 1)] + XK, writes=XK)
                    P.tt("pool", hsum[:, NCTX:NT], hsum[:, NCTX:NT], XR[:, NCTX:NT], ALU.add, ["hsum"] + XK, ["hsum"])
                    if self.debug and n == 0:
                        P.dump("d_hsum", hsum[:], ["hsum"])
                    P.tt("dve", yr[:].rearrange("p (w r) -> p w r", r=64), GG[n % 2][:].rearrange("p (w r) -> p w r", r=64),
                         HSl[:, 0:32, :], ALU.mult, [("GG", 0), "hsum"], ["yr"])
                    P.dma("act", yT[8 + n], yr[:], reads=["yr"], writes=[("yT", 8 + n)])
                S.barrier()
            if self.upto < 5:
                return self.finish(outT)

            with ExitStack() as st:
                sb = lambda name, shape, dt: st.enter_context(nc.sbuf_tensor(name, list(shape), dt))
                T = 1024
                hx2 = sb("hx2", [128, KC, T], BF16)
                aT = sb("aT", [128, NF, T], BF16)
                stg = [sb("stg%d" % i, [128, T], F32) for i in range(2)]
                sqf = [sb("sqf%d" % i, [128, 512], F32) for i in range(4)]
                sqsum = sb("sqsum", [128, T], F32)
                sdb = sb("sdb", [128, T], F32)
                rb = sb("rb", [128, T], F32)
                in1 = [sb("in1_%d" % i, [128, 512], F32) for i in range(4)]
                in2 = [sb("in2_%d" % i, [128, 512], F32) for i in range(4)]
                sg = [sb("sg%d" % i, [128, T], BF16) for i in range(2)]
                WB = [sb("WB%d" % i, [128, 8192], BF16) for i in range(2)]
                Wo4 = [WB[i][:, :].rearrange("p (k n) -> p k n", n=512) for i in range(2)]
                Wg_ = [WB[i][:, 0:4096].rearrange("p (k n) -> p k n", n=256) for i in range(2)]
                Wu_ = [WB[i][:, 4096:8192].rearrange("p (k n) -> p k n", n=256) for i in range(2)]
                Wf4 = [WB[i][:, 0:11 * 512].rearrange("p (f n) -> p f n", n=512) for i in range(2)]
                WK = lambda wi: [("WB", wi, q_) for q_ in range(4)]
                w_out_v = w_out.rearrange("(k p) n -> p k n", p=128)
                w_fi_v = w_ffn_in.rearrange("(k p) n -> p k n", p=128)
                w_fo_v = w_ffn_out.rearrange("(f p) n -> p f n", p=128)
                xTv = pk(xT)
                cnt = [0]
                wcnt = [0]
                H = lambda tt_: slice(tt_ * 512, (tt_ + 1) * 512)

                def rstd_from_sqsum():
                    for tt_ in range(2):
                        P.mm(ps[tt_][:, 0:512], ones_f[:], sqsum[:, H(tt_)], True, True, ["ones_f", ("sqsum", tt_)], [PS(tt_)])
                        P.act(sdb[:, H(tt_)], ps[tt_][:, 0:512], AF.Sqrt, [PS(tt_)], [("sdb", tt_)], scale=1.0 / D, bias=EPS)
                        P.recip(rb[:, H(tt_)], sdb[:, H(tt_)], [("sdb", tt_)], [("rb", tt_)])

                def accum_sq(src, src_keys, first, tt_):
                    i_ = cnt[0] % 4
                    cnt[0] += 1
                    if first:
                        P.act(sqsum[:, H(tt_)], src, AF.Square, src_keys, [("sqsum", tt_)])
                    else:
                        P.act(sqf[i_][:], src, AF.Square, src_keys, [("sqf", i_)])
                        P.tt("pool", sqsum[:, H(tt_)], sqsum[:, H(tt_)], sqf[i_][:], ALU.add, [("sqsum", tt_), ("sqf", i_)], [("sqsum", tt_)])

                def evac_group(cg, th, dst_dram, dst_key):
                    tok0 = th * T
                    for cc in range(4):
                        c = 4 * cg + cc
                        si = c % 2
                        for tt_ in range(2):
                            bk = cc * 2 + tt_
                            P.act(stg[si][:, H(tt_)], ps[bk][:, 0:512], AF.Copy, [PS(bk)], [("stg", si, tt_)])
                            accum_sq(stg[si][:, H(tt_)], [("stg", si, tt_)], c == 0, tt_)
                        P.dma("act", dst_dram[c][:, tok0:tok0 + T], stg[si][:], reads=[("stg", si, 0), ("stg", si, 1)],
                              writes=[(dst_key, c, th)])

                def norm_pass(th, a_dram, a_key, b_src, scale_idx, out_fn, crange=range(KC)):
                    tok0 = th * T
                    for c in crange:
                        for tt_ in range(2):
                            j = (2 * c + tt_) % 4
                            P.dma("sp", in1[j][:], a_dram[c][:, tok0 + tt_ * 512:tok0 + (tt_ + 1) * 512], reads=[(a_key, c, th)],
                                  writes=[("in1", j)])
                            bsrc, bk_ = b_src(c, tt_)
                            P.dma("sp", in2[j][:], bsrc, reads=bk_, writes=[("in2", j)])
                            P.tt("dve", in1[j][:], in1[j][:], rb[:, H(tt_)], ALU.mult, [("in1", j), ("rb", tt_)], [("in1", j)])
                            si = c % 2
                            P.stt(stg[si][:, H(tt_)], in1[j][:], vec[:, scale_idx, c:c + 1], in2[j][:], ALU.mult, ALU.add,
                                  [("in1", j), ("in2", j)] + VEC, [("stg", si, tt_)])
                            out_fn(c, tt_, si)

                pending = []

                def p7c(th_, crange):
                    t0_ = th_ * T

                    def out7c(c, tt_, si):
                        self.fin.append(P.dma("act", outT[c][:, t0_ + tt_ * 512:t0_ + (tt_ + 1) * 512], stg[si][:, H(tt_)],
                                              reads=[("stg", si, tt_)], writes=[("outT", c, th_, tt_)]))

                    norm_pass(th_, ffnT, "ffnT",
                              lambda c, tt_: (xmidT[c][:, t0_ + tt_ * 512:t0_ + (tt_ + 1) * 512], [("xmidT", c, th_, tt_)]),
                              7, out7c, crange)

                for th in range(2):
                    tok0 = th * T
                    P.dma("pool", aT[:, 0:KC, :], pk(yT)[:, :, tok0:tok0 + T], reads=[("yT", k) for k in range(KC)],
                          writes=[("aT", f) for f in range(KC)])
                    for cg in range(4):
                        wi = wcnt[0] % 2
                        wcnt[0] += 1
                        for q_ in range(4):
                            P.dma("pool", Wo4[wi][:, 4 * q_:4 * q_ + 4, :], w_out_v[:, 4 * q_:4 * q_ + 4, cg * 512:(cg + 1) * 512],
                                  writes=[("WB", wi, q_)])
                        for cc in range(4):
                            for tt_ in range(2):
                                bk = cc * 2 + tt_
                                for k in range(KC):
                                    P.mm(ps[bk][:, 0:512], Wo4[wi][:, k, cc * 128:(cc + 1) * 128], aT[:, k, H(tt_)],
                                         k == 0, k == KC - 1, [("WB", wi, k // 4), ("aT", k)], [PS(bk)])
                        if pending:
                            p7c(pending[0], range(4 * cg, 4 * cg + 4))
                        evac_group(cg, th, yxT, "yxT")
                    pending.clear()
                    rstd_from_sqsum()

                    def out5c(c, tt_, si):
                        accum_sq(stg[si][:, H(tt_)], [("stg", si, tt_)], c == 0, tt_)
                        P.dma("act", xmidT[c][:, tok0 + tt_ * 512:tok0 + (tt_ + 1) * 512], stg[si][:, H(tt_)],
                              reads=[("stg", si, tt_)], writes=[("xmidT", c, th, tt_)])

                    norm_pass(th, yxT, "yxT",
                              lambda c, tt_: (xTv[:, c, NCTX + tok0 + tt_ * 512:NCTX + tok0 + (tt_ + 1) * 512], []), 4, out5c)
                    rstd_from_sqsum()
                    for c in range(KC):
                        for tt_ in range(2):
                            j = (2 * c + tt_) % 4
                            P.dma("sp", in1[j][:], xmidT[c][:, tok0 + tt_ * 512:tok0 + (tt_ + 1) * 512], reads=[("xmidT", c, th, tt_)],
                                  writes=[("in1", j)])
                            P.tt("dve", in1[j][:], in1[j][:], rb[:, H(tt_)], ALU.mult, [("in1", j), ("rb", tt_)], [("in1", j)])
                            P.act(hx2[:, c, H(tt_)], in1[j][:], AF.Identity, [("in1", j)] + VEC, [("hx2", c)],
                                  scale=vec[:, 5, c:c + 1], bias=vec[:, 6, c:c + 1])
                    for fp in range(NF // 2):
                        wi = wcnt[0] % 2
                        wcnt[0] += 1
                        for q_ in range(2):
                            P.dma("pool", Wg_[wi][:, 8 * q_:8 * q_ + 8, :], w_fi_v[:, 8 * q_:8 * q_ + 8, fp * 256:(fp + 1) * 256],
                                  writes=[("WB", wi, q_)])
                            P.dma("pool", Wu_[wi][:, 8 * q_:8 * q_ + 8, :],
                                  w_fi_v[:, 8 * q_:8 * q_ + 8, DFF + fp * 256:DFF + (fp + 1) * 256], writes=[("WB", wi, 2 + q_)])
                        for ff in range(2):
                            f = 2 * fp + ff
                            base = 4 * (f % 2)
                            for tt_ in range(2):
                                for (W_, wk, off) in ((Wg_, 0, 0), (Wu_, 1, 2)):
                                    bk = base + off + tt_
                                    for k in range(KC):
                                        P.mm(ps[bk][:, 0:512], W_[wi][:, k, ff * 128:(ff + 1) * 128], hx2[:, k, H(tt_)],
                                             k == 0, k == KC - 1, [("WB", wi, 2 * wk + k // 8), ("hx2", k)], [PS(bk)])
                            si = f % 2
                            for tt_ in range(2):
                                P.act(sg[si][:, H(tt_)], ps[base + tt_][:, 0:512], AF.Silu, [PS(base + tt_)], [("sg", si, tt_)])
                                P.tt("dve", aT[:, f, H(tt_)], sg[si][:, H(tt_)], ps[base + 2 + tt_][:, 0:512],
                                     ALU.mult, [("sg", si, tt_), PS(base + 2 + tt_)], [("aT", f)])
                    for cg in range(4):
                        for fq in range(4):
                            wi = wcnt[0] % 2
                            wcnt[0] += 1
                            for q_, (fa, fb) in enumerate(((0, 4), (4, 8), (8, 11))):
                                P.dma("pool", Wf4[wi][:, fa:fb, :], w_fo_v[:, fq * 11 + fa:fq * 11 + fb, cg * 512:(cg + 1) * 512],
                                      writes=[("WB", wi, q_)])
                            for cc in range(4):
                                for tt_ in range(2):
                                    bk = cc * 2 + tt_
                                    for f_ in range(11):
                                        f = fq * 11 + f_
                                        P.mm(ps[bk][:, 0:512], Wf4[wi][:, f_, cc * 128:(cc + 1) * 128], aT[:, f, H(tt_)],
                                             f == 0, f == NF - 1, [("WB", wi, f_ // 4), ("aT", f)], [PS(bk)])
                        evac_group(cg, th, ffnT, "ffnT")
                    rstd_from_sqsum()
                    if th == 0:
                        pending.append(0)
                    else:
                        p7c(1, range(KC))
            return self.finish(outT)

    def finish(self, outT):
        if not self.fin:
            pass
        self.S.emit(final_waits=self.fin)
        return self.nc


def _consts():
    l = np.arange(128)[:, None]
    j = np.arange(128)[None, :]
    c = np.zeros((128, 5, 128), np.float32)
    c[:, 0, :] = (l <= j)
    c[:, 1, :] = (l >= j)
    c[:, 2, :] = (l == j)
    c[:, 3, :] = np.where(l > j, NEG, 0.0)
    c[:, 4, :] = np.where(l < j, NEG, 0.0)
    return c


def _pvec(v):
    return np.ascontiguousarray(v.reshape(-1, 128).T)


def prep_inputs(inp):
    f = lambda a: np.ascontiguousarray(np.asarray(a, dtype=np.float32))
    x, c, ctx, c_ctx = f(inp["x"]), f(inp["c"]), f(inp["ctx"]), f(inp["c_ctx"])
    w_in = f(inp["w_in"][0])
    shared = {
        "consts": _consts(),
        "w_mod": f(inp["w_mod"][0]),
        "bmodT": _pvec(f(inp["b_mod"][0])),
        "gains": np.ascontiguousarray(np.stack([_pvec(f(inp[k][0])) for k in
                                                ("g_pre_mix", "g_post_mix", "g_pre_ffn", "g_post_ffn")], axis=1)),
        "w_in": w_in,
        "mhg": _pvec(f(inp["mh_norm_g"][0])),
        "cb": _pvec(f(inp["conv_b"][0])),
        "w_out": f(inp["w_out"][0]),
        "w_ffn_in": f(inp["w_ffn_in"][0]),
        "w_ffn_out": f(inp["w_ffn_out"][0]),
    }
    wgate = w_in[:, 4096:4112]
    bgate = f(inp["b_gates"][0]).reshape(16)
    conv_w = f(inp["conv_w"][0])
    lwa, lwx = f(inp["lru_w_a"][0]), f(inp["lru_w_x"][0])
    lba, lbx, lam = f(inp["lru_b_a"][0]), f(inp["lru_b_x"][0]), f(inp["lru_lambda"][0])
    maps = []
    for core in range(8):
        b, half = core // 2, core % 2
        xl = x[b].reshape(64, 64, D).transpose(1, 0, 2).reshape(NLAT, D)
        xc = ctx[b]
        perm = np.arange(16)
        dsel = [0, 1]
        w5 = np.zeros((5, 1024), np.float32)
        if half == 1:
            xl = xl[::-1]
            xc = xc[::-1]
            perm = np.concatenate([np.arange(8, 16), np.arange(0, 8)])
            dsel = [1, 0]
            w5[1], w5[2], w5[3], w5[4] = conv_w[3], conv_w[2], conv_w[1], conv_w[0]
        else:
            w5[0], w5[1], w5[2], w5[3] = conv_w[0], conv_w[1], conv_w[2], conv_w[3]
        xs = np.concatenate([xc, xl], axis=0)
        m = dict(shared)
        m["xT"] = np.ascontiguousarray(xs.T).reshape(KC, 128, NT)
        cc2 = np.stack([c[b], c_ctx], axis=1)
        m["ccT"] = np.ascontiguousarray(cc2.reshape(KC, 128, 2).transpose(1, 0, 2))
        m["wg"] = np.ascontiguousarray(wgate[:, perm])
        m["bg"] = np.ascontiguousarray(np.broadcast_to(bgate[perm][None, :], (128, 16)))
        m["cw5"] = np.ascontiguousarray(w5.reshape(5, 8, 128).transpose(2, 1, 0))
        m["lwa"] = np.ascontiguousarray(lwa[dsel].transpose(2, 0, 1, 3))
        m["lwx"] = np.ascontiguousarray(lwx[dsel].transpose(2, 0, 1, 3))
        m["lba"] = np.ascontiguousarray(lba[dsel].reshape(2, 8, 128).transpose(2, 0, 1))
        m["lbx"] = np.ascontiguousarray(lbx[dsel].reshape(2, 8, 128).transpose(2, 0, 1))
        m["llam"] = np.ascontiguousarray(lam[dsel].reshape(2, 8, 128).transpose(2, 0, 1))
        maps.append(m)
    return maps


def assemble(results):
    out = np.zeros((4, NLAT, D), np.float32)
    for core in range(8):
        b, half = core // 2, core % 2
        o = results[core]["outT"].reshape(D, NOWN).T
        i = np.arange(NOWN)
        cm = i if half == 0 else (NLAT - 1 - i)
        w, r = cm // 64, cm % 64
        out[b, r * 64 + w] = o
    return out


_CACHE = {}


def kernel(**inputs):
    if "nc" not in _CACHE:
        _CACHE["nc"] = Prog().build()
    nc = _CACHE["nc"]
    maps = prep_inputs(inputs)
    res = run_bass_kernel_spmd(nc, maps, core_ids=list(range(8)))
    return assemble(res.results)
```
